# Optimizing a Trainium2 kernel written in Bass

```python
import math, functools
import jax, jax.numpy as jnp
from jax import lax
import numpy as np

D_MODEL = 2048
BATCH = 4
SEQ = 4096
DEPTH = 2

MIX_WIDTH = D_MODEL
GROUP_WIDTH = MIX_WIDTH // 2
ML_HEADS = 4
ML_HEAD_DIM = GROUP_WIDTH // ML_HEADS
RET_HEADS = 4
RET_HEAD_DIM = GROUP_WIDTH // RET_HEADS
SB_HEADS = 16
SB_HEAD_DIM = D_MODEL // SB_HEADS
CHUNK = 128
Q_BLOCK = 128
CONV_WIDTH = 4
FFN_HIDDEN = -(-8 * D_MODEL // (3 * 256)) * 256
IN0_WIDTH = 8 * GROUP_WIDTH + 2 * ML_HEADS
ROPE_BASE = 10000.0
EPS = 1e-6

kernel_name = "hybrid_mlstm_retention_stickbreaking_block"


def rms_norm(x, g):
    xf = x.astype(jnp.float32)
    y = xf * lax.rsqrt(jnp.mean(xf * xf, axis=-1, keepdims=True) + EPS)
    return (y * g.astype(jnp.float32)).astype(x.dtype)


def head_norm(h, g, center):
    H, d = h.shape[-2], h.shape[-1]
    if center:
        h = h - jnp.mean(h, axis=-1, keepdims=True)
    h = h * lax.rsqrt(jnp.mean(h * h, axis=-1, keepdims=True) + EPS)
    return h * g.astype(jnp.float32).reshape(H, d)


def causal_conv(x, w):
    C = x.shape[-1]
    return lax.conv_general_dilated(
        x, w[:, None, :].astype(x.dtype), window_strides=(1,),
        padding=[(w.shape[0] - 1, 0)], dimension_numbers=('NWC', 'WIO', 'NWC'),
        feature_group_count=C)


def rotary(x):
    S, d = x.shape[1], x.shape[-1]
    inv = ROPE_BASE ** (-jnp.arange(0, d, 2, dtype=jnp.float32) / d)
    ang = jnp.arange(S, dtype=jnp.float32)[:, None] * inv[None, :]
    cos = jnp.cos(ang)[None, :, None, :]
    sin = jnp.sin(ang)[None, :, None, :]
    x1, x2 = x[..., : d // 2], x[..., d // 2:]
    return jnp.concatenate([x1 * cos - x2 * sin, x1 * sin + x2 * cos], axis=-1)


def to_chunks(t):
    B, S, H, d = t.shape
    return t.reshape(B, S // CHUNK, CHUNK, H, d).transpose(0, 3, 1, 2, 4)


def from_chunks(t):
    B, H, NC, L, d = t.shape
    return t.transpose(0, 2, 3, 1, 4).reshape(B, NC * L, H, d)


def gate_chunks(t):
    B, S, H = t.shape
    return t.reshape(B, S // CHUNK, CHUNK, H).transpose(0, 3, 1, 2)


def mlstm_chunkwise(q, k, v, i_pre, f_pre):
    d = q.shape[-1]
    q = to_chunks(q.astype(jnp.float32)) * (d ** -0.5)
    k = to_chunks(k.astype(jnp.float32))
    v = to_chunks(v.astype(jnp.float32))
    B, H, NC, L, _ = q.shape
    ig = gate_chunks(i_pre.astype(jnp.float32))
    lf = gate_chunks(jax.nn.log_sigmoid(f_pre.astype(jnp.float32)))
    b = jnp.cumsum(lf, axis=-1)
    g = b[..., -1]
    a = g[..., None] - b + ig
    m_loc = jnp.max(a, axis=-1)
    w_end = jnp.exp(a - m_loc[..., None])
    kv_loc = jnp.einsum('bhcld,bhcle->bhcde', k * w_end[..., None], v)
    n_loc = jnp.einsum('bhcld,bhcl->bhcd', k, w_end)

    def step(carry, inp):
        C, n, m = carry
        g_c, m_l, kv_l, n_l = inp
        m_new = jnp.maximum(g_c + m, m_l)
        s_old = jnp.exp(g_c + m - m_new)
        s_new = jnp.exp(m_l - m_new)
        C_new = s_old[..., None, None] * C + s_new[..., None, None] * kv_l
        n_new = s_old[..., None] * n + s_new[..., None] * n_l
        return (C_new, n_new, m_new), (C, n, m)

    init = (jnp.zeros((B, H, d, d), jnp.float32), jnp.zeros((B, H, d), jnp.float32),
            jnp.zeros((B, H), jnp.float32))
    xs = (jnp.moveaxis(g, 2, 0), jnp.moveaxis(m_loc, 2, 0),
          jnp.moveaxis(kv_loc, 2, 0), jnp.moveaxis(n_loc, 2, 0))
    _, (C0, n0, m0) = lax.scan(step, init, xs)
    C0 = jnp.moveaxis(C0, 0, 2)
    n0 = jnp.moveaxis(n0, 0, 2)
    m0 = jnp.moveaxis(m0, 0, 2)

    causal = jnp.tril(jnp.ones((L, L), dtype=bool))
    log_d = b[..., :, None] - b[..., None, :] + ig[..., None, :]
    log_d = jnp.where(causal, log_d, -jnp.inf)
    m_inter = b + m0[..., None]
    m_t = jnp.maximum(m_inter, jnp.max(log_d, axis=-1))
    w = jnp.einsum('bhcld,bhcsd->bhcls', q, k) * jnp.exp(log_d - m_t[..., None])
    s_inter = jnp.exp(m_inter - m_t)
    num = (jnp.einsum('bhcls,bhcse->bhcle', w, v)
           + s_inter[..., None] * jnp.einsum('bhcld,bhcde->bhcle', q, C0))
    den = jnp.sum(w, axis=-1) + s_inter * jnp.einsum('bhcld,bhcd->bhcl', q, n0)
    h = num / jnp.maximum(jnp.abs(den), jnp.exp(-m_t))[..., None]
    return from_chunks(h)


def retention_chunkwise(q, k, v):
    d = q.shape[-1]
    q = to_chunks(q)
    k = to_chunks(k) * (d ** -0.5)
    v = to_chunks(v)
    B, H, NC, L, _ = q.shape
    log_gamma = jnp.log(1.0 - 2.0 ** (-5.0 - 2.0 * jnp.arange(H, dtype=jnp.float32)))
    pos = jnp.arange(L, dtype=jnp.float32)
    diff = pos[:, None] - pos[None, :]
    causal = diff >= 0
    decay_intra = jnp.where(causal, jnp.exp(jnp.maximum(diff, 0.0)[None] * log_gamma[:, None, None]), 0.0)
    decay_q = jnp.exp((pos + 1.0)[None, :] * log_gamma[:, None])
    decay_k = jnp.exp((L - 1.0 - pos)[None, :] * log_gamma[:, None])
    decay_chunk = jnp.exp(L * log_gamma)

    kv_loc = jnp.einsum('bhcld,bhcle->bhcde', k * decay_k[None, :, None, :, None], v)

    def step(R, kv_l):
        return decay_chunk[None, :, None, None] * R + kv_l, R

    _, R0 = lax.scan(step, jnp.zeros((B, H, d, d), jnp.float32), jnp.moveaxis(kv_loc, 2, 0))
    R0 = jnp.moveaxis(R0, 0, 2)
    scores = jnp.einsum('bhcld,bhcsd->bhcls', q, k) * decay_intra[None, :, None]
    inner = jnp.einsum('bhcls,bhcse->bhcle', scores, v)
    cross = jnp.einsum('bhcld,bhcde->bhcle', q * decay_q[None, :, None, :, None], R0)
    return from_chunks(inner + cross)


def stick_breaking_attention(q, k, v):
    B, S, H, d = q.shape
    q = q.astype(jnp.float32).transpose(0, 2, 1, 3) * (d ** -0.5)
    k = k.astype(jnp.float32).transpose(0, 2, 1, 3)
    v = v.astype(jnp.float32).transpose(0, 2, 1, 3)
    outs = []
    for blk in range(S // Q_BLOCK):
        q0 = blk * Q_BLOCK
        kl = q0 + Q_BLOCK
        z = jnp.einsum('bhtd,bhsd->bhts', q[:, :, q0:kl], k[:, :, :kl])
        t_idx = q0 + jnp.arange(Q_BLOCK)
        s_idx = jnp.arange(kl)
        valid = s_idx[None, :] < t_idx[:, None]
        log_keep = jnp.where(valid, jax.nn.log_sigmoid(-z), 0.0)
        log_remain = lax.cumsum(log_keep, axis=3, reverse=True) - log_keep
        a = jnp.where(valid, jnp.exp(jax.nn.log_sigmoid(z) + log_remain), 0.0)
        outs.append(jnp.einsum('bhts,bhsd->bhtd', a, v[:, :, :kl]))
    o = jnp.concatenate(outs, axis=2)
    return o.transpose(0, 2, 1, 3)


def mlstm_retention_mixer(h, w_in, b_gates, w_conv, g_ml, g_ret, w_out):
    B, S, _ = h.shape
    GW = GROUP_WIDTH
    proj = h @ w_in
    ml_qk, ml_v, ml_o, r_q, r_k, r_v, r_g, gates = jnp.split(
        proj, [2 * GW, 3 * GW, 4 * GW, 5 * GW, 6 * GW, 7 * GW, 8 * GW], axis=-1)
    ml_qk = jax.nn.silu(causal_conv(ml_qk, w_conv))
    ml_q, ml_k = ml_qk[..., :GW], ml_qk[..., GW:]
    gates = gates.astype(jnp.float32) + b_gates.astype(jnp.float32)
    i_pre, f_pre = gates[..., :ML_HEADS], gates[..., ML_HEADS:]
    heads_ml = lambda t: t.reshape(B, S, ML_HEADS, ML_HEAD_DIM)
    heads_ret = lambda t: t.astype(jnp.float32).reshape(B, S, RET_HEADS, RET_HEAD_DIM)
    h_ml = mlstm_chunkwise(heads_ml(ml_q), heads_ml(ml_k), heads_ml(ml_v), i_pre, f_pre)
    h_ml = head_norm(h_ml, g_ml, center=False) * jax.nn.sigmoid(heads_ml(ml_o).astype(jnp.float32))
    h_ret = retention_chunkwise(rotary(heads_ret(r_q)), rotary(heads_ret(r_k)), heads_ret(r_v))
    h_ret = head_norm(h_ret, g_ret, center=True) * jax.nn.silu(heads_ret(r_g))
    y = jnp.concatenate([h_ml.reshape(B, S, GW), h_ret.reshape(B, S, GW)], axis=-1)
    return y.astype(h.dtype) @ w_out


def stick_breaking_mixer(h, w_qkv, w_out):
    B, S, _ = h.shape
    q, k, v = jnp.split(h @ w_qkv, 3, axis=-1)
    heads = lambda t: t.reshape(B, S, SB_HEADS, SB_HEAD_DIM)
    o = stick_breaking_attention(heads(q), heads(k), heads(v))
    return o.reshape(B, S, D_MODEL).astype(h.dtype) @ w_out


def swiglu_ffn(h, w_gu, w_down):
    gate, up = jnp.split(h @ w_gu, 2, axis=-1)
    return (jax.nn.silu(gate) * up) @ w_down


def setup_inputs(seed: int = 0) -> dict:
    key = jax.random.key(seed)
    ks = jax.random.split(key, 24)
    f32 = jnp.float32
    dense = lambda k, fi, fo: jax.random.normal(k, (fi, fo), f32) * (fi ** -0.5)
    gain = lambda k, n: 1.0 + 0.02 * jax.random.normal(k, (n,), f32)
    b_gates0 = jnp.concatenate([
        0.1 * jax.random.normal(ks[3], (ML_HEADS,), f32),
        jnp.linspace(3.0, 6.0, ML_HEADS, dtype=f32) + 0.1 * jax.random.normal(ks[4], (ML_HEADS,), f32),
    ])
    return {
        "x": jax.random.normal(ks[0], (BATCH, SEQ, D_MODEL), f32),
        "norm_mix0": gain(ks[1], D_MODEL),
        "w_in0": dense(ks[2], D_MODEL, IN0_WIDTH),
        "b_gates0": b_gates0,
        "w_conv0": jax.random.normal(ks[5], (CONV_WIDTH, 2 * GROUP_WIDTH), f32) * (CONV_WIDTH ** -0.5),
        "g_ml0": gain(ks[6], GROUP_WIDTH),
        "g_ret0": gain(ks[7], GROUP_WIDTH),
        "w_out0": dense(ks[8], MIX_WIDTH, D_MODEL),
        "norm_ffn0": gain(ks[9], D_MODEL),
        "w_gu0": dense(ks[10], D_MODEL, 2 * FFN_HIDDEN),
        "w_down0": dense(ks[11], FFN_HIDDEN, D_MODEL),
        "norm_mix1": gain(ks[12], D_MODEL),
        "w_qkv1": dense(ks[13], D_MODEL, 3 * D_MODEL),
        "w_out1": dense(ks[14], D_MODEL, D_MODEL),
        "norm_ffn1": gain(ks[15], D_MODEL),
        "w_gu1": dense(ks[16], D_MODEL, 2 * FFN_HIDDEN),
        "w_down1": dense(ks[17], FFN_HIDDEN, D_MODEL),
        "final_norm": gain(ks[18], D_MODEL),
    }


def reference(x, norm_mix0, w_in0, b_gates0, w_conv0, g_ml0, g_ret0, w_out0, norm_ffn0,
              w_gu0, w_down0, norm_mix1, w_qkv1, w_out1, norm_ffn1, w_gu1, w_down1, final_norm):
    mixers = (
        functools.partial(mlstm_retention_mixer, w_in=w_in0, b_gates=b_gates0, w_conv=w_conv0,
                          g_ml=g_ml0, g_ret=g_ret0, w_out=w_out0),
        functools.partial(stick_breaking_mixer, w_qkv=w_qkv1, w_out=w_out1),
    )
    mix_norms = (norm_mix0, norm_mix1)
    ffn_norms = (norm_ffn0, norm_ffn1)
    ffn_weights = ((w_gu0, w_down0), (w_gu1, w_down1))
    for layer in range(DEPTH):
        x = x + mixers[layer](rms_norm(x, mix_norms[layer]))
        x = x + swiglu_ffn(rms_norm(x, ffn_norms[layer]), *ffn_weights[layer])
    return rms_norm(x, final_norm)
```

```python
import numpy as np
import concourse.bass as bass
import concourse.mybir as mybir
from concourse.bass_utils import run_bass_kernel_spmd

F32 = mybir.dt.float32
BF16 = mybir.dt.bfloat16
AF = mybir.ActivationFunctionType
ALU = mybir.AluOpType

D = 2048
GW = 1024
FF = 5632
NIN = 8200
EPS = 1e-6
EPOCH = 16000
COMPUTE = ('pe', 'act', 'dve', 'pool')
PG = 1024


def _keys(a):
    if isinstance(a, str):
        return (a,)
    if isinstance(a, tuple):
        return a
    sp = a.space.name
    name = a.tensor.name
    if sp == 'DRAM':
        return (name,)
    ap = a.ap
    stride = ap[0][0]
    off = a.offset % stride if stride > 0 else a.offset
    ext = 0
    for st, cnt in ap[1:]:
        ext += (cnt - 1) * abs(st)
    sz = mybir.dt.size(a.dtype)
    lo = off * sz
    hi = (off + ext) * sz + sz - 1
    pg = 2048 if sp == 'PSUM' else PG
    return tuple((name, p) for p in range(lo // pg, hi // pg + 1))


class Instr:
    __slots__ = ('idx', 'eng', 'fn', 'deps', 'lane', 'lane_idx', 'needs_inc', 'seq',
                 'waits', 'is_dma', 'vc')


class Prog:
    def __init__(self, nc):
        self.nc = nc
        self.instrs = []
        self.last_w = {}
        self.readers = {}
        self.lane_last = {}
        self.lane_cnt = {}
        self.lane_sem = {}

    def add(self, eng, fn, reads, writes, lane=None):
        ins = Instr()
        ins.idx = len(self.instrs)
        ins.eng = eng
        ins.fn = fn
        ins.lane = lane
        ins.is_dma = lane is not None
        ins.needs_inc = False
        ins.seq = None
        ins.waits = None
        ins.vc = None
        rk = []
        for r in reads:
            rk.extend(_keys(r))
        wk = []
        for w in writes:
            wk.extend(_keys(w))
        rk = list(dict.fromkeys(rk))
        wk = list(dict.fromkeys(wk))
        deps = set()
        for k in rk:
            if k in self.last_w:
                deps.add(self.last_w[k])
        for k in wk:
            if k in self.last_w:
                deps.add(self.last_w[k])
            for r in self.readers.get(k, ()):
                deps.add(r)
        if lane is not None:
            if lane in self.lane_last:
                deps.add(self.lane_last[lane])
            self.lane_last[lane] = ins.idx
            self.lane_cnt[lane] = self.lane_cnt.get(lane, 0) + 1
            ins.lane_idx = self.lane_cnt[lane]
        else:
            ins.lane_idx = None
        deps.discard(ins.idx)
        if eng == 'pe' and not ins.is_dma:
            deps = {d for d in deps
                    if not (self.instrs[d].eng == 'pe' and not self.instrs[d].is_dma)}
        ins.deps = deps
        for k in wk:
            self.last_w[k] = ins.idx
            self.readers[k] = []
        wks = set(wk)
        for k in rk:
            if k in wks:
                continue
            lst = self.readers.setdefault(k, [])
            if not ins.is_dma:
                lst[:] = [r for r in lst
                          if self.instrs[r].is_dma or self.instrs[r].eng != eng]
            lst.append(ins.idx)
        self.instrs.append(ins)
        return ins

    def dma(self, q, lane, out, in_, rk=None, wk=None, **kw):
        r = rk if rk is not None else [in_]
        w = wk if wk is not None else [out]
        return self.add(q, lambda e: e.dma_start(out=out, in_=in_, **kw), r, w, lane=lane)

    def mm(self, out, lhsT, rhs, start=True, stop=True):
        return self.add('pe', lambda e: e.matmul(out, lhsT, rhs, start=start, stop=stop),
                        [lhsT, rhs], [out])

    def tr(self, out, in_, ident):
        return self.add('pe', lambda e: e.transpose(out, in_, ident), [in_, ident], [out])

    def act(self, out, in_, func, bias=None, scale=None, accum_out=None):
        kw = {}
        r = [in_]
        w = [out]
        if bias is not None:
            kw['bias'] = bias
            if not isinstance(bias, (int, float)):
                r.append(bias)
        if scale is not None:
            kw['scale'] = scale
            if not isinstance(scale, (int, float)):
                r.append(scale)
        if accum_out is not None:
            kw['accum_out'] = accum_out
            w.append(accum_out)
        return self.add('act', lambda e: e.activation(out=out, in_=in_, func=func, **kw), r, w)

    def tt(self, eng, out, in0, in1, op):
        return self.add(eng, lambda e: e.tensor_tensor(out=out, in0=in0, in1=in1, op=op),
                        [in0, in1], [out])

    def ts(self, eng, out, in0, s1, s2=None, op0=ALU.mult, op1=None):
        r = [in0]
        for s in (s1, s2):
            if s is not None and not isinstance(s, (int, float)):
                r.append(s)
        kw = {}
        if op1 is not None:
            kw['op1'] = op1
        return self.add(eng, lambda e: e.tensor_scalar(out=out, in0=in0, scalar1=s1, scalar2=s2,
                                                       op0=op0, **kw), r, [out])

    def stt(self, eng, out, in0, scalar, in1, op0, op1):
        r = [in0, in1]
        if not isinstance(scalar, (int, float)):
            r.append(scalar)
        return self.add(eng, lambda e: e.scalar_tensor_tensor(
            out=out, in0=in0, scalar=scalar, in1=in1, op0=op0, op1=op1), r, [out])

    def copy(self, eng, out, in_):
        if eng == 'act':
            return self.add(eng, lambda e: e.copy(out=out, in_=in_), [in_], [out])
        return self.add(eng, lambda e: e.tensor_copy(out=out, in_=in_), [in_], [out])

    def memset(self, eng, out, val):
        return self.add(eng, lambda e: e.memset(out, val), [], [out])

    def recip(self, out, in_):
        return self.add('dve', lambda e: e.reciprocal(out=out, in_=in_), [in_], [out])

    def finalize(self, final_lanes=()):
        nc = self.nc
        instrs = self.instrs
        for ins in instrs:
            for d in ins.deps:
                di = instrs[d]
                if not di.is_dma:
                    di.needs_inc = True
        cnt = {e: 0 for e in COMPUTE}
        for ins in instrs:
            if not ins.is_dma and ins.needs_inc:
                ins.seq = cnt[ins.eng]
                cnt[ins.eng] += 1
        sems = {}
        for e in COMPUTE:
            nep = max(1, (cnt[e] + EPOCH - 1) // EPOCH)
            for k in range(nep):
                sems[(e, k)] = nc.alloc_semaphore("s_%s_%d" % (e, k))
        for ln in self.lane_cnt:
            self.lane_sem[ln] = nc.alloc_semaphore("l_%s" % ln)

        def src_of(di):
            if di.is_dma:
                return ('L', di.lane), di.lane_idx * 16
            return (di.eng, di.seq // EPOCH), di.seq % EPOCH + 1

        known = {e: {} for e in ('pe', 'act', 'dve', 'pool', 'sp')}
        nwaits = 0
        for ins in instrs:
            kn = known[ins.eng]
            waits = {}
            for d in ins.deps:
                s, v = src_of(instrs[d])
                if kn.get(s, 0) >= v:
                    continue
                if waits.get(s, 0) < v:
                    waits[s] = v
            for d in ins.deps:
                di = instrs[d]
                s, v = src_of(di)
                if s in waits and di.vc is not None:
                    for s2, v2 in di.vc.items():
                        if kn.get(s2, 0) < v2:
                            kn[s2] = v2
            for s, v in waits.items():
                if kn.get(s, 0) < v:
                    kn[s] = v
            ins.waits = waits
            nwaits += len(waits)
            if ins.is_dma:
                vc = dict(kn)
                vc[('L', ins.lane)] = ins.lane_idx * 16
                ins.vc = vc
            elif ins.needs_inc:
                s, v = src_of(ins)
                vc = dict(kn)
                vc[s] = v
                ins.vc = vc
        self.stats = dict(n=len(instrs), nwaits=nwaits, incs=dict(cnt),
                          nsem=len(sems) + len(self.lane_sem))
        per_eng = {e: [] for e in ('pe', 'act', 'dve', 'pool', 'sp')}
        for ins in instrs:
            per_eng[ins.eng].append(ins)

        def sem_of(s):
            if s[0] == 'L':
                return self.lane_sem[s[1]]
            return sems[s]

        def emit(eng_obj, lst, tail):
            for ins in lst:
                for s, v in ins.waits.items():
                    eng_obj.wait_ge(sem_of(s), v)
                bi = ins.fn(eng_obj)
                if ins.is_dma:
                    bi.then_inc(self.lane_sem[ins.lane], 16)
                elif ins.needs_inc:
                    bi.then_inc(sems[(ins.eng, ins.seq // EPOCH)], 1)
            for ln in tail:
                eng_obj.wait_ge(self.lane_sem[ln], self.lane_cnt[ln] * 16)

        with nc.Block() as block:
            @block.tensor
            def _(e):
                emit(e, per_eng['pe'], ())

            @block.scalar
            def _(e):
                emit(e, per_eng['act'], ())

            @block.vector
            def _(e):
                emit(e, per_eng['dve'], ())

            @block.gpsimd
            def _(e):
                emit(e, per_eng['pool'], ())

            @block.sync
            def _(e):
                emit(e, per_eng['sp'], final_lanes)
        return self.stats


def host_consts(S):
    c = np.zeros((128, 648), np.float32)
    i = np.arange(128)
    c[:, 0:128] = np.eye(128)
    c[:, 128:256] = (i[:, None] <= i[None, :])
    c[:, 256:384] = 1.0
    c[:, 384:512] = (i[:, None] >= i[None, :])
    c[:, 512:640] = (i[:, None] < i[None, :])
    lg = np.log(1.0 - 2.0 ** (-5.0 - 2.0 * np.arange(4, dtype=np.float64)))
    pos = np.arange(128, dtype=np.float64)
    c[:, 640:644] = np.exp(-(pos[:, None] + 1.0) * lg[None, :]) * (256.0 ** -0.5)
    c[:, 644:648] = np.exp((pos[:, None] + 1.0) * lg[None, :])
    am = np.zeros((128, 4, 512), np.float32)
    t = np.arange(512)
    for k in range(4):
        am[:, k, :] = ((128 * k + i)[:, None] < t[None, :])
    inv = (10000.0 ** (-np.arange(0, 256, 2, dtype=np.float32) / np.float32(256))).astype(np.float32)
    ang = inv[:, None] * np.arange(S, dtype=np.float32)[None, :]
    return dict(cst=c, amask=am.reshape(128, 2048),
                cosT=np.cos(ang).astype(np.float32), sinT=np.sin(ang).astype(np.float32))


RET_DECAY = [float(np.exp(128.0 * np.log(1.0 - 2.0 ** (-5.0 - 2.0 * h)))) for h in range(4)]

WSPEC = [("w_in0", D, NIN), ("w_out0", D, D), ("w_gu0", D, 2 * FF), ("w_down0", FF, D),
         ("w_qkv1", D, 3 * D), ("w_out1", D, D), ("w_gu1", D, 2 * FF), ("w_down1", FF, D)]
VSPEC = [("norm_mix0", D), ("b_gates0", 8), ("g_ml0", GW), ("g_ret0", GW), ("norm_ffn0", D),
         ("norm_mix1", D), ("norm_ffn1", D), ("final_norm", D)]


def build(S, stop_after=None):
    NB = S // 512
    NKB = S // 128
    nc = bass.Bass("TRN2", target_bir_lowering=False)
    P = Prog(nc)
    dt = {}

    def din(name, shape, d=F32):
        dt[name] = nc.dram_tensor(name, list(shape), d, kind="ExternalInput").ap()
        return dt[name]

    x = din("x", [S, D])
    wf = {}
    for n, k, m in WSPEC:
        wf[n] = din(n, [k, m])
    vf = {}
    for n, m in VSPEC:
        vf[n] = din(n, [m])
    wconv = din("w_conv0", [4, D])
    cst = din("cst", [128, 648])
    amask_d = din("amask", [128, 2048])
    cosT = din("cosT", [128, S])
    sinT = din("sinT", [128, S])
    y = nc.dram_tensor("y", [S, D], F32, kind="ExternalOutput").ap()
    def slab_specs(n):
        if n in ("w_in0",):
            return [(0, 16, g * 512, 512) for g in range(16)]
        if n in ("w_out0", "w_out1"):
            return [(0, 16, g * 512, 512) for g in range(4)]
        if n in ("w_qkv1",):
            return [(0, 16, g * 512, 512) for g in range(12)]
        if n in ("w_gu0", "w_gu1"):
            r = []
            for sgi in range(11):
                r.append((0, 16, sgi * 512, 512))
                r.append((0, 16, FF + sgi * 512, 512))
            return r
        r = []
        for g in range(4):
            for (kc0, nk) in ((0, 16), (16, 16), (32, 12)):
                r.append((kc0, nk, g * 512, 512))
        return r
    SL = {n: slab_specs(n) for n, _, _ in WSPEC}
    SLI = {n: {sp_: i for i, sp_ in enumerate(SL[n])} for n in SL}
    wbs = {n: nc.dram_tensor(n + "_s", [len(SL[n]), 128, 8192], BF16, kind="Internal").ap()
           for n, _, _ in WSPEC}
    wgb = nc.dram_tensor("wgate_s", [128, 128], BF16, kind="Internal").ap()
    x2 = nc.dram_tensor("x2s", [S, D], F32, kind="Internal").ap()
    qT1 = nc.dram_tensor("qT1", [16, 128, S], BF16, kind="Internal").ap()
    kT1 = nc.dram_tensor("kT1", [16, 128, S], BF16, kind="Internal").ap()
    v1 = nc.dram_tensor("v1", [S, D], BF16, kind="Internal").ap()
    oT1 = nc.dram_tensor("oT1", [16, 128, S], BF16, kind="Internal").ap()

    AR_BYTES = 206 * 1024
    ar = nc.alloc_sbuf_tensor("arena", [128, AR_BYTES // 2], BF16)
    cur = [0]

    def view_at(off, shape, d):
        n = 1
        for s_ in shape[1:]:
            n *= s_
        sz = 4 if d == F32 else 2
        a = ar[:, off // 2: off // 2 + n * sz // 2]
        if d == F32:
            a = a.bitcast(F32)
        if len(shape) == 3:
            a = a.rearrange("p (a b) -> p a b", a=shape[1])
        elif len(shape) == 4:
            a = a.rearrange("p (a b c) -> p a b c", a=shape[1], b=shape[2])
        elif len(shape) == 5:
            a = a.rearrange("p (a b c e) -> p a b c e", a=shape[1], b=shape[2], c=shape[3])
        return a

    def alloc(shape, d):
        n = 1
        for s_ in shape[1:]:
            n *= s_
        nb = n * (4 if d == F32 else 2)
        off = cur[0]
        cur[0] = (off + nb + 63) // 64 * 64
        assert cur[0] <= AR_BYTES, ("SBUF overflow", cur[0])
        return view_at(off, shape, d)

    K = 1024
    ident = alloc([128, 128], BF16)
    linc = alloc([128, 128], BF16)
    lrest = alloc([128, 128], BF16)
    maskle = alloc([128, 128], F32)
    onesf = alloc([128, 128], F32)
    retc = alloc([128, 8], F32)
    gains = alloc([128, 5, 16], F32)
    wcv = alloc([128, 4, 16], F32)
    bgt = alloc([128, 8], F32)
    epsb = alloc([128, 1], F32)
    oneb = alloc([128, 1], F32)
    small = alloc([128, 64], F32)
    gts = alloc([128, 4, 8], F32)
    gsp = alloc([128, 16], F32)
    gcs = alloc([128, 16], F32)
    gws = alloc([128, 16], F32)
    gosc = alloc([128, 16], F32)
    geg = alloc([128, 16], F32)
    ss4 = alloc([128, 4], F32)
    rstd4 = alloc([128, 4], F32)
    convc = alloc([128, 16, 3], F32)
    cst_off = cur[0]
    Cst = alloc([128, 4, 2, 257], F32)
    Cb = alloc([128, 4, 2, 257], BF16)
    Rst = alloc([128, 4, 2, 256], F32)
    Rb = alloc([128, 4, 2, 256], BF16)
    ktm = [alloc([128, 256], BF16) for _ in range(2)]
    PT = [alloc([128, 128], BF16) for _ in range(2)]
    ctile = [alloc([128, 256], F32) for _ in range(2)]
    ytok1 = alloc([128, D], BF16)
    ytok = [ytok1, ytok1]
    xs = [alloc([128, D], BF16), ytok1]
    NWB = 2
    wbuf = [alloc([128, 16, 512], BF16) for _ in range(NWB)]
    wgate = alloc([128, 16, 8], BF16)
    actT = alloc([128, 16, 512], BF16)
    xres = [alloc([128, D], F32) for _ in range(4)]
    big0 = cur[0]
    BIGSZ = 80 * K
    cur[0] += BIGSZ
    assert cur[0] <= AR_BYTES, ("SBUF overflow", cur[0])
    qkT_ml = view_at(big0, [128, 16, 512], BF16)
    qkT_r = view_at(big0 + 16 * K, [128, 16, 512], BF16)
    vml = view_at(big0 + 32 * K, [128, 4, 4, 258], BF16)
    rv = view_at(big0 + 41 * K, [128, 4, 4, 256], BF16)
    og = view_at(big0 + 49 * K, [128, 4, GW], BF16)
    gs = view_at(big0 + 57 * K, [128, 4, GW], BF16)
    tmpb = big0 + 65 * K
    ubuf = view_at(tmpb, [128, 516], F32)
    cacc = view_at(tmpb + 2112, [128, 512], F32)
    cs_t = view_at(tmpb + 4352, [128, 2, 512], F32)
    rta = view_at(tmpb + 8448, [128, 512], F32)
    rtb = view_at(tmpb + 10496, [128, 512], F32)
    gml_b = view_at(tmpb, [128, GW], F32)
    gret_b = view_at(tmpb + 4 * K, [128, GW], F32)
    hT = view_at(big0, [128, 44, 512], BF16)
    sg = [view_at(big0 + 44 * K + i * 2 * K, [128, 512], F32) for i in range(2)]
    sbuf_used = cur[0]

    psb = [nc.alloc_psum_tensor("psb%d" % i, [128, 512], F32) for i in range(8)]

    def ps16(i):
        return psb[i][:].bitcast(BF16)

    P.dma('pool', 'c_ident', ident, cst[:, 0:128])
    P.dma('pool', 'c_linc', linc, cst[:, 384:512])
    P.dma('pool', 'c_lrest', lrest, cst[:, 512:640])
    P.dma('sp', 'c_maskle', maskle, cst[:, 128:256])
    P.dma('sp', 'c_ones', onesf, cst[:, 256:384])
    P.dma('sp', 'c_retc', retc, cst[:, 640:648])
    gl = [("norm_mix0", 0), ("norm_ffn0", 1), ("norm_mix1", 2), ("norm_ffn1", 3)]
    for n, gi in gl:
        P.dma('sp', 'c_gain%d' % gi, gains[:, gi, :], vf[n].rearrange("(c p) -> p c", p=128),
              allow_slow_non_contiguous=True)
    for k_ in range(4):
        P.dma('sp', 'c_wcv', wcv[:, k_, :], wconv[k_, :].rearrange("(c p) -> p c", p=128),
              allow_slow_non_contiguous=True)
    P.dma('sp', 'c_bgt', bgt, vf["b_gates0"].partition_broadcast(128))
    P.memset('pool', epsb, EPS)
    P.memset('pool', oneb, 1.0)
    P.memset('pool', convc, 0.0)
    P.memset('pool', Cst, 0.0)
    P.memset('pool', Cb, 0.0)
    P.memset('pool', Rst, 0.0)
    P.memset('pool', Rb, 0.0)

    cast_i = [0]

    def cast_weights(names):
        for n in names:
            for si, (kc0, nk, c0, ncol) in enumerate(SL[n]):
                dst = wbs[n][si][:, 0:nk * ncol].rearrange("p (c n) -> p c n", c=nk)
                src = wf[n][kc0 * 128:(kc0 + nk) * 128, c0:c0 + ncol].rearrange("(c p) n -> p c n", p=128)
                P.dma('pool', 'cast%d' % (cast_i[0] % 16), dst, src, rk=[], wk=["%s#%d" % (n, si)])
                cast_i[0] += 1

    P.dma('pool', 'cast15', wgb.rearrange("p (c n) -> p c n", c=16),
          wf["w_in0"][:, 8192:8200].rearrange("(c p) n -> p c n", p=128), rk=[], wk=["wgate#"],
          allow_slow_non_contiguous=True)
    cast_weights(["w_in0"])

    slab_i = [0]

    xbuf = [view_at(big0 + 48 * K, [128, 16, 512], BF16), view_at(big0 + 64 * K, [128, 16, 512], BF16)]
    slab_pool = [wbuf]

    def load_slab(n, kc0, nk, c0, ncol):
        pool_ = slab_pool[0]
        i = slab_i[0] % len(pool_)
        slab_i[0] += 1
        buf = pool_[i]
        si = SLI[n][(kc0, nk, c0, ncol)]
        src = wbs[n][si][:, 0:nk * ncol].rearrange("p (c n) -> p c n", c=nk)
        P.dma('sp', 'wbuf%d' % i, buf[:, 0:nk, 0:ncol], src, rk=["%s#%d" % (n, si)])
        return buf

    tcnt = [0]

    def rms_to_T(gi):
        P.memset('dve', ss4, 0.0)
        for j in range(4):
            P.act(xs[j % 2], xres[j], AF.Square, accum_out=ss4[:, j:j + 1])
        P.act(rstd4, ss4, AF.Sqrt, scale=1.0 / D, bias=epsb[:, 0:1])
        P.recip(rstd4, rstd4)
        for j in range(4):
            xj = xs[j % 2]
            P.act(xj, xres[j], AF.Copy, scale=rstd4[:, j:j + 1])
            for cg in range(4):
                bank = 6 + (tcnt[0] % 2)
                tcnt[0] += 1
                pv = ps16(bank)
                for ci in range(4):
                    c = cg * 4 + ci
                    P.tr(pv[:, ci * 128:(ci + 1) * 128], xj[:, c * 128:(c + 1) * 128], ident)
                gb = gains[:, gi, cg * 4:(cg + 1) * 4].unsqueeze(2).to_broadcast([128, 4, 128])
                P.tt('dve', actT[:, cg * 4:(cg + 1) * 4, j * 128:(j + 1) * 128],
                     pv[:, 0:512].rearrange("p (a b) -> p a b", a=4), gb, ALU.mult)

    def tok_major(slab, bank0, epi):
        for j in range(4):
            pb = psb[bank0 + j]
            for kc in range(16):
                P.mm(pb[:, 0:512], actT[:, kc, j * 128:(j + 1) * 128], slab[:, kc, :],
                     start=(kc == 0), stop=(kc == 15))
            epi(j, pb)

    def feat_major(slab, fcl, bank):
        pb = psb[bank]
        for kc in range(16):
            P.mm(pb[:, 0:512], slab[:, kc, fcl * 128:(fcl + 1) * 128], actT[:, kc, :],
                 start=(kc == 0), stop=(kc == 15))
        return pb

    def ffn(layer, mid_hook=None):
        slab_pool[0] = [wbuf[0], wbuf[1], xbuf[0], xbuf[1]]
        ffn_body(layer, mid_hook)
        slab_pool[0] = wbuf

    def ffn_body(layer, mid_hook):
        gu = "w_gu%d" % layer
        dn = "w_down%d" % layer
        bi = 0
        for sgi in range(11):
            gslab = load_slab(gu, 0, 16, sgi * 512, 512)
            uslab = load_slab(gu, 0, 16, FF + sgi * 512, 512)
            for hcl in range(4):
                hc = sgi * 4 + hcl
                pa = feat_major(gslab, hcl, (bi % 2) * 2)
                pu = feat_major(uslab, hcl, (bi % 2) * 2 + 1)
                s_ = sg[bi % 2]
                bi += 1
                P.act(s_, pa[:, 0:512], AF.Silu)
                P.tt('dve', hT[:, hc, :], pu[:, 0:512], s_, ALU.mult)
        if mid_hook is not None:
            mid_hook()
        for g in range(4):
            for si, (kc0, nk) in enumerate(((0, 16), (16, 16), (32, 12))):
                slab = load_slab(dn, kc0, nk, g * 512, 512)
                for j in range(4):
                    pb = psb[4 + j]
                    for kc in range(nk):
                        P.mm(pb[:, 0:512], hT[:, kc0 + kc, j * 128:(j + 1) * 128], slab[:, kc, :],
                             start=(si == 0 and kc == 0), stop=(si == 2 and kc == nk - 1))
            for j in range(4):
                P.tt('dve', xres[j][:, g * 512:(g + 1) * 512], psb[4 + j][:, 0:512],
                     xres[j][:, g * 512:(g + 1) * 512], ALU.add)

    def out_proj(wname):
        for g in range(4):
            slab = load_slab(wname, 0, 16, g * 512, 512)

            def epi(j, pb, g=g):
                P.tt('dve', xres[j][:, g * 512:(g + 1) * 512], pb[:, 0:512],
                     xres[j][:, g * 512:(g + 1) * 512], ALU.add)
            tok_major(slab, 0, epi)

    for tb in range(NB):
        t0 = tb * 512
        for j in range(4):
            P.dma('sp', 'xres%d' % j, xres[j], x[t0 + j * 128:t0 + (j + 1) * 128, :])
        P.dma('sp', 'cs_t', cs_t[:, 0, :], cosT[:, t0:t0 + 512])
        P.dma('sp', 'cs_t', cs_t[:, 1, :], sinT[:, t0:t0 + 512])
        rms_to_T(0)
        P.dma('sp', 'wgate', wgate, wgb.rearrange("p (c n) -> p c n", c=16), rk=["wgate#"])
        pg = psb[4]
        for j in range(4):
            for kc in range(16):
                P.mm(pg[:, j * 8:(j + 1) * 8], actT[:, kc, j * 128:(j + 1) * 128], wgate[:, kc, :],
                     start=(kc == 0), stop=(kc == 15))
        for j in range(4):
            P.tt('dve', gts[:, j, :], pg[:, j * 8:(j + 1) * 8], bgt, ALU.add)
        gsp3 = gsp.rearrange("p (j h) -> p j h", j=4)
        P.act(gsp3, gts[:, :, 4:8], AF.Exp, scale=-1.0)
        P.act(gsp, gsp, AF.Ln, bias=oneb[:, 0:1])
        pcs = psb[5]
        P.mm(pcs[:, 0:16], maskle, gsp, start=True, stop=True)
        P.mm(pcs[:, 16:32], onesf, gsp, start=True, stop=True)
        P.copy('dve', gcs, pcs[:, 0:16])
        P.tt('dve', gws.rearrange("p (j h) -> p j h", j=4), gts[:, :, 0:4],
             gcs.rearrange("p (j h) -> p j h", j=4), ALU.add)
        P.act(gws, gws, AF.Exp)
        P.act(gosc, gcs, AF.Exp, scale=-1.0)
        P.ts('dve', gosc, gosc, 1.0 / 16.0, None, op0=ALU.mult)
        P.act(geg, pcs[:, 16:32], AF.Exp, scale=-1.0)

        if stop_after == 'gates':
            pr = [gts.rearrange("p a b -> p (a b)"), gsp, gcs, gws, gosc, geg, ss4, rstd4]
            for i_, a_ in enumerate(pr):
                w_ = a_.shape[1]
                P.copy('dve', xres[3][:, 0:w_], a_)
                P.dma('sp', 'yst3', y[i_ * 128:(i_ + 1) * 128, 0:w_], xres[3][:, 0:w_])
            P.copy('dve', xres[2][:, 0:512], actT[:, 0, :])
            P.dma('sp', 'yst2', y[7 * 128:8 * 128, 512:1024], xres[2][:, 0:512])
            st = P.finalize(final_lanes=['yst3', 'yst2'])
            return nc, st, sbuf_used
        for g in range(4):
            slab = load_slab("w_in0", 0, 16, g * 512, 512)
            for fcl in range(4):
                fc = g * 4 + fcl
                pb = feat_major(slab, fcl, fc % 4)
                P.copy('act', ubuf[:, 3:515], pb[:, 0:512])
                P.copy('pool', ubuf[:, 0:3], convc[:, fc, :])
                P.ts('pool', cacc, ubuf[:, 3:515], wcv[:, 3, fc:fc + 1], None, op0=ALU.mult)
                for k in (2, 1, 0):
                    P.stt('dve', cacc, ubuf[:, k:k + 512], wcv[:, k, fc:fc + 1], cacc,
                          ALU.mult, ALU.add)
                P.copy('pool', convc[:, fc, :], ubuf[:, 512:515])
                P.act(qkT_ml[:, fc, :], cacc, AF.Silu)
        for g in (4, 5):
            slab = load_slab("w_in0", 0, 16, g * 512, 512)

            def epi(j, pb, g=g):
                for hl in range(2):
                    h = (g - 4) * 2 + hl
                    P.ts('dve', vml[:, j, h, 0:256], pb[:, hl * 256:(hl + 1) * 256],
                         gws[:, j * 4 + h:j * 4 + h + 1], None, op0=ALU.mult)
                    P.copy('dve', vml[:, j, h, 256:257], gws[:, j * 4 + h:j * 4 + h + 1])
            tok_major(slab, 0, epi)
        for g in (6, 7):
            slab = load_slab("w_in0", 0, 16, g * 512, 512)

            def epi(j, pb, g=g):
                P.act(og[:, j, (g - 6) * 512:(g - 5) * 512], pb[:, 0:512], AF.Sigmoid)
            tok_major(slab, 0, epi)
        for g in (8, 9, 10, 11):
            slab = load_slab("w_in0", 0, 16, g * 512, 512)
            for hl in range(2):
                fc = (g - 8) * 4 + hl * 2
                p1 = feat_major(slab, hl * 2, 0 + hl * 2)
                p2 = feat_major(slab, hl * 2 + 1, 1 + hl * 2)
                ta = rta
                tb_ = rtb
                P.tt('dve', ta, p1[:, 0:512], cs_t[:, 0, :], ALU.mult)
                P.tt('dve', tb_, p2[:, 0:512], cs_t[:, 1, :], ALU.mult)
                P.tt('pool', qkT_r[:, fc, :], ta, tb_, ALU.subtract)
                P.tt('dve', ta, p1[:, 0:512], cs_t[:, 1, :], ALU.mult)
                P.tt('dve', tb_, p2[:, 0:512], cs_t[:, 0, :], ALU.mult)
                P.tt('pool', qkT_r[:, fc + 1, :], ta, tb_, ALU.add)
        for g in (12, 13):
            slab = load_slab("w_in0", 0, 16, g * 512, 512)

            def epi(j, pb, g=g):
                for hl in range(2):
                    h = (g - 12) * 2 + hl
                    P.ts('dve', rv[:, j, h, :], pb[:, hl * 256:(hl + 1) * 256],
                         retc[:, h:h + 1], None, op0=ALU.mult)
            tok_major(slab, 0, epi)
        for g in (14, 15):
            slab = load_slab("w_in0", 0, 16, g * 512, 512)

            def epi(j, pb, g=g):
                P.act(gs[:, j, (g - 14) * 512:(g - 13) * 512], pb[:, 0:512], AF.Silu)
            tok_major(slab, 0, epi)
        P.dma('sp', 'gml_b', gml_b, vf["g_ml0"].partition_broadcast(128))
        P.dma('sp', 'gret_b', gret_b, vf["g_ret0"].partition_broadcast(128))
        for j in range(4):
            P.tt('pool', og[:, j, :], og[:, j, :], gml_b, ALU.mult)
            P.tt('pool', gs[:, j, :], gs[:, j, :], gret_b, ALU.mult)

        sm = small
        for j in range(4):
            jc = slice(j * 128, (j + 1) * 128)
            yt = ytok[j % 2]
            for hh in range(8):
                is_ml = hh < 4
                h = hh % 4
                par = hh % 2
                qk = qkT_ml if is_ml else qkT_r
                vv = vml[:, j, h, 0:257] if is_ml else rv[:, j, h, :]
                nv = 257 if is_ml else 256
                Sf = Cst if is_ml else Rst
                Sb = Cb if is_ml else Rb
                pk = ps16(6)
                for dc in range(2):
                    P.tr(pk[:, dc * 128:(dc + 1) * 128], qk[:, 8 + 2 * h + dc, jc], ident)
                P.copy('act', ktm[par], pk[:, 0:256])
                pst = psb[par]
                for dc in range(2):
                    P.mm(pst[:, 0:128], qk[:, 8 + 2 * h + dc, jc], qk[:, 2 * h + dc, jc],
                         start=(dc == 0), stop=(dc == 1))
                P.tt('dve', PT[par], pst[:, 0:128], maskle, ALU.mult)
                po = psb[2 + par]
                P.mm(po[:, 0:nv], PT[par], vv, start=True, stop=False)
                for dc in range(2):
                    P.mm(po[:, 0:nv], qk[:, 2 * h + dc, jc], Sb[:, h, dc, 0:nv],
                         start=False, stop=(dc == 1))
                for dc in range(2):
                    pd = psb[4 + dc]
                    P.mm(pd[:, 0:nv], ktm[par][:, dc * 128:(dc + 1) * 128], vv, start=True, stop=True)
                    P.tt('dve', Sf[:, h, dc, 0:nv], pd[:, 0:nv], Sf[:, h, dc, 0:nv], ALU.add)
                    if is_ml:
                        P.ts('pool', Sf[:, h, dc, 0:nv], Sf[:, h, dc, 0:nv],
                             geg[:, j * 4 + h:j * 4 + h + 1], None, op0=ALU.mult)
                    else:
                        P.ts('pool', Sf[:, h, dc, 0:nv], Sf[:, h, dc, 0:nv], RET_DECAY[h], None,
                             op0=ALU.mult)
                    P.copy('pool', Sb[:, h, dc, 0:nv], Sf[:, h, dc, 0:nv])
                b0 = hh * 8
                if is_ml:
                    osc = gosc[:, j * 4 + h:j * 4 + h + 1]
                    P.ts('dve', sm[:, b0:b0 + 1], po[:, 256:257], osc, None, op0=ALU.mult)
                    P.ts('dve', sm[:, b0 + 6:b0 + 7], sm[:, b0:b0 + 1], -1.0, None, op0=ALU.mult)
                    P.tt('dve', sm[:, b0:b0 + 1], sm[:, b0:b0 + 1], sm[:, b0 + 6:b0 + 7], ALU.max)
                    P.ts('dve', sm[:, b0:b0 + 1], sm[:, b0:b0 + 1], 1.0, None, op0=ALU.max)
                    P.recip(sm[:, b0:b0 + 1], sm[:, b0:b0 + 1])
                    P.tt('dve', sm[:, b0 + 1:b0 + 2], sm[:, b0:b0 + 1], osc, ALU.mult)
                    P.memset('dve', sm[:, b0 + 2:b0 + 3], 0.0)
                    P.act(ctile[par], po[:, 0:256], AF.Square, scale=sm[:, b0 + 1:b0 + 2],
                          accum_out=sm[:, b0 + 2:b0 + 3])
                    P.act(sm[:, b0 + 3:b0 + 4], sm[:, b0 + 2:b0 + 3], AF.Sqrt, scale=1.0 / 256.0,
                          bias=epsb[:, 0:1])
                    P.recip(sm[:, b0 + 3:b0 + 4], sm[:, b0 + 3:b0 + 4])
                    P.tt('dve', sm[:, b0 + 4:b0 + 5], sm[:, b0 + 3:b0 + 4], sm[:, b0 + 1:b0 + 2],
                         ALU.mult)
                    P.stt('dve', yt[:, h * 256:(h + 1) * 256], po[:, 0:256], sm[:, b0 + 4:b0 + 5],
                          og[:, j, h * 256:(h + 1) * 256], ALU.mult, ALU.mult)
                else:
                    osc = retc[:, 4 + h:5 + h]
                    P.memset('dve', sm[:, b0:b0 + 2], 0.0)
                    P.act(ctile[par], po[:, 0:256], AF.Copy, accum_out=sm[:, b0:b0 + 1])
                    P.ts('dve', sm[:, b0 + 2:b0 + 3], sm[:, b0:b0 + 1], -1.0 / 256.0, None,
                         op0=ALU.mult)
                    P.act(ctile[par], po[:, 0:256], AF.Square, bias=sm[:, b0 + 2:b0 + 3],
                          accum_out=sm[:, b0 + 1:b0 + 2])
                    P.tt('dve', sm[:, b0 + 3:b0 + 4], sm[:, b0 + 1:b0 + 2], osc, ALU.mult)
                    P.tt('dve', sm[:, b0 + 3:b0 + 4], sm[:, b0 + 3:b0 + 4], osc, ALU.mult)
                    P.act(sm[:, b0 + 3:b0 + 4], sm[:, b0 + 3:b0 + 4], AF.Sqrt, scale=1.0 / 256.0,
                          bias=epsb[:, 0:1])
                    P.recip(sm[:, b0 + 3:b0 + 4], sm[:, b0 + 3:b0 + 4])
                    P.tt('dve', sm[:, b0 + 4:b0 + 5], sm[:, b0 + 3:b0 + 4], osc, ALU.mult)
                    P.tt('dve', sm[:, b0 + 5:b0 + 6], sm[:, b0 + 4:b0 + 5], sm[:, b0 + 2:b0 + 3],
                         ALU.mult)
                    P.act(ctile[par], po[:, 0:256], AF.Identity, scale=sm[:, b0 + 4:b0 + 5],
                          bias=sm[:, b0 + 5:b0 + 6])
                    P.tt('pool', yt[:, GW + h * 256:GW + (h + 1) * 256], ctile[par],
                         gs[:, j, h * 256:(h + 1) * 256], ALU.mult)
            for cg in range(4):
                pv = ps16(7)
                for ci in range(4):
                    c = cg * 4 + ci
                    P.tr(pv[:, ci * 128:(ci + 1) * 128], yt[:, c * 128:(c + 1) * 128], ident)
                P.copy('act', actT[:, cg * 4:(cg + 1) * 4, jc],
                       pv[:, 0:512].rearrange("p (a b) -> p a b", a=4))
        if stop_after == 'ymix':
            for kc in range(4):
                P.copy('dve', xres[kc].rearrange("p (a b) -> p a b", a=4), actT[:, kc * 4:(kc + 1) * 4, :])
                P.dma('sp', 'yst%d' % kc, y[t0 + kc * 128:t0 + (kc + 1) * 128, :], xres[kc])
            continue
        if tb == 0:
            cast_weights(["w_out0", "w_gu0", "w_down0"])
        out_proj("w_out0")
        if stop_after == 'x1':
            for j in range(4):
                P.dma('sp', 'yst%d' % j, y[t0 + j * 128:t0 + (j + 1) * 128, :], xres[j])
            continue
        rms_to_T(1)
        ffn(0)
        if tb == 0:
            cast_weights(["w_qkv1", "w_out1", "w_gu1", "w_down1"])
        for j in range(4):
            P.dma('sp', 'x2st%d' % j, x2[t0 + j * 128:t0 + (j + 1) * 128, :], xres[j],
                  wk=["x2#%d" % tb])

    if stop_after in ('x1', 'ymix'):
        st = P.finalize(final_lanes=['yst%d' % j for j in range(4)])
        return nc, st, sbuf_used
    if stop_after == 'l0':
        for tb in range(NB):
            for j in range(4):
                t0 = tb * 512 + j * 128
                P.dma('sp', 'xres%d' % j, xres[j], x2[t0:t0 + 128, :], rk=["x2#%d" % tb])
                P.dma('sp', 'yst%d' % j, y[t0:t0 + 128, :], xres[j])
        st = P.finalize(final_lanes=['yst%d' % j for j in range(4)])
        return nc, st, sbuf_used

    stg = view_at(big0, [128, 4, 512], BF16)
    stv = view_at(big0 + 4 * K, [128, 4, 512], BF16)
    for tb in range(NB):
        t0 = tb * 512
        for j in range(4):
            P.dma('sp', 'xres%d' % j, xres[j], x2[t0 + j * 128:t0 + (j + 1) * 128, :],
                  rk=["x2#%d" % tb])
        rms_to_T(2)
        for g in range(8):
            slab = load_slab("w_qkv1", 0, 16, g * 512, 512)
            for fcl in range(4):
                pb = feat_major(slab, fcl, fcl)
                if g < 4:
                    P.act(stg[:, fcl, :], pb[:, 0:512], AF.Copy, scale=128.0 ** -0.5)
                else:
                    P.copy('dve', stg[:, fcl, :], pb[:, 0:512])
            dst = (qT1 if g < 4 else kT1)[(g % 4) * 4:(g % 4) * 4 + 4, :, t0:t0 + 512]
            P.dma('sp', 'stg', dst.rearrange("h p t -> p h t"), stg,
                  wk=[("qT1#%d" if g < 4 else "kT1#%d") % tb])
        for g in range(8, 12):
            slab = load_slab("w_qkv1", 0, 16, g * 512, 512)

            def epi(j, pb, g=g):
                P.copy('act' if j % 2 else 'dve', stv[:, j, :], pb[:, 0:512])
            tok_major(slab, 0, epi)
            c0 = (g - 8) * 512
            P.dma('sp', 'stv', v1[t0:t0 + 512, c0:c0 + 512].rearrange("(j p) c -> p j c", p=128),
                  stv, wk=["v1#%d" % tb])

    amask = view_at(big0, [128, 4, 512], BF16)
    kTh = [view_at(big0 + 4 * K + i * 24 * K, [128, S], BF16) for i in range(2)]
    qTh = [view_at(big0 + 4 * K + i * 24 * K + 8 * K, [128, S], BF16) for i in range(2)]
    vh = [view_at(big0 + 4 * K + i * 24 * K + 16 * K, [128, NKB, 128], BF16) for i in range(2)]
    assert 4 * K + 48 * K <= BIGSZ and S * 2 <= 8 * K
    def wtiles(sidx):
        f = wbuf[sidx].bitcast(F32).rearrange("p a b -> p (a b)")
        b = wbuf[sidx].rearrange("p a b -> p (a b)")
        return dict(e=[f[:, 0:512], f[:, 512:1024]], ecs=[f[:, 1024:1536], f[:, 1536:2048]],
                    sp=[b[:, 4096:4608], b[:, 4608:5120]], a=[b[:, 5120:5632], b[:, 5632:6144]],
                    o=[b[:, 6144:6656], b[:, 6656:7168]])
    WT = [wtiles(0), wtiles(1)]
    P.dma('pool', 'c_amask', amask, amask_d.rearrange("p (a b) -> p a b", a=4))
    allq = ["qT1#%d" % tb for tb in range(NB)]
    allk = ["kT1#%d" % tb for tb in range(NB)]
    allv = ["v1#%d" % tb for tb in range(NB)]
    ocnt = [0, 0]
    for hpair in range(8):
        for sidx in range(2):
            h = hpair * 2 + sidx
            P.dma('sp', 'kTh%d' % sidx, kTh[sidx], kT1[h], rk=allk)
            P.dma('sp', 'qTh%d' % sidx, qTh[sidx], qT1[h], rk=allq)
            P.dma('sp', 'vh%d' % sidx, vh[sidx],
                  v1[:, h * 128:(h + 1) * 128].rearrange("(b p) e -> p b e", p=128), rk=allv)
        for G in range(NB):
            qs = slice(G * 512, (G + 1) * 512)
            kbs = list(range(4 * G + 3, -1, -1))

            L = len(kbs)

            def zmm(sidx, n):
                pz = psb[2 * sidx + n % 2]
                kb = kbs[n]
                P.mm(pz[:, 0:512], kTh[sidx][:, kb * 128:(kb + 1) * 128], qTh[sidx][:, qs],
                     start=True, stop=True)

            def expln(sidx, n):
                w = WT[sidx]
                kb = kbs[n]
                pz = psb[2 * sidx + n % 2]
                e = w['e'][n % 2]
                sp_ = w['sp'][n % 2]
                P.act(e, pz[:, 0:512], AF.Exp)
                P.act(sp_, e, AF.Ln, bias=oneb[:, 0:1])
                if kb >= 4 * G:
                    i = kb - 4 * G
                    P.tt('pool', sp_, sp_, amask[:, i, :], ALU.mult)
                    P.tt('pool', e, e, amask[:, i, :], ALU.mult)

            for n in range(-2, L + 1):
                for sidx in range(2):
                    if 0 <= n + 2 < L:
                        zmm(sidx, n + 2)
                for sidx in range(2):
                    if 0 <= n + 1 < L:
                        expln(sidx, n + 1)
                for sidx in range(2):
                    if 0 <= n - 1 < L:
                        w = WT[sidx]
                        m = n - 1
                        P.mm(psb[6 + sidx][:, 0:512], vh[sidx][:, kbs[m], :], w['a'][m % 2],
                             start=(m == 0), stop=(m == L - 1))
                for sidx in range(2):
                    if 0 <= n < L:
                        w = WT[sidx]
                        P.act(w['ecs'][n % 2], psb[4 + sidx][:, 0:512], AF.Exp, scale=-1.0)
                for sidx in range(2):
                    w = WT[sidx]
                    if 0 <= n < L - 1:
                        P.mm(psb[4 + sidx][:, 0:512], lrest, w['sp'][n % 2], start=False, stop=False)
                    if 0 <= n + 1 < L:
                        P.mm(psb[4 + sidx][:, 0:512], linc, w['sp'][(n + 1) % 2],
                             start=(n + 1 == 0), stop=False)
                for sidx in range(2):
                    if 0 <= n < L:
                        w = WT[sidx]
                        P.tt('dve', w['a'][n % 2], w['e'][n % 2], w['ecs'][n % 2], ALU.mult)
            for sidx in range(2):
                h = hpair * 2 + sidx
                ot = WT[sidx]['o'][ocnt[sidx] % 2]
                ocnt[sidx] += 1
                P.copy('dve', ot, psb[6 + sidx][:, 0:512])
                P.dma('sp', 'ost%d' % sidx, oT1[h, :, qs], ot, wk=["oT1#%d" % G])

    fnb = view_at(cst_off, [128, D], F32)
    P.dma('sp', 'fnb', fnb, vf["final_norm"].partition_broadcast(128))
    for tb in range(NB):
        t0 = tb * 512
        for j in range(4):
            P.dma('sp', 'xres%d' % j, xres[j], x2[t0 + j * 128:t0 + (j + 1) * 128, :],
                  rk=["x2#%d" % tb])
        def load_actT(tbn):
            P.dma('sp', 'actT', actT, oT1[:, :, tbn * 512:tbn * 512 + 512].rearrange("h p t -> p h t"),
                  rk=["oT1#%d" % tbn])
        if tb == 0:
            load_actT(0)
        out_proj("w_out1")
        rms_to_T(3)
        ffn(1, mid_hook=(lambda tbn=tb + 1: load_actT(tbn)) if tb + 1 < NB else None)
        P.memset('dve', ss4, 0.0)
        for j in range(4):
            P.act(xs[j % 2], xres[j], AF.Square, accum_out=ss4[:, j:j + 1])
        P.act(rstd4, ss4, AF.Sqrt, scale=1.0 / D, bias=epsb[:, 0:1])
        P.recip(rstd4, rstd4)
        for j in range(4):
            P.stt('dve', xres[j], xres[j], rstd4[:, j:j + 1], fnb,
                  ALU.mult, ALU.mult)
            P.dma('sp', 'yst%d' % j, y[t0 + j * 128:t0 + (j + 1) * 128, :], xres[j])
    st = P.finalize(final_lanes=['yst%d' % j for j in range(4)])
    return nc, st, sbuf_used


_CACHE = {}


def kernel(**inputs):
    x = np.asarray(inputs["x"], dtype=np.float32)
    B, S, _ = x.shape
    if S not in _CACHE:
        _CACHE[S] = (build(S)[0], host_consts(S))
    nc, consts = _CACHE[S]
    base = {}
    for n, _, _ in WSPEC:
        base[n] = np.ascontiguousarray(np.asarray(inputs[n], dtype=np.float32))
    for n, _ in VSPEC:
        base[n] = np.ascontiguousarray(np.asarray(inputs[n], dtype=np.float32))
    base["w_conv0"] = np.ascontiguousarray(np.asarray(inputs["w_conv0"], dtype=np.float32))
    base.update(consts)
    in_maps = []
    for b in range(B):
        m = dict(base)
        m["x"] = np.ascontiguousarray(x[b])
        in_maps.append(m)
    res = run_bass_kernel_spmd(nc, in_maps, core_ids=list(range(B)))
    return np.stack([np.asarray(r["y"], dtype=np.float32) for r in res.results], axis=0)
```

```python
import numpy as np
import concourse.bass as bass
import concourse.mybir as mybir
from concourse.bass_utils import run_bass_kernel_spmd

F32 = mybir.dt.float32
BF16 = mybir.dt.bfloat16
AF = mybir.ActivationFunctionType
ALU = mybir.AluOpType

D = 2048
GW = 1024
FF = 5632
NIN = 8200
EPS = 1e-6
EPOCH = 16000
COMPUTE = ('pe', 'act', 'dve', 'pool')
PG = 1024


def _keys(a):
    if isinstance(a, str):
        return (a,)
    if isinstance(a, tuple):
        return a
    sp = a.space.name
    name = a.tensor.name
    if sp == 'DRAM':
        return (name,)
    ap = a.ap
    stride = ap[0][0]
    off = a.offset % stride if stride > 0 else a.offset
    ext = 0
    for st, cnt in ap[1:]:
        ext += (cnt - 1) * abs(st)
    sz = mybir.dt.size(a.dtype)
    lo = off * sz
    hi = (off + ext) * sz + sz - 1
    pg = 2048 if sp == 'PSUM' else PG
    return tuple((name, p) for p in range(lo // pg, hi // pg + 1))


class Instr:
    __slots__ = ('idx', 'eng', 'fn', 'deps', 'lane', 'lane_idx', 'needs_inc', 'seq',
                 'waits', 'is_dma', 'vc')


class Prog:
    def __init__(self, nc):
        self.nc = nc
        self.instrs = []
        self.last_w = {}
        self.readers = {}
        self.lane_last = {}
        self.lane_cnt = {}
        self.lane_sem = {}

    def add(self, eng, fn, reads, writes, lane=None):
        ins = Instr()
        ins.idx = len(self.instrs)
        ins.eng = eng
        ins.fn = fn
        ins.lane = lane
        ins.is_dma = lane is not None
        ins.needs_inc = False
        ins.seq = None
        ins.waits = None
        ins.vc = None
        rk = []
        for r in reads:
            rk.extend(_keys(r))
        wk = []
        for w in writes:
            wk.extend(_keys(w))
        rk = list(dict.fromkeys(rk))
        wk = list(dict.fromkeys(wk))
        deps = set()
        for k in rk:
            if k in self.last_w:
                deps.add(self.last_w[k])
        for k in wk:
            if k in self.last_w:
                deps.add(self.last_w[k])
            for r in self.readers.get(k, ()):
                deps.add(r)
        if lane is not None:
            if lane in self.lane_last:
                deps.add(self.lane_last[lane])
            self.lane_last[lane] = ins.idx
            self.lane_cnt[lane] = self.lane_cnt.get(lane, 0) + 1
            ins.lane_idx = self.lane_cnt[lane]
        else:
            ins.lane_idx = None
        deps.discard(ins.idx)
        if eng == 'pe' and not ins.is_dma:
            deps = {d for d in deps
                    if not (self.instrs[d].eng == 'pe' and not self.instrs[d].is_dma)}
        ins.deps = deps
        for k in wk:
            self.last_w[k] = ins.idx
            self.readers[k] = []
        wks = set(wk)
        for k in rk:
            if k in wks:
                continue
            lst = self.readers.setdefault(k, [])
            if not ins.is_dma:
                lst[:] = [r for r in lst
                          if self.instrs[r].is_dma or self.instrs[r].eng != eng]
            lst.append(ins.idx)
        self.instrs.append(ins)
        return ins

    def dma(self, q, lane, out, in_, rk=None, wk=None, **kw):
        r = rk if rk is not None else [in_]
        w = wk if wk is not None else [out]
        return self.add(q, lambda e: e.dma_start(out=out, in_=in_, **kw), r, w, lane=lane)

    def mm(self, out, lhsT, rhs, start=True, stop=True):
        return self.add('pe', lambda e: e.matmul(out, lhsT, rhs, start=start, stop=stop),
                        [lhsT, rhs], [out])

    def tr(self, out, in_, ident):
        return self.add('pe', lambda e: e.transpose(out, in_, ident), [in_, ident], [out])

    def act(self, out, in_, func, bias=None, scale=None, accum_out=None):
        kw = {}
        r = [in_]
        w = [out]
        if bias is not None:
            kw['bias'] = bias
            if not isinstance(bias, (int, float)):
                r.append(bias)
        if scale is not None:
            kw['scale'] = scale
            if not isinstance(scale, (int, float)):
                r.append(scale)
        if accum_out is not None:
            kw['accum_out'] = accum_out
            w.append(accum_out)
        return self.add('act', lambda e: e.activation(out=out, in_=in_, func=func, **kw), r, w)

    def tt(self, eng, out, in0, in1, op):
        return self.add(eng, lambda e: e.tensor_tensor(out=out, in0=in0, in1=in1, op=op),
                        [in0, in1], [out])

    def ts(self, eng, out, in0, s1, s2=None, op0=ALU.mult, op1=None):
        r = [in0]
        for s in (s1, s2):
            if s is not None and not isinstance(s, (int, float)):
                r.append(s)
        kw = {}
        if op1 is not None:
            kw['op1'] = op1
        return self.add(eng, lambda e: e.tensor_scalar(out=out, in0=in0, scalar1=s1, scalar2=s2,
                                                       op0=op0, **kw), r, [out])

    def stt(self, eng, out, in0, scalar, in1, op0, op1):
        r = [in0, in1]
        if not isinstance(scalar, (int, float)):
            r.append(scalar)
        return self.add(eng, lambda e: e.scalar_tensor_tensor(
            out=out, in0=in0, scalar=scalar, in1=in1, op0=op0, op1=op1), r, [out])

    def copy(self, eng, out, in_):
        if eng == 'act':
            return self.add(eng, lambda e: e.copy(out=out, in_=in_), [in_], [out])
        return self.add(eng, lambda e: e.tensor_copy(out=out, in_=in_), [in_], [out])

    def memset(self, eng, out, val):
        return self.add(eng, lambda e: e.memset(out, val), [], [out])

    def recip(self, out, in_):
        return self.add('dve', lambda e: e.reciprocal(out=out, in_=in_), [in_], [out])

    def finalize(self, final_lanes=()):
        nc = self.nc
        instrs = self.instrs
        for ins in instrs:
            for d in ins.deps:
                di = instrs[d]
                if not di.is_dma:
                    di.needs_inc = True
        cnt = {e: 0 for e in COMPUTE}
        for ins in instrs:
            if not ins.is_dma and ins.needs_inc:
                ins.seq = cnt[ins.eng]
                cnt[ins.eng] += 1
        sems = {}
        for e in COMPUTE:
            nep = max(1, (cnt[e] + EPOCH - 1) // EPOCH)
            for k in range(nep):
                sems[(e, k)] = nc.alloc_semaphore("s_%s_%d" % (e, k))
        for ln in self.lane_cnt:
            self.lane_sem[ln] = nc.alloc_semaphore("l_%s" % ln)

        def src_of(di):
            if di.is_dma:
                return ('L', di.lane), di.lane_idx * 16
            return (di.eng, di.seq // EPOCH), di.seq % EPOCH + 1

        known = {e: {} for e in ('pe', 'act', 'dve', 'pool', 'sp')}
        nwaits = 0
        for ins in instrs:
            kn = known[ins.eng]
            waits = {}
            for d in ins.deps:
                s, v = src_of(instrs[d])
                if kn.get(s, 0) >= v:
                    continue
                if waits.get(s, 0) < v:
                    waits[s] = v
            for d in ins.deps:
                di = instrs[d]
                s, v = src_of(di)
                if s in waits and di.vc is not None:
                    for s2, v2 in di.vc.items():
                        if kn.get(s2, 0) < v2:
                            kn[s2] = v2
            for s, v in waits.items():
                if kn.get(s, 0) < v:
                    kn[s] = v
            ins.waits = waits
            nwaits += len(waits)
            if ins.is_dma:
                vc = dict(kn)
                vc[('L', ins.lane)] = ins.lane_idx * 16
                ins.vc = vc
            elif ins.needs_inc:
                s, v = src_of(ins)
                vc = dict(kn)
                vc[s] = v
                ins.vc = vc
        self.stats = dict(n=len(instrs), nwaits=nwaits, incs=dict(cnt),
                          nsem=len(sems) + len(self.lane_sem))
        per_eng = {e: [] for e in ('pe', 'act', 'dve', 'pool', 'sp')}
        for ins in instrs:
            per_eng[ins.eng].append(ins)

        def sem_of(s):
            if s[0] == 'L':
                return self.lane_sem[s[1]]
            return sems[s]

        def emit(eng_obj, lst, tail):
            for ins in lst:
                for s, v in ins.waits.items():
                    eng_obj.wait_ge(sem_of(s), v)
                bi = ins.fn(eng_obj)
                if ins.is_dma:
                    bi.then_inc(self.lane_sem[ins.lane], 16)
                elif ins.needs_inc:
                    bi.then_inc(sems[(ins.eng, ins.seq // EPOCH)], 1)
            for ln in tail:
                eng_obj.wait_ge(self.lane_sem[ln], self.lane_cnt[ln] * 16)

        with nc.Block() as block:
            @block.tensor
            def _(e):
                emit(e, per_eng['pe'], ())

            @block.scalar
            def _(e):
                emit(e, per_eng['act'], ())

            @block.vector
            def _(e):
                emit(e, per_eng['dve'], ())

            @block.gpsimd
            def _(e):
                emit(e, per_eng['pool'], ())

            @block.sync
            def _(e):
                emit(e, per_eng['sp'], final_lanes)
        return self.stats


def host_consts(S):
    c = np.zeros((128, 648), np.float32)
    i = np.arange(128)
    c[:, 0:128] = np.eye(128)
    c[:, 128:256] = (i[:, None] <= i[None, :])
    c[:, 256:384] = 1.0
    c[:, 384:512] = (i[:, None] >= i[None, :])
    c[:, 512:640] = (i[:, None] < i[None, :])
    lg = np.log(1.0 - 2.0 ** (-5.0 - 2.0 * np.arange(4, dtype=np.float64)))
    pos = np.arange(128, dtype=np.float64)
    c[:, 640:644] = np.exp(-(pos[:, None] + 1.0) * lg[None, :]) * (256.0 ** -0.5)
    c[:, 644:648] = np.exp((pos[:, None] + 1.0) * lg[None, :])
    am = np.zeros((128, 4, 512), np.float32)
    t = np.arange(512)
    for k in range(4):
        am[:, k, :] = ((128 * k + i)[:, None] < t[None, :])
    inv = (10000.0 ** (-np.arange(0, 256, 2, dtype=np.float32) / np.float32(256))).astype(np.float32)
    ang = inv[:, None] * np.arange(S, dtype=np.float32)[None, :]
    return dict(cst=c, amask=am.reshape(128, 2048),
                cosT=np.cos(ang).astype(np.float32), sinT=np.sin(ang).astype(np.float32))


RET_DECAY = [float(np.exp(128.0 * np.log(1.0 - 2.0 ** (-5.0 - 2.0 * h)))) for h in range(4)]

WSPEC = [("w_in0", D, NIN), ("w_out0", D, D), ("w_gu0", D, 2 * FF), ("w_down0", FF, D),
         ("w_qkv1", D, 3 * D), ("w_out1", D, D), ("w_gu1", D, 2 * FF), ("w_down1", FF, D)]
VSPEC = [("norm_mix0", D), ("b_gates0", 8), ("g_ml0", GW), ("g_ret0", GW), ("norm_ffn0", D),
         ("norm_mix1", D), ("norm_ffn1", D), ("final_norm", D)]


def build(S, stop_after=None):
    NB = S // 512
    NKB = S // 128
    nc = bass.Bass("TRN2", target_bir_lowering=False)
    P = Prog(nc)
    dt = {}

    def din(name, shape, d=F32):
        dt[name] = nc.dram_tensor(name, list(shape), d, kind="ExternalInput").ap()
        return dt[name]

    x = din("x", [S, D])
    wf = {}
    for n, k, m in WSPEC:
        wf[n] = din(n, [k, m])
    vf = {}
    for n, m in VSPEC:
        vf[n] = din(n, [m])
    wconv = din("w_conv0", [4, D])
    cst = din("cst", [128, 648])
    amask_d = din("amask", [128, 2048])
    cosT = din("cosT", [128, S])
    sinT = din("sinT", [128, S])
    y = nc.dram_tensor("y", [S, D], F32, kind="ExternalOutput").ap()
    def slab_specs(n):
        if n in ("w_in0",):
            return [(0, 16, g * 512, 512) for g in range(16)]
        if n in ("w_out0", "w_out1"):
            return [(0, 16, g * 512, 512) for g in range(4)]
        if n in ("w_qkv1",):
            return [(0, 16, g * 512, 512) for g in range(12)]
        if n in ("w_gu0", "w_gu1"):
            r = []
            for sgi in range(11):
                r.append((0, 16, sgi * 512, 512))
                r.append((0, 16, FF + sgi * 512, 512))
            return r
        r = []
        for g in range(4):
            for (kc0, nk) in ((0, 16), (16, 16), (32, 12)):
                r.append((kc0, nk, g * 512, 512))
        return r
    SL = {n: slab_specs(n) for n, _, _ in WSPEC}
    SLI = {n: {sp_: i for i, sp_ in enumerate(SL[n])} for n in SL}
    wbs = {n: nc.dram_tensor(n + "_s", [len(SL[n]), 128, 8192], BF16, kind="Internal").ap()
           for n, _, _ in WSPEC}
    wgb = nc.dram_tensor("wgate_s", [128, 128], BF16, kind="Internal").ap()
    x2 = nc.dram_tensor("x2s", [S, D], F32, kind="Internal").ap()
    qT1 = nc.dram_tensor("qT1", [16, 128, S], BF16, kind="Internal").ap()
    kT1 = nc.dram_tensor("kT1", [16, 128, S], BF16, kind="Internal").ap()
    v1 = nc.dram_tensor("v1", [S, D], BF16, kind="Internal").ap()
    oT1 = nc.dram_tensor("oT1", [16, 128, S], BF16, kind="Internal").ap()

    AR_BYTES = 206 * 1024
    ar = nc.alloc_sbuf_tensor("arena", [128, AR_BYTES // 2], BF16)
    cur = [0]

    def view_at(off, shape, d):
        n = 1
        for s_ in shape[1:]:
            n *= s_
        sz = 4 if d == F32 else 2
        a = ar[:, off // 2: off // 2 + n * sz // 2]
        if d == F32:
            a = a.bitcast(F32)
        if len(shape) == 3:
            a = a.rearrange("p (a b) -> p a b", a=shape[1])
        elif len(shape) == 4:
            a = a.rearrange("p (a b c) -> p a b c", a=shape[1], b=shape[2])
        elif len(shape) == 5:
            a = a.rearrange("p (a b c e) -> p a b c e", a=shape[1], b=shape[2], c=shape[3])
        return a

    def alloc(shape, d):
        n = 1
        for s_ in shape[1:]:
            n *= s_
        nb = n * (4 if d == F32 else 2)
        off = cur[0]
        cur[0] = (off + nb + 63) // 64 * 64
        assert cur[0] <= AR_BYTES, ("SBUF overflow", cur[0])
        return view_at(off, shape, d)

    K = 1024
    ident = alloc([128, 128], BF16)
    linc = alloc([128, 128], BF16)
    lrest = alloc([128, 128], BF16)
    maskle = alloc([128, 128], F32)
    onesf = alloc([128, 128], F32)
    retc = alloc([128, 8], F32)
    gains = alloc([128, 5, 16], F32)
    wcv = alloc([128, 4, 16], F32)
    bgt = alloc([128, 8], F32)
    epsb = alloc([128, 1], F32)
    oneb = alloc([128, 1], F32)
    small = alloc([128, 64], F32)
    gts = alloc([128, 4, 8], F32)
    gsp = alloc([128, 16], F32)
    gcs = alloc([128, 16], F32)
    gws = alloc([128, 16], F32)
    gosc = alloc([128, 16], F32)
    geg = alloc([128, 16], F32)
    ss4 = alloc([128, 4], F32)
    rstd4 = alloc([128, 4], F32)
    convc = alloc([128, 16, 3], F32)
    cst_off = cur[0]
    Cst = alloc([128, 4, 2, 257], F32)
    Cb = alloc([128, 4, 2, 257], BF16)
    Rst = alloc([128, 4, 2, 256], F32)
    Rb = alloc([128, 4, 2, 256], BF16)
    ktm = [alloc([128, 256], BF16) for _ in range(2)]
    PT = [alloc([128, 128], BF16) for _ in range(2)]
    ctile = [alloc([128, 256], F32) for _ in range(2)]
    ytok1 = alloc([128, D], BF16)
    ytok = [ytok1, ytok1]
    xs = [alloc([128, D], BF16), ytok1]
    NWB = 2
    wbuf = [alloc([128, 16, 512], BF16) for _ in range(NWB)]
    wgate = alloc([128, 16, 8], BF16)
    actT = alloc([128, 16, 512], BF16)
    xres = [alloc([128, D], F32) for _ in range(4)]
    big0 = cur[0]
    BIGSZ = 80 * K
    cur[0] += BIGSZ
    assert cur[0] <= AR_BYTES, ("SBUF overflow", cur[0])
    qkT_ml = view_at(big0, [128, 16, 512], BF16)
    qkT_r = view_at(big0 + 16 * K, [128, 16, 512], BF16)
    vml = view_at(big0 + 32 * K, [128, 4, 4, 258], BF16)
    rv = view_at(big0 + 41 * K, [128, 4, 4, 256], BF16)
    og = view_at(big0 + 49 * K, [128, 4, GW], BF16)
    gs = view_at(big0 + 57 * K, [128, 4, GW], BF16)
    tmpb = big0 + 65 * K
    ubufs = [view_at(tmpb, [128, 516], F32), view_at(tmpb + 8448, [128, 516], F32)]
    caccs = [view_at(tmpb + 2112, [128, 512], F32), view_at(tmpb + 10560, [128, 512], F32)]
    cs_t = view_at(tmpb + 4352, [128, 2, 512], F32)
    rta = view_at(tmpb + 8448, [128, 512], F32)
    rtb = view_at(tmpb + 10560, [128, 512], F32)
    gml_b = view_at(tmpb, [128, GW], F32)
    gret_b = view_at(tmpb + 4 * K, [128, GW], F32)
    hT = view_at(big0, [128, 44, 512], BF16)
    sg = [view_at(big0 + 44 * K + i * 2 * K, [128, 512], F32) for i in range(2)]
    sbuf_used = cur[0]

    psb = [nc.alloc_psum_tensor("psb%d" % i, [128, 512], F32) for i in range(8)]

    def ps16(i):
        return psb[i][:].bitcast(BF16)

    P.dma('pool', 'c_ident', ident, cst[:, 0:128])
    P.dma('pool', 'c_linc', linc, cst[:, 384:512])
    P.dma('pool', 'c_lrest', lrest, cst[:, 512:640])
    P.dma('sp', 'c_maskle', maskle, cst[:, 128:256])
    P.dma('sp', 'c_ones', onesf, cst[:, 256:384])
    P.dma('sp', 'c_retc', retc, cst[:, 640:648])
    gl = [("norm_mix0", 0), ("norm_ffn0", 1), ("norm_mix1", 2), ("norm_ffn1", 3)]
    for n, gi in gl:
        P.dma('sp', 'c_gain%d' % gi, gains[:, gi, :], vf[n].rearrange("(c p) -> p c", p=128),
              allow_slow_non_contiguous=True)
    for k_ in range(4):
        P.dma('sp', 'c_wcv', wcv[:, k_, :], wconv[k_, :].rearrange("(c p) -> p c", p=128),
              allow_slow_non_contiguous=True)
    P.dma('sp', 'c_bgt', bgt, vf["b_gates0"].partition_broadcast(128))
    P.memset('pool', epsb, EPS)
    P.memset('pool', oneb, 1.0)
    P.memset('pool', convc, 0.0)
    P.memset('pool', Cst, 0.0)
    P.memset('pool', Cb, 0.0)
    P.memset('pool', Rst, 0.0)
    P.memset('pool', Rb, 0.0)

    cast_i = [0]

    def cast_weights(names):
        for n in names:
            for si, (kc0, nk, c0, ncol) in enumerate(SL[n]):
                dst = wbs[n][si][:, 0:nk * ncol].rearrange("p (c n) -> p c n", c=nk)
                src = wf[n][kc0 * 128:(kc0 + nk) * 128, c0:c0 + ncol].rearrange("(c p) n -> p c n", p=128)
                P.dma('pool', 'cast%d' % (cast_i[0] % 16), dst, src, rk=[], wk=["%s#%d" % (n, si)])
                cast_i[0] += 1

    P.dma('pool', 'cast15', wgb.rearrange("p (c n) -> p c n", c=16),
          wf["w_in0"][:, 8192:8200].rearrange("(c p) n -> p c n", p=128), rk=[], wk=["wgate#"],
          allow_slow_non_contiguous=True)
    cast_weights(["w_in0"])

    slab_i = [0]

    xbuf = [view_at(big0 + 48 * K, [128, 16, 512], BF16), view_at(big0 + 64 * K, [128, 16, 512], BF16)]
    slab_pool = [wbuf]

    def load_slab(n, kc0, nk, c0, ncol):
        pool_ = slab_pool[0]
        i = slab_i[0] % len(pool_)
        slab_i[0] += 1
        buf = pool_[i]
        si = SLI[n][(kc0, nk, c0, ncol)]
        src = wbs[n][si][:, 0:nk * ncol].rearrange("p (c n) -> p c n", c=nk)
        P.dma('sp', 'wbuf%d' % i, buf[:, 0:nk, 0:ncol], src, rk=["%s#%d" % (n, si)])
        return buf

    tcnt = [0]

    def rms_to_T(gi):
        P.memset('dve', ss4, 0.0)
        for j in range(4):
            P.act(xs[j % 2], xres[j], AF.Square, accum_out=ss4[:, j:j + 1])
        P.act(rstd4, ss4, AF.Sqrt, scale=1.0 / D, bias=epsb[:, 0:1])
        P.recip(rstd4, rstd4)
        for j in range(4):
            xj = xs[j % 2]
            P.act(xj, xres[j], AF.Copy, scale=rstd4[:, j:j + 1])
            for cg in range(4):
                bank = 6 + (tcnt[0] % 2)
                tcnt[0] += 1
                pv = ps16(bank)
                for ci in range(4):
                    c = cg * 4 + ci
                    P.tr(pv[:, ci * 128:(ci + 1) * 128], xj[:, c * 128:(c + 1) * 128], ident)
                gb = gains[:, gi, cg * 4:(cg + 1) * 4].unsqueeze(2).to_broadcast([128, 4, 128])
                P.tt('dve', actT[:, cg * 4:(cg + 1) * 4, j * 128:(j + 1) * 128],
                     pv[:, 0:512].rearrange("p (a b) -> p a b", a=4), gb, ALU.mult)

    def tok_major(slab, bank0, epi):
        for j in range(4):
            pb = psb[bank0 + j]
            for kc in range(16):
                P.mm(pb[:, 0:512], actT[:, kc, j * 128:(j + 1) * 128], slab[:, kc, :],
                     start=(kc == 0), stop=(kc == 15))
            epi(j, pb)

    def feat_major(slab, fcl, bank):
        pb = psb[bank]
        for kc in range(16):
            P.mm(pb[:, 0:512], slab[:, kc, fcl * 128:(fcl + 1) * 128], actT[:, kc, :],
                 start=(kc == 0), stop=(kc == 15))
        return pb

    POOL4 = [wbuf[0], wbuf[1], xbuf[0], xbuf[1]]

    def ffn(layer, mid_hook=None):
        ffn_body(layer, mid_hook)

    def ffn_body(layer, mid_hook):
        gu = "w_gu%d" % layer
        dn = "w_down%d" % layer
        bi = 0
        for sgi in range(11):
            gslab = load_slab(gu, 0, 16, sgi * 512, 512)
            uslab = load_slab(gu, 0, 16, FF + sgi * 512, 512)
            for hcl in range(4):
                hc = sgi * 4 + hcl
                pa = feat_major(gslab, hcl, (bi % 2) * 2)
                pu = feat_major(uslab, hcl, (bi % 2) * 2 + 1)
                s_ = sg[bi % 2]
                bi += 1
                P.act(s_, pa[:, 0:512], AF.Silu)
                P.tt('dve', hT[:, hc, :], pu[:, 0:512], s_, ALU.mult)
        if mid_hook is not None:
            mid_hook()
        for g in range(4):
            for si, (kc0, nk) in enumerate(((0, 16), (16, 16), (32, 12))):
                slab = load_slab(dn, kc0, nk, g * 512, 512)
                for j in range(4):
                    pb = psb[4 + j]
                    for kc in range(nk):
                        P.mm(pb[:, 0:512], hT[:, kc0 + kc, j * 128:(j + 1) * 128], slab[:, kc, :],
                             start=(si == 0 and kc == 0), stop=(si == 2 and kc == nk - 1))
            for j in range(4):
                P.tt('dve', xres[j][:, g * 512:(g + 1) * 512], psb[4 + j][:, 0:512],
                     xres[j][:, g * 512:(g + 1) * 512], ALU.add)

    def out_proj(wname):
        for g in range(4):
            slab = load_slab(wname, 0, 16, g * 512, 512)

            def epi(j, pb, g=g):
                P.tt('dve', xres[j][:, g * 512:(g + 1) * 512], pb[:, 0:512],
                     xres[j][:, g * 512:(g + 1) * 512], ALU.add)
            tok_major(slab, 0, epi)

    for tb in range(NB):
        t0 = tb * 512
        for j in range(4):
            P.dma('sp', 'xres%d' % j, xres[j], x[t0 + j * 128:t0 + (j + 1) * 128, :])
        P.dma('sp', 'cs_t', cs_t[:, 0, :], cosT[:, t0:t0 + 512])
        P.dma('sp', 'cs_t', cs_t[:, 1, :], sinT[:, t0:t0 + 512])
        rms_to_T(0)
        P.dma('sp', 'wgate', wgate, wgb.rearrange("p (c n) -> p c n", c=16), rk=["wgate#"])
        pg = psb[4]
        for j in range(4):
            for kc in range(16):
                P.mm(pg[:, j * 8:(j + 1) * 8], actT[:, kc, j * 128:(j + 1) * 128], wgate[:, kc, :],
                     start=(kc == 0), stop=(kc == 15))
        for j in range(4):
            P.tt('dve', gts[:, j, :], pg[:, j * 8:(j + 1) * 8], bgt, ALU.add)
        gsp3 = gsp.rearrange("p (j h) -> p j h", j=4)
        P.act(gsp3, gts[:, :, 4:8], AF.Exp, scale=-1.0)
        P.act(gsp, gsp, AF.Ln, bias=oneb[:, 0:1])
        pcs = psb[5]
        P.mm(pcs[:, 0:16], maskle, gsp, start=True, stop=True)
        P.mm(pcs[:, 16:32], onesf, gsp, start=True, stop=True)
        P.copy('dve', gcs, pcs[:, 0:16])
        P.tt('dve', gws.rearrange("p (j h) -> p j h", j=4), gts[:, :, 0:4],
             gcs.rearrange("p (j h) -> p j h", j=4), ALU.add)
        P.act(gws, gws, AF.Exp)
        P.act(gosc, gcs, AF.Exp, scale=-1.0)
        P.ts('dve', gosc, gosc, 1.0 / 16.0, None, op0=ALU.mult)
        P.act(geg, pcs[:, 16:32], AF.Exp, scale=-1.0)

        if stop_after == 'gates':
            pr = [gts.rearrange("p a b -> p (a b)"), gsp, gcs, gws, gosc, geg, ss4, rstd4]
            for i_, a_ in enumerate(pr):
                w_ = a_.shape[1]
                P.copy('dve', xres[3][:, 0:w_], a_)
                P.dma('sp', 'yst3', y[i_ * 128:(i_ + 1) * 128, 0:w_], xres[3][:, 0:w_])
            P.copy('dve', xres[2][:, 0:512], actT[:, 0, :])
            P.dma('sp', 'yst2', y[7 * 128:8 * 128, 512:1024], xres[2][:, 0:512])
            st = P.finalize(final_lanes=['yst3', 'yst2'])
            return nc, st, sbuf_used
        for g in range(4):
            slab = load_slab("w_in0", 0, 16, g * 512, 512)
            for fcl in range(4):
                fc = g * 4 + fcl
                pb = feat_major(slab, fcl, fc % 4)
                ubuf = ubufs[fc % 2]
                cacc = caccs[fc % 2]
                P.copy('act', ubuf[:, 3:515], pb[:, 0:512])
                P.act(cacc, pb[:, 0:512], AF.Copy, scale=wcv[:, 3, fc:fc + 1])
                P.copy('pool', ubuf[:, 0:3], convc[:, fc, :])
                for k in (2, 1, 0):
                    P.stt('dve', cacc, ubuf[:, k:k + 512], wcv[:, k, fc:fc + 1], cacc,
                          ALU.mult, ALU.add)
                P.copy('pool', convc[:, fc, :], ubuf[:, 512:515])
                P.act(qkT_ml[:, fc, :], cacc, AF.Silu)
        for g in (4, 5):
            slab = load_slab("w_in0", 0, 16, g * 512, 512)

            def epi(j, pb, g=g):
                for hl in range(2):
                    h = (g - 4) * 2 + hl
                    P.ts('dve', vml[:, j, h, 0:256], pb[:, hl * 256:(hl + 1) * 256],
                         gws[:, j * 4 + h:j * 4 + h + 1], None, op0=ALU.mult)
                    P.copy('dve', vml[:, j, h, 256:257], gws[:, j * 4 + h:j * 4 + h + 1])
            tok_major(slab, 0, epi)
        for g in (6, 7):
            slab = load_slab("w_in0", 0, 16, g * 512, 512)

            def epi(j, pb, g=g):
                P.act(og[:, j, (g - 6) * 512:(g - 5) * 512], pb[:, 0:512], AF.Sigmoid)
            tok_major(slab, 0, epi)
        for g in (8, 9, 10, 11):
            slab = load_slab("w_in0", 0, 16, g * 512, 512)
            for hl in range(2):
                fc = (g - 8) * 4 + hl * 2
                p1 = feat_major(slab, hl * 2, 0 + hl * 2)
                p2 = feat_major(slab, hl * 2 + 1, 1 + hl * 2)
                ta = rta
                tb_ = rtb
                P.tt('dve', ta, p1[:, 0:512], cs_t[:, 0, :], ALU.mult)
                P.tt('dve', tb_, p2[:, 0:512], cs_t[:, 1, :], ALU.mult)
                P.tt('pool', qkT_r[:, fc, :], ta, tb_, ALU.subtract)
                P.tt('dve', ta, p1[:, 0:512], cs_t[:, 1, :], ALU.mult)
                P.tt('dve', tb_, p2[:, 0:512], cs_t[:, 0, :], ALU.mult)
                P.tt('pool', qkT_r[:, fc + 1, :], ta, tb_, ALU.add)
        for g in (12, 13):
            slab = load_slab("w_in0", 0, 16, g * 512, 512)

            def epi(j, pb, g=g):
                for hl in range(2):
                    h = (g - 12) * 2 + hl
                    P.ts('dve', rv[:, j, h, :], pb[:, hl * 256:(hl + 1) * 256],
                         retc[:, h:h + 1], None, op0=ALU.mult)
            tok_major(slab, 0, epi)
        for g in (14, 15):
            slab = load_slab("w_in0", 0, 16, g * 512, 512)

            def epi(j, pb, g=g):
                P.act(gs[:, j, (g - 14) * 512:(g - 13) * 512], pb[:, 0:512], AF.Silu)
            tok_major(slab, 0, epi)
        P.dma('sp', 'gml_b', gml_b, vf["g_ml0"].partition_broadcast(128))
        P.dma('sp', 'gret_b', gret_b, vf["g_ret0"].partition_broadcast(128))
        for j in range(4):
            P.tt('pool', og[:, j, :], og[:, j, :], gml_b, ALU.mult)
            P.tt('pool', gs[:, j, :], gs[:, j, :], gret_b, ALU.mult)

        sm = small
        for j in range(4):
            jc = slice(j * 128, (j + 1) * 128)
            yt = ytok[j % 2]
            def hinfo(hh):
                is_ml = hh < 4
                h = hh % 4
                qk = qkT_ml if is_ml else qkT_r
                vv = vml[:, j, h, 0:257] if is_ml else rv[:, j, h, :]
                nv = 257 if is_ml else 256
                return is_ml, h, hh % 2, qk, vv, nv, (Cst if is_ml else Rst), (Cb if is_ml else Rb)

            def stageX(hh):
                is_ml, h, par, qk, vv, nv, Sf, Sb = hinfo(hh)
                pk = ps16(6 + par)
                for dc in range(2):
                    P.tr(pk[:, dc * 128:(dc + 1) * 128], qk[:, 8 + 2 * h + dc, jc], ident)
                P.copy('act', ktm[par], pk[:, 0:256])
                pst = psb[par]
                for dc in range(2):
                    P.mm(pst[:, 0:128], qk[:, 8 + 2 * h + dc, jc], qk[:, 2 * h + dc, jc],
                         start=(dc == 0), stop=(dc == 1))
                P.tt('dve', PT[par], pst[:, 0:128], maskle, ALU.mult)
                egs = geg[:, j * 4 + h:j * 4 + h + 1] if is_ml else RET_DECAY[h]
                P.ts('pool', Sf[:, h, :, 0:nv], Sf[:, h, :, 0:nv], egs, None, op0=ALU.mult)

            def stageY(hh):
                is_ml, h, par, qk, vv, nv, Sf, Sb = hinfo(hh)
                egs = geg[:, j * 4 + h:j * 4 + h + 1] if is_ml else RET_DECAY[h]
                po = psb[2 + par]
                P.mm(po[:, 0:nv], PT[par], vv, start=True, stop=False)
                for dc in range(2):
                    P.mm(po[:, 0:nv], qk[:, 2 * h + dc, jc], Sb[:, h, dc, 0:nv],
                         start=False, stop=(dc == 1))
                for dc in range(2):
                    pd = psb[4 + dc]
                    P.mm(pd[:, 0:nv], ktm[par][:, dc * 128:(dc + 1) * 128], vv, start=True, stop=True)
                    P.stt('dve', Sf[:, h, dc, 0:nv], pd[:, 0:nv], egs, Sf[:, h, dc, 0:nv],
                          ALU.mult, ALU.add)
                    P.copy('act', Sb[:, h, dc, 0:nv], Sf[:, h, dc, 0:nv])
                b0 = hh * 8
                if is_ml:
                    osc = gosc[:, j * 4 + h:j * 4 + h + 1]
                    P.ts('dve', sm[:, b0:b0 + 1], po[:, 256:257], osc, None, op0=ALU.mult)
                    P.ts('dve', sm[:, b0 + 6:b0 + 7], sm[:, b0:b0 + 1], -1.0, None, op0=ALU.mult)
                    P.tt('dve', sm[:, b0:b0 + 1], sm[:, b0:b0 + 1], sm[:, b0 + 6:b0 + 7], ALU.max)
                    P.ts('dve', sm[:, b0:b0 + 1], sm[:, b0:b0 + 1], 1.0, None, op0=ALU.max)
                    P.recip(sm[:, b0:b0 + 1], sm[:, b0:b0 + 1])
                    P.tt('dve', sm[:, b0 + 1:b0 + 2], sm[:, b0:b0 + 1], osc, ALU.mult)
                    P.memset('dve', sm[:, b0 + 2:b0 + 3], 0.0)
                    P.act(ctile[par], po[:, 0:256], AF.Square, scale=sm[:, b0 + 1:b0 + 2],
                          accum_out=sm[:, b0 + 2:b0 + 3])
                    P.act(sm[:, b0 + 3:b0 + 4], sm[:, b0 + 2:b0 + 3], AF.Sqrt, scale=1.0 / 256.0,
                          bias=epsb[:, 0:1])
                    P.recip(sm[:, b0 + 3:b0 + 4], sm[:, b0 + 3:b0 + 4])
                    P.tt('dve', sm[:, b0 + 4:b0 + 5], sm[:, b0 + 3:b0 + 4], sm[:, b0 + 1:b0 + 2],
                         ALU.mult)
                    P.stt('dve', yt[:, h * 256:(h + 1) * 256], po[:, 0:256], sm[:, b0 + 4:b0 + 5],
                          og[:, j, h * 256:(h + 1) * 256], ALU.mult, ALU.mult)
                else:
                    osc = retc[:, 4 + h:5 + h]
                    P.memset('dve', sm[:, b0:b0 + 2], 0.0)
                    P.act(ctile[par], po[:, 0:256], AF.Copy, accum_out=sm[:, b0:b0 + 1])
                    P.ts('dve', sm[:, b0 + 2:b0 + 3], sm[:, b0:b0 + 1], -1.0 / 256.0, None,
                         op0=ALU.mult)
                    P.act(ctile[par], po[:, 0:256], AF.Square, bias=sm[:, b0 + 2:b0 + 3],
                          accum_out=sm[:, b0 + 1:b0 + 2])
                    P.tt('dve', sm[:, b0 + 3:b0 + 4], sm[:, b0 + 1:b0 + 2], osc, ALU.mult)
                    P.tt('dve', sm[:, b0 + 3:b0 + 4], sm[:, b0 + 3:b0 + 4], osc, ALU.mult)
                    P.act(sm[:, b0 + 3:b0 + 4], sm[:, b0 + 3:b0 + 4], AF.Sqrt, scale=1.0 / 256.0,
                          bias=epsb[:, 0:1])
                    P.recip(sm[:, b0 + 3:b0 + 4], sm[:, b0 + 3:b0 + 4])
                    P.tt('dve', sm[:, b0 + 4:b0 + 5], sm[:, b0 + 3:b0 + 4], osc, ALU.mult)
                    P.tt('dve', sm[:, b0 + 5:b0 + 6], sm[:, b0 + 4:b0 + 5], sm[:, b0 + 2:b0 + 3],
                         ALU.mult)
                    P.act(ctile[par], po[:, 0:256], AF.Identity, scale=sm[:, b0 + 4:b0 + 5],
                          bias=sm[:, b0 + 5:b0 + 6])
                    P.tt('dve', yt[:, GW + h * 256:GW + (h + 1) * 256], ctile[par],
                         gs[:, j, h * 256:(h + 1) * 256], ALU.mult)

            stageX(0)
            for hh in range(8):
                if hh + 1 < 8:
                    stageX(hh + 1)
                stageY(hh)
            for cg in range(4):
                pv = ps16(7)
                for ci in range(4):
                    c = cg * 4 + ci
                    P.tr(pv[:, ci * 128:(ci + 1) * 128], yt[:, c * 128:(c + 1) * 128], ident)
                P.copy('act', actT[:, cg * 4:(cg + 1) * 4, jc],
                       pv[:, 0:512].rearrange("p (a b) -> p a b", a=4))
        if stop_after == 'ymix':
            for kc in range(4):
                P.copy('dve', xres[kc].rearrange("p (a b) -> p a b", a=4), actT[:, kc * 4:(kc + 1) * 4, :])
                P.dma('sp', 'yst%d' % kc, y[t0 + kc * 128:t0 + (kc + 1) * 128, :], xres[kc])
            continue
        if tb == 0:
            cast_weights(["w_out0", "w_gu0", "w_down0"])
        slab_pool[0] = POOL4
        out_proj("w_out0")
        if stop_after == 'x1':
            for j in range(4):
                P.dma('sp', 'yst%d' % j, y[t0 + j * 128:t0 + (j + 1) * 128, :], xres[j])
            continue
        rms_to_T(1)
        ffn(0)
        slab_pool[0] = wbuf
        if tb == 0:
            cast_weights(["w_qkv1", "w_out1", "w_gu1", "w_down1"])
        for j in range(4):
            P.dma('sp', 'x2st%d' % j, x2[t0 + j * 128:t0 + (j + 1) * 128, :], xres[j],
                  wk=["x2#%d" % tb])

    if stop_after in ('x1', 'ymix'):
        st = P.finalize(final_lanes=['yst%d' % j for j in range(4)])
        return nc, st, sbuf_used
    if stop_after == 'l0':
        for tb in range(NB):
            for j in range(4):
                t0 = tb * 512 + j * 128
                P.dma('sp', 'xres%d' % j, xres[j], x2[t0:t0 + 128, :], rk=["x2#%d" % tb])
                P.dma('sp', 'yst%d' % j, y[t0:t0 + 128, :], xres[j])
        st = P.finalize(final_lanes=['yst%d' % j for j in range(4)])
        return nc, st, sbuf_used

    slab_pool[0] = POOL4
    stg = view_at(big0, [128, 4, 512], BF16)
    stv = view_at(big0 + 4 * K, [128, 4, 512], BF16)
    for tb in range(NB):
        t0 = tb * 512
        for j in range(4):
            P.dma('sp', 'xres%d' % j, xres[j], x2[t0 + j * 128:t0 + (j + 1) * 128, :],
                  rk=["x2#%d" % tb])
        rms_to_T(2)
        for g in range(8):
            slab = load_slab("w_qkv1", 0, 16, g * 512, 512)
            for fcl in range(4):
                pb = feat_major(slab, fcl, fcl)
                if g < 4:
                    P.act(stg[:, fcl, :], pb[:, 0:512], AF.Copy, scale=128.0 ** -0.5)
                else:
                    P.copy('dve', stg[:, fcl, :], pb[:, 0:512])
            dst = (qT1 if g < 4 else kT1)[(g % 4) * 4:(g % 4) * 4 + 4, :, t0:t0 + 512]
            P.dma('sp', 'stg', dst.rearrange("h p t -> p h t"), stg,
                  wk=[("qT1#%d" if g < 4 else "kT1#%d") % tb])
        for g in range(8, 12):
            slab = load_slab("w_qkv1", 0, 16, g * 512, 512)

            def epi(j, pb, g=g):
                P.copy('act' if j % 2 else 'dve', stv[:, j, :], pb[:, 0:512])
            tok_major(slab, 0, epi)
            c0 = (g - 8) * 512
            P.dma('sp', 'stv', v1[t0:t0 + 512, c0:c0 + 512].rearrange("(j p) c -> p j c", p=128),
                  stv, wk=["v1#%d" % tb])

    amask = view_at(big0, [128, 4, 512], BF16)
    kTh = [view_at(big0 + 4 * K + i * 24 * K, [128, S], BF16) for i in range(2)]
    qTh = [view_at(big0 + 4 * K + i * 24 * K + 8 * K, [128, S], BF16) for i in range(2)]
    vh = [view_at(big0 + 4 * K + i * 24 * K + 16 * K, [128, NKB, 128], BF16) for i in range(2)]
    assert 4 * K + 48 * K <= BIGSZ and S * 2 <= 8 * K
    def wtiles(sidx):
        f = wbuf[sidx].bitcast(F32).rearrange("p a b -> p (a b)")
        b = wbuf[sidx].rearrange("p a b -> p (a b)")
        return dict(e=[f[:, 0:512], f[:, 512:1024]], ecs=[f[:, 1024:1536], f[:, 1536:2048]],
                    sp=[b[:, 4096:4608], b[:, 4608:5120]], a=[b[:, 5120:5632], b[:, 5632:6144]],
                    o=[b[:, 6144:6656], b[:, 6656:7168]])
    WT = [wtiles(0), wtiles(1)]
    P.dma('pool', 'c_amask', amask, amask_d.rearrange("p (a b) -> p a b", a=4))
    allq = ["qT1#%d" % tb for tb in range(NB)]
    allk = ["kT1#%d" % tb for tb in range(NB)]
    allv = ["v1#%d" % tb for tb in range(NB)]
    ocnt = [0, 0]
    for hpair in range(8):
        for sidx in range(2):
            h = hpair * 2 + sidx
            P.dma('sp', 'kTh%d' % sidx, kTh[sidx], kT1[h], rk=allk)
            P.dma('sp', 'qTh%d' % sidx, qTh[sidx], qT1[h], rk=allq)
            P.dma('sp', 'vh%d' % sidx, vh[sidx],
                  v1[:, h * 128:(h + 1) * 128].rearrange("(b p) e -> p b e", p=128), rk=allv)
        for G in range(NB):
            qs = slice(G * 512, (G + 1) * 512)
            kbs = list(range(4 * G + 3, -1, -1))

            L = len(kbs)

            def zmm(sidx, n):
                pz = psb[2 * sidx + n % 2]
                kb = kbs[n]
                P.mm(pz[:, 0:512], kTh[sidx][:, kb * 128:(kb + 1) * 128], qTh[sidx][:, qs],
                     start=True, stop=True)

            def expln(sidx, n):
                w = WT[sidx]
                kb = kbs[n]
                pz = psb[2 * sidx + n % 2]
                e = w['e'][n % 2]
                sp_ = w['sp'][n % 2]
                P.act(e, pz[:, 0:512], AF.Exp)
                P.act(sp_, e, AF.Ln, bias=oneb[:, 0:1])
                if kb >= 4 * G:
                    i = kb - 4 * G
                    P.tt('pool', sp_, sp_, amask[:, i, :], ALU.mult)
                    P.tt('pool', e, e, amask[:, i, :], ALU.mult)

            for n in range(-2, L + 1):
                for sidx in range(2):
                    if 0 <= n + 2 < L:
                        zmm(sidx, n + 2)
                for sidx in range(2):
                    if 0 <= n + 1 < L:
                        expln(sidx, n + 1)
                for sidx in range(2):
                    if 0 <= n - 1 < L:
                        w = WT[sidx]
                        m = n - 1
                        P.mm(psb[6 + sidx][:, 0:512], vh[sidx][:, kbs[m], :], w['a'][m % 2],
                             start=(m == 0), stop=(m == L - 1))
                for sidx in range(2):
                    if 0 <= n < L:
                        w = WT[sidx]
                        P.act(w['ecs'][n % 2], psb[4 + sidx][:, 0:512], AF.Exp, scale=-1.0)
                for sidx in range(2):
                    w = WT[sidx]
                    if 0 <= n < L - 1:
                        P.mm(psb[4 + sidx][:, 0:512], lrest, w['sp'][n % 2], start=False, stop=False)
                    if 0 <= n + 1 < L:
                        P.mm(psb[4 + sidx][:, 0:512], linc, w['sp'][(n + 1) % 2],
                             start=(n + 1 == 0), stop=False)
                for sidx in range(2):
                    if 0 <= n < L:
                        w = WT[sidx]
                        P.tt('dve', w['a'][n % 2], w['e'][n % 2], w['ecs'][n % 2], ALU.mult)
            for sidx in range(2):
                h = hpair * 2 + sidx
                ot = WT[sidx]['o'][ocnt[sidx] % 2]
                ocnt[sidx] += 1
                P.copy('dve', ot, psb[6 + sidx][:, 0:512])
                P.dma('sp', 'ost%d' % sidx, oT1[h, :, qs], ot, wk=["oT1#%d" % G])

    fnb = view_at(cst_off, [128, D], F32)
    P.dma('sp', 'fnb', fnb, vf["final_norm"].partition_broadcast(128))
    for tb in range(NB):
        t0 = tb * 512
        for j in range(4):
            P.dma('sp', 'xres%d' % j, xres[j], x2[t0 + j * 128:t0 + (j + 1) * 128, :],
                  rk=["x2#%d" % tb])
        def load_actT(tbn):
            P.dma('sp', 'actT', actT, oT1[:, :, tbn * 512:tbn * 512 + 512].rearrange("h p t -> p h t"),
                  rk=["oT1#%d" % tbn])
        if tb == 0:
            load_actT(0)
        out_proj("w_out1")
        rms_to_T(3)
        ffn(1, mid_hook=(lambda tbn=tb + 1: load_actT(tbn)) if tb + 1 < NB else None)
        P.memset('dve', ss4, 0.0)
        for j in range(4):
            P.act(xs[j % 2], xres[j], AF.Square, accum_out=ss4[:, j:j + 1])
        P.act(rstd4, ss4, AF.Sqrt, scale=1.0 / D, bias=epsb[:, 0:1])
        P.recip(rstd4, rstd4)
        for j in range(4):
            P.stt('dve', xres[j], xres[j], rstd4[:, j:j + 1], fnb,
                  ALU.mult, ALU.mult)
            P.dma('sp', 'yst%d' % j, y[t0 + j * 128:t0 + (j + 1) * 128, :], xres[j])
    st = P.finalize(final_lanes=['yst%d' % j for j in range(4)])
    return nc, st, sbuf_used


_CACHE = {}


def kernel(**inputs):
    x = np.asarray(inputs["x"], dtype=np.float32)
    B, S, _ = x.shape
    if S not in _CACHE:
        _CACHE[S] = (build(S)[0], host_consts(S))
    nc, consts = _CACHE[S]
    base = {}
    for n, _, _ in WSPEC:
        base[n] = np.ascontiguousarray(np.asarray(inputs[n], dtype=np.float32))
    for n, _ in VSPEC:
        base[n] = np.ascontiguousarray(np.asarray(inputs[n], dtype=np.float32))
    base["w_conv0"] = np.ascontiguousarray(np.asarray(inputs["w_conv0"], dtype=np.float32))
    base.update(consts)
    in_maps = []
    for b in range(B):
        m = dict(base)
        m["x"] = np.ascontiguousarray(x[b])
        in_maps.append(m)
    res = run_bass_kernel_spmd(nc, in_maps, core_ids=list(range(B)))
    return np.stack([np.asarray(r["y"], dtype=np.float32) for r in res.results], axis=0)
```

```python
import numpy as np
import concourse.bass as bass
import concourse.mybir as mybir
from concourse.bass_utils import run_bass_kernel_spmd

F32 = mybir.dt.float32
BF16 = mybir.dt.bfloat16
AF = mybir.ActivationFunctionType
ALU = mybir.AluOpType

D = 2048
GW = 1024
FF = 5632
NIN = 8200
EPS = 1e-6
EPOCH = 16000
COMPUTE = ('pe', 'act', 'dve', 'pool')
PG = 1024


def _keys(a):
    if isinstance(a, str):
        return (a,)
    if isinstance(a, tuple):
        return a
    sp = a.space.name
    name = a.tensor.name
    if sp == 'DRAM':
        return (name,)
    ap = a.ap
    stride = ap[0][0]
    off = a.offset % stride if stride > 0 else a.offset
    ext = 0
    for st, cnt in ap[1:]:
        ext += (cnt - 1) * abs(st)
    sz = mybir.dt.size(a.dtype)
    lo = off * sz
    hi = (off + ext) * sz + sz - 1
    pg = 2048 if sp == 'PSUM' else PG
    return tuple((name, p) for p in range(lo // pg, hi // pg + 1))


class Instr:
    __slots__ = ('idx', 'eng', 'fn', 'deps', 'lane', 'lane_idx', 'needs_inc', 'seq',
                 'waits', 'is_dma', 'vc')


class Prog:
    def __init__(self, nc):
        self.nc = nc
        self.instrs = []
        self.last_w = {}
        self.readers = {}
        self.lane_last = {}
        self.lane_cnt = {}
        self.lane_sem = {}

    def add(self, eng, fn, reads, writes, lane=None):
        ins = Instr()
        ins.idx = len(self.instrs)
        ins.eng = eng
        ins.fn = fn
        ins.lane = lane
        ins.is_dma = lane is not None
        ins.needs_inc = False
        ins.seq = None
        ins.waits = None
        ins.vc = None
        rk = []
        for r in reads:
            rk.extend(_keys(r))
        wk = []
        for w in writes:
            wk.extend(_keys(w))
        rk = list(dict.fromkeys(rk))
        wk = list(dict.fromkeys(wk))
        deps = set()
        for k in rk:
            if k in self.last_w:
                deps.add(self.last_w[k])
        for k in wk:
            if k in self.last_w:
                deps.add(self.last_w[k])
            for r in self.readers.get(k, ()):
                deps.add(r)
        if lane is not None:
            if lane in self.lane_last:
                deps.add(self.lane_last[lane])
            self.lane_last[lane] = ins.idx
            self.lane_cnt[lane] = self.lane_cnt.get(lane, 0) + 1
            ins.lane_idx = self.lane_cnt[lane]
        else:
            ins.lane_idx = None
        deps.discard(ins.idx)
        if eng == 'pe' and not ins.is_dma:
            deps = {d for d in deps
                    if not (self.instrs[d].eng == 'pe' and not self.instrs[d].is_dma)}
        ins.deps = deps
        for k in wk:
            self.last_w[k] = ins.idx
            self.readers[k] = []
        wks = set(wk)
        for k in rk:
            if k in wks:
                continue
            lst = self.readers.setdefault(k, [])
            if not ins.is_dma:
                lst[:] = [r for r in lst
                          if self.instrs[r].is_dma or self.instrs[r].eng != eng]
            lst.append(ins.idx)
        self.instrs.append(ins)
        return ins

    def dma(self, q, lane, out, in_, rk=None, wk=None, **kw):
        r = rk if rk is not None else [in_]
        w = wk if wk is not None else [out]
        return self.add(q, lambda e: e.dma_start(out=out, in_=in_, **kw), r, w, lane=lane)

    def mm(self, out, lhsT, rhs, start=True, stop=True):
        return self.add('pe', lambda e: e.matmul(out, lhsT, rhs, start=start, stop=stop),
                        [lhsT, rhs], [out])

    def tr(self, out, in_, ident):
        return self.add('pe', lambda e: e.transpose(out, in_, ident), [in_, ident], [out])

    def act(self, out, in_, func, bias=None, scale=None, accum_out=None):
        kw = {}
        r = [in_]
        w = [out]
        if bias is not None:
            kw['bias'] = bias
            if not isinstance(bias, (int, float)):
                r.append(bias)
        if scale is not None:
            kw['scale'] = scale
            if not isinstance(scale, (int, float)):
                r.append(scale)
        if accum_out is not None:
            kw['accum_out'] = accum_out
            w.append(accum_out)
        return self.add('act', lambda e: e.activation(out=out, in_=in_, func=func, **kw), r, w)

    def tt(self, eng, out, in0, in1, op):
        return self.add(eng, lambda e: e.tensor_tensor(out=out, in0=in0, in1=in1, op=op),
                        [in0, in1], [out])

    def ts(self, eng, out, in0, s1, s2=None, op0=ALU.mult, op1=None):
        r = [in0]
        for s in (s1, s2):
            if s is not None and not isinstance(s, (int, float)):
                r.append(s)
        kw = {}
        if op1 is not None:
            kw['op1'] = op1
        return self.add(eng, lambda e: e.tensor_scalar(out=out, in0=in0, scalar1=s1, scalar2=s2,
                                                       op0=op0, **kw), r, [out])

    def stt(self, eng, out, in0, scalar, in1, op0, op1):
        r = [in0, in1]
        if not isinstance(scalar, (int, float)):
            r.append(scalar)
        return self.add(eng, lambda e: e.scalar_tensor_tensor(
            out=out, in0=in0, scalar=scalar, in1=in1, op0=op0, op1=op1), r, [out])

    def copy(self, eng, out, in_):
        if eng == 'act':
            return self.add(eng, lambda e: e.copy(out=out, in_=in_), [in_], [out])
        return self.add(eng, lambda e: e.tensor_copy(out=out, in_=in_), [in_], [out])

    def memset(self, eng, out, val):
        return self.add(eng, lambda e: e.memset(out, val), [], [out])

    def recip(self, out, in_):
        return self.add('dve', lambda e: e.reciprocal(out=out, in_=in_), [in_], [out])

    def finalize(self, final_lanes=()):
        nc = self.nc
        instrs = self.instrs
        for ins in instrs:
            for d in ins.deps:
                di = instrs[d]
                if not di.is_dma:
                    di.needs_inc = True
        cnt = {e: 0 for e in COMPUTE}
        for ins in instrs:
            if not ins.is_dma and ins.needs_inc:
                ins.seq = cnt[ins.eng]
                cnt[ins.eng] += 1
        sems = {}
        for e in COMPUTE:
            nep = max(1, (cnt[e] + EPOCH - 1) // EPOCH)
            for k in range(nep):
                sems[(e, k)] = nc.alloc_semaphore("s_%s_%d" % (e, k))
        for ln in self.lane_cnt:
            self.lane_sem[ln] = nc.alloc_semaphore("l_%s" % ln)

        def src_of(di):
            if di.is_dma:
                return ('L', di.lane), di.lane_idx * 16
            return (di.eng, di.seq // EPOCH), di.seq % EPOCH + 1

        known = {e: {} for e in ('pe', 'act', 'dve', 'pool', 'sp')}
        nwaits = 0
        for ins in instrs:
            kn = known[ins.eng]
            waits = {}
            for d in ins.deps:
                s, v = src_of(instrs[d])
                if kn.get(s, 0) >= v:
                    continue
                if waits.get(s, 0) < v:
                    waits[s] = v
            for d in ins.deps:
                di = instrs[d]
                s, v = src_of(di)
                if s in waits and di.vc is not None:
                    for s2, v2 in di.vc.items():
                        if kn.get(s2, 0) < v2:
                            kn[s2] = v2
            for s, v in waits.items():
                if kn.get(s, 0) < v:
                    kn[s] = v
            ins.waits = waits
            nwaits += len(waits)
            if ins.is_dma:
                vc = dict(kn)
                vc[('L', ins.lane)] = ins.lane_idx * 16
                ins.vc = vc
            elif ins.needs_inc:
                s, v = src_of(ins)
                vc = dict(kn)
                vc[s] = v
                ins.vc = vc
        self.stats = dict(n=len(instrs), nwaits=nwaits, incs=dict(cnt),
                          nsem=len(sems) + len(self.lane_sem))
        per_eng = {e: [] for e in ('pe', 'act', 'dve', 'pool', 'sp')}
        for ins in instrs:
            per_eng[ins.eng].append(ins)

        def sem_of(s):
            if s[0] == 'L':
                return self.lane_sem[s[1]]
            return sems[s]

        def emit(eng_obj, lst, tail):
            for ins in lst:
                for s, v in ins.waits.items():
                    eng_obj.wait_ge(sem_of(s), v)
                bi = ins.fn(eng_obj)
                if ins.is_dma:
                    bi.then_inc(self.lane_sem[ins.lane], 16)
                elif ins.needs_inc:
                    bi.then_inc(sems[(ins.eng, ins.seq // EPOCH)], 1)
            for ln in tail:
                eng_obj.wait_ge(self.lane_sem[ln], self.lane_cnt[ln] * 16)

        with nc.Block() as block:
            @block.tensor
            def _(e):
                emit(e, per_eng['pe'], ())

            @block.scalar
            def _(e):
                emit(e, per_eng['act'], ())

            @block.vector
            def _(e):
                emit(e, per_eng['dve'], ())

            @block.gpsimd
            def _(e):
                emit(e, per_eng['pool'], ())

            @block.sync
            def _(e):
                emit(e, per_eng['sp'], final_lanes)
        return self.stats


def host_consts(S):
    c = np.zeros((128, 648), np.float32)
    i = np.arange(128)
    c[:, 0:128] = np.eye(128)
    c[:, 128:256] = (i[:, None] <= i[None, :])
    c[:, 256:384] = 1.0
    c[:, 384:512] = (i[:, None] >= i[None, :])
    c[:, 512:640] = (i[:, None] < i[None, :])
    lg = np.log(1.0 - 2.0 ** (-5.0 - 2.0 * np.arange(4, dtype=np.float64)))
    pos = np.arange(128, dtype=np.float64)
    c[:, 640:644] = np.exp(-(pos[:, None] + 1.0) * lg[None, :]) * (256.0 ** -0.5)
    c[:, 644:648] = np.exp((pos[:, None] + 1.0) * lg[None, :])
    am = np.zeros((128, 4, 512), np.float32)
    t = np.arange(512)
    for k in range(4):
        am[:, k, :] = ((128 * k + i)[:, None] < t[None, :])
    inv = (10000.0 ** (-np.arange(0, 256, 2, dtype=np.float32) / np.float32(256))).astype(np.float32)
    ang = inv[:, None] * np.arange(S, dtype=np.float32)[None, :]
    return dict(cst=c, amask=am.reshape(128, 2048),
                cosT=np.cos(ang).astype(np.float32), sinT=np.sin(ang).astype(np.float32))


RET_DECAY = [float(np.exp(128.0 * np.log(1.0 - 2.0 ** (-5.0 - 2.0 * h)))) for h in range(4)]

WSPEC = [("w_in0", D, NIN), ("w_out0", D, D), ("w_gu0", D, 2 * FF), ("w_down0", FF, D),
         ("w_qkv1", D, 3 * D), ("w_out1", D, D), ("w_gu1", D, 2 * FF), ("w_down1", FF, D)]
VSPEC = [("norm_mix0", D), ("b_gates0", 8), ("g_ml0", GW), ("g_ret0", GW), ("norm_ffn0", D),
         ("norm_mix1", D), ("norm_ffn1", D), ("final_norm", D)]


def build(S, stop_after=None):
    NB = S // 512
    NKB = S // 128
    nc = bass.Bass("TRN2", target_bir_lowering=False)
    P = Prog(nc)
    dt = {}

    def din(name, shape, d=F32):
        dt[name] = nc.dram_tensor(name, list(shape), d, kind="ExternalInput").ap()
        return dt[name]

    x = din("x", [S, D])
    wf = {}
    for n, k, m in WSPEC:
        wf[n] = din(n, [k, m])
    vf = {}
    for n, m in VSPEC:
        vf[n] = din(n, [m])
    wconv = din("w_conv0", [4, D])
    cst = din("cst", [128, 648])
    amask_d = din("amask", [128, 2048])
    cosT = din("cosT", [128, S])
    sinT = din("sinT", [128, S])
    y = nc.dram_tensor("y", [S, D], F32, kind="ExternalOutput").ap()
    def slab_specs(n):
        if n in ("w_in0",):
            return [(0, 16, g * 512, 512) for g in range(16)]
        if n in ("w_out0", "w_out1"):
            return [(0, 16, g * 512, 512) for g in range(4)]
        if n in ("w_qkv1",):
            return [(0, 16, g * 512, 512) for g in range(12)]
        if n in ("w_gu0", "w_gu1"):
            r = []
            for sgi in range(11):
                r.append((0, 16, sgi * 512, 512))
                r.append((0, 16, FF + sgi * 512, 512))
            return r
        r = []
        for g in range(4):
            for (kc0, nk) in ((0, 16), (16, 16), (32, 12)):
                r.append((kc0, nk, g * 512, 512))
        return r
    SL = {n: slab_specs(n) for n, _, _ in WSPEC}
    SLI = {n: {sp_: i for i, sp_ in enumerate(SL[n])} for n in SL}
    wbs = {n: nc.dram_tensor(n + "_s", [len(SL[n]), 128, 8192], BF16, kind="Internal").ap()
           for n, _, _ in WSPEC}
    wgb = nc.dram_tensor("wgate_s", [128, 128], BF16, kind="Internal").ap()
    x2 = nc.dram_tensor("x2s", [S, D], F32, kind="Internal").ap()
    qT1 = nc.dram_tensor("qT1", [16, 128, S], BF16, kind="Internal").ap()
    kT1 = nc.dram_tensor("kT1", [16, 128, S], BF16, kind="Internal").ap()
    v1 = nc.dram_tensor("v1", [S, D], BF16, kind="Internal").ap()
    oT1 = nc.dram_tensor("oT1", [16, 128, S], BF16, kind="Internal").ap()

    AR_BYTES = 206 * 1024
    ar = nc.alloc_sbuf_tensor("arena", [128, AR_BYTES // 2], BF16)
    cur = [0]

    def view_at(off, shape, d):
        n = 1
        for s_ in shape[1:]:
            n *= s_
        sz = 4 if d == F32 else 2
        a = ar[:, off // 2: off // 2 + n * sz // 2]
        if d == F32:
            a = a.bitcast(F32)
        if len(shape) == 3:
            a = a.rearrange("p (a b) -> p a b", a=shape[1])
        elif len(shape) == 4:
            a = a.rearrange("p (a b c) -> p a b c", a=shape[1], b=shape[2])
        elif len(shape) == 5:
            a = a.rearrange("p (a b c e) -> p a b c e", a=shape[1], b=shape[2], c=shape[3])
        return a

    def alloc(shape, d):
        n = 1
        for s_ in shape[1:]:
            n *= s_
        nb = n * (4 if d == F32 else 2)
        off = cur[0]
        cur[0] = (off + nb + 63) // 64 * 64
        assert cur[0] <= AR_BYTES, ("SBUF overflow", cur[0])
        return view_at(off, shape, d)

    K = 1024
    ident = alloc([128, 128], BF16)
    linc = alloc([128, 128], BF16)
    lrest = alloc([128, 128], BF16)
    maskle = alloc([128, 128], F32)
    onesf = alloc([128, 128], F32)
    retc = alloc([128, 8], F32)
    gains = alloc([128, 5, 16], F32)
    wcv = alloc([128, 4, 16], F32)
    bgt = alloc([128, 8], F32)
    epsb = alloc([128, 1], F32)
    oneb = alloc([128, 1], F32)
    small = alloc([128, 64], F32)
    gts = alloc([128, 4, 8], F32)
    gsp = alloc([128, 16], F32)
    gcs = alloc([128, 16], F32)
    gws = alloc([128, 16], F32)
    gosc = alloc([128, 16], F32)
    geg = alloc([128, 16], F32)
    ss4 = alloc([128, 4], F32)
    rstd4 = alloc([128, 4], F32)
    convc = alloc([128, 16, 3], F32)
    cst_off = cur[0]
    Cst = alloc([128, 4, 2, 257], F32)
    Cb = alloc([128, 4, 2, 257], BF16)
    Rst = alloc([128, 4, 2, 256], F32)
    Rb = alloc([128, 4, 2, 256], BF16)
    ktm = [alloc([128, 256], BF16) for _ in range(2)]
    PT = [alloc([128, 128], BF16) for _ in range(2)]
    ctile = [alloc([128, 256], F32) for _ in range(2)]
    ytok1 = alloc([128, D], BF16)
    ytok = [ytok1, ytok1]
    xs = [alloc([128, D], BF16), ytok1]
    NWB = 2
    wbuf = [alloc([128, 16, 512], BF16) for _ in range(NWB)]
    wgate = alloc([128, 16, 8], BF16)
    actT = alloc([128, 16, 512], BF16)
    xres = [alloc([128, D], F32) for _ in range(4)]
    big0 = cur[0]
    BIGSZ = 80 * K
    cur[0] += BIGSZ
    assert cur[0] <= AR_BYTES, ("SBUF overflow", cur[0])
    qkT_ml = view_at(big0, [128, 16, 512], BF16)
    qkT_r = view_at(big0 + 16 * K, [128, 16, 512], BF16)
    vml = view_at(big0 + 32 * K, [128, 4, 4, 258], BF16)
    rv = view_at(big0 + 41 * K, [128, 4, 4, 256], BF16)
    og = view_at(big0 + 49 * K, [128, 4, GW], BF16)
    gs = view_at(big0 + 57 * K, [128, 4, GW], BF16)
    tmpb = big0 + 65 * K
    ubufs = [view_at(tmpb, [128, 516], F32), view_at(tmpb + 8448, [128, 516], F32)]
    caccs = [view_at(tmpb + 2112, [128, 512], F32), view_at(tmpb + 10560, [128, 512], F32)]
    cs_t = view_at(tmpb + 4352, [128, 2, 512], F32)
    rta = view_at(tmpb + 8448, [128, 512], F32)
    rtb = view_at(tmpb + 10560, [128, 512], F32)
    gml_b = view_at(tmpb, [128, GW], F32)
    gret_b = view_at(tmpb + 4 * K, [128, GW], F32)
    hT = view_at(big0, [128, 44, 512], BF16)
    sg = [view_at(big0 + 44 * K + i * 2 * K, [128, 512], F32) for i in range(2)]
    sbuf_used = cur[0]

    psb = [nc.alloc_psum_tensor("psb%d" % i, [128, 512], F32) for i in range(8)]

    def ps16(i):
        return psb[i][:].bitcast(BF16)

    P.dma('pool', 'c_ident', ident, cst[:, 0:128])
    P.dma('pool', 'c_linc', linc, cst[:, 384:512])
    P.dma('pool', 'c_lrest', lrest, cst[:, 512:640])
    P.dma('sp', 'c_maskle', maskle, cst[:, 128:256])
    P.dma('sp', 'c_ones', onesf, cst[:, 256:384])
    P.dma('sp', 'c_retc', retc, cst[:, 640:648])
    gl = [("norm_mix0", 0), ("norm_ffn0", 1), ("norm_mix1", 2), ("norm_ffn1", 3)]
    for n, gi in gl:
        P.dma('sp', 'c_gain%d' % gi, gains[:, gi, :], vf[n].rearrange("(c p) -> p c", p=128),
              allow_slow_non_contiguous=True)
    for k_ in range(4):
        P.dma('sp', 'c_wcv', wcv[:, k_, :], wconv[k_, :].rearrange("(c p) -> p c", p=128),
              allow_slow_non_contiguous=True)
    P.dma('sp', 'c_bgt', bgt, vf["b_gates0"].partition_broadcast(128))
    P.memset('pool', epsb, EPS)
    P.memset('pool', oneb, 1.0)
    P.memset('pool', convc, 0.0)
    P.memset('pool', Cst, 0.0)
    P.memset('pool', Cb, 0.0)
    P.memset('pool', Rst, 0.0)
    P.memset('pool', Rb, 0.0)

    cast_i = [0]

    def cast_weights(names):
        for n in names:
            for si, (kc0, nk, c0, ncol) in enumerate(SL[n]):
                dst = wbs[n][si][:, 0:nk * ncol].rearrange("p (c n) -> p c n", c=nk)
                src = wf[n][kc0 * 128:(kc0 + nk) * 128, c0:c0 + ncol].rearrange("(c p) n -> p c n", p=128)
                P.dma('pool', 'cast%d' % (cast_i[0] % 16), dst, src, rk=[], wk=["%s#%d" % (n, si)])
                cast_i[0] += 1

    P.dma('pool', 'cast15', wgb.rearrange("p (c n) -> p c n", c=16),
          wf["w_in0"][:, 8192:8200].rearrange("(c p) n -> p c n", p=128), rk=[], wk=["wgate#"],
          allow_slow_non_contiguous=True)
    cast_weights(["w_in0", "w_out0", "w_gu0", "w_down0", "w_qkv1", "w_out1", "w_gu1", "w_down1"])

    slab_i = [0]

    xbuf = [view_at(big0 + 48 * K, [128, 16, 512], BF16), view_at(big0 + 64 * K, [128, 16, 512], BF16)]
    slab_pool = [wbuf]

    def load_slab(n, kc0, nk, c0, ncol):
        pool_ = slab_pool[0]
        i = slab_i[0] % len(pool_)
        slab_i[0] += 1
        buf = pool_[i]
        si = SLI[n][(kc0, nk, c0, ncol)]
        src = wbs[n][si][:, 0:nk * ncol].rearrange("p (c n) -> p c n", c=nk)
        P.dma('sp', 'wbuf%d' % i, buf[:, 0:nk, 0:ncol], src, rk=["%s#%d" % (n, si)])
        return buf

    tcnt = [0]

    def rms_to_T(gi):
        P.memset('dve', ss4, 0.0)
        for j in range(4):
            P.act(xs[j % 2], xres[j], AF.Square, accum_out=ss4[:, j:j + 1])
        P.act(rstd4, ss4, AF.Sqrt, scale=1.0 / D, bias=epsb[:, 0:1])
        P.recip(rstd4, rstd4)
        for j in range(4):
            xj = xs[j % 2]
            P.act(xj, xres[j], AF.Copy, scale=rstd4[:, j:j + 1])
            for cg in range(4):
                bank = 6 + (tcnt[0] % 2)
                tcnt[0] += 1
                pv = ps16(bank)
                for ci in range(4):
                    c = cg * 4 + ci
                    P.tr(pv[:, ci * 128:(ci + 1) * 128], xj[:, c * 128:(c + 1) * 128], ident)
                gb = gains[:, gi, cg * 4:(cg + 1) * 4].unsqueeze(2).to_broadcast([128, 4, 128])
                P.tt('dve', actT[:, cg * 4:(cg + 1) * 4, j * 128:(j + 1) * 128],
                     pv[:, 0:512].rearrange("p (a b) -> p a b", a=4), gb, ALU.mult)

    def tok_major(slab, bank0, epi):
        for j in range(4):
            pb = psb[bank0 + j]
            for kc in range(16):
                P.mm(pb[:, 0:512], actT[:, kc, j * 128:(j + 1) * 128], slab[:, kc, :],
                     start=(kc == 0), stop=(kc == 15))
            epi(j, pb)

    def feat_major(slab, fcl, bank):
        pb = psb[bank]
        for kc in range(16):
            P.mm(pb[:, 0:512], slab[:, kc, fcl * 128:(fcl + 1) * 128], actT[:, kc, :],
                 start=(kc == 0), stop=(kc == 15))
        return pb

    POOL4 = [wbuf[0], wbuf[1], xbuf[0], xbuf[1]]

    def ffn(layer, mid_hook=None):
        ffn_body(layer, mid_hook)

    def ffn_body(layer, mid_hook):
        gu = "w_gu%d" % layer
        dn = "w_down%d" % layer
        bi = 0
        for sgi in range(11):
            gslab = load_slab(gu, 0, 16, sgi * 512, 512)
            uslab = load_slab(gu, 0, 16, FF + sgi * 512, 512)
            for hcl in range(4):
                hc = sgi * 4 + hcl
                pa = feat_major(gslab, hcl, (bi % 2) * 2)
                pu = feat_major(uslab, hcl, (bi % 2) * 2 + 1)
                s_ = sg[bi % 2]
                bi += 1
                P.act(s_, pa[:, 0:512], AF.Silu)
                P.tt('dve', hT[:, hc, :], pu[:, 0:512], s_, ALU.mult)
        if mid_hook is not None:
            mid_hook()
        for g in range(4):
            for si, (kc0, nk) in enumerate(((0, 16), (16, 16), (32, 12))):
                slab = load_slab(dn, kc0, nk, g * 512, 512)
                for j in range(4):
                    pb = psb[4 + j]
                    for kc in range(nk):
                        P.mm(pb[:, 0:512], hT[:, kc0 + kc, j * 128:(j + 1) * 128], slab[:, kc, :],
                             start=(si == 0 and kc == 0), stop=(si == 2 and kc == nk - 1))
            for j in range(4):
                P.tt('dve', xres[j][:, g * 512:(g + 1) * 512], psb[4 + j][:, 0:512],
                     xres[j][:, g * 512:(g + 1) * 512], ALU.add)

    def out_proj(wname):
        for g in range(4):
            slab = load_slab(wname, 0, 16, g * 512, 512)

            def epi(j, pb, g=g):
                P.tt('dve', xres[j][:, g * 512:(g + 1) * 512], pb[:, 0:512],
                     xres[j][:, g * 512:(g + 1) * 512], ALU.add)
            tok_major(slab, 0, epi)

    for tb in range(NB):
        t0 = tb * 512
        for j in range(4):
            P.dma('sp', 'xres%d' % j, xres[j], x[t0 + j * 128:t0 + (j + 1) * 128, :])
        P.dma('sp', 'cs_t', cs_t[:, 0, :], cosT[:, t0:t0 + 512])
        P.dma('sp', 'cs_t', cs_t[:, 1, :], sinT[:, t0:t0 + 512])
        rms_to_T(0)
        P.dma('sp', 'wgate', wgate, wgb.rearrange("p (c n) -> p c n", c=16), rk=["wgate#"])
        pg = psb[4]
        for j in range(4):
            for kc in range(16):
                P.mm(pg[:, j * 8:(j + 1) * 8], actT[:, kc, j * 128:(j + 1) * 128], wgate[:, kc, :],
                     start=(kc == 0), stop=(kc == 15))
        for j in range(4):
            P.tt('dve', gts[:, j, :], pg[:, j * 8:(j + 1) * 8], bgt, ALU.add)
        gsp3 = gsp.rearrange("p (j h) -> p j h", j=4)
        P.act(gsp3, gts[:, :, 4:8], AF.Exp, scale=-1.0)
        P.act(gsp, gsp, AF.Ln, bias=oneb[:, 0:1])
        pcs = psb[5]
        P.mm(pcs[:, 0:16], maskle, gsp, start=True, stop=True)
        P.mm(pcs[:, 16:32], onesf, gsp, start=True, stop=True)
        P.copy('dve', gcs, pcs[:, 0:16])
        P.tt('dve', gws.rearrange("p (j h) -> p j h", j=4), gts[:, :, 0:4],
             gcs.rearrange("p (j h) -> p j h", j=4), ALU.add)
        P.act(gws, gws, AF.Exp)
        P.act(gosc, gcs, AF.Exp, scale=-1.0)
        P.ts('dve', gosc, gosc, 1.0 / 16.0, None, op0=ALU.mult)
        P.act(geg, pcs[:, 16:32], AF.Exp, scale=-1.0)

        if stop_after == 'gates':
            pr = [gts.rearrange("p a b -> p (a b)"), gsp, gcs, gws, gosc, geg, ss4, rstd4]
            for i_, a_ in enumerate(pr):
                w_ = a_.shape[1]
                P.copy('dve', xres[3][:, 0:w_], a_)
                P.dma('sp', 'yst3', y[i_ * 128:(i_ + 1) * 128, 0:w_], xres[3][:, 0:w_])
            P.copy('dve', xres[2][:, 0:512], actT[:, 0, :])
            P.dma('sp', 'yst2', y[7 * 128:8 * 128, 512:1024], xres[2][:, 0:512])
            st = P.finalize(final_lanes=['yst3', 'yst2'])
            return nc, st, sbuf_used
        for g in range(4):
            slab = load_slab("w_in0", 0, 16, g * 512, 512)
            for fcl in range(4):
                fc = g * 4 + fcl
                pb = feat_major(slab, fcl, fc % 4)
                ubuf = ubufs[fc % 2]
                cacc = caccs[fc % 2]
                P.copy('act', ubuf[:, 3:515], pb[:, 0:512])
                P.act(cacc, pb[:, 0:512], AF.Copy, scale=wcv[:, 3, fc:fc + 1])
                P.copy('act', ubuf[:, 0:3], convc[:, fc, :])
                for k in (2, 1, 0):
                    P.stt('dve', cacc, ubuf[:, k:k + 512], wcv[:, k, fc:fc + 1], cacc,
                          ALU.mult, ALU.add)
                P.copy('act', convc[:, fc, :], ubuf[:, 512:515])
                P.act(qkT_ml[:, fc, :], cacc, AF.Silu)
        for g in (4, 5):
            slab = load_slab("w_in0", 0, 16, g * 512, 512)

            def epi(j, pb, g=g):
                for hl in range(2):
                    h = (g - 4) * 2 + hl
                    P.ts('dve', vml[:, j, h, 0:256], pb[:, hl * 256:(hl + 1) * 256],
                         gws[:, j * 4 + h:j * 4 + h + 1], None, op0=ALU.mult)
                    P.copy('dve', vml[:, j, h, 256:257], gws[:, j * 4 + h:j * 4 + h + 1])
            tok_major(slab, 0, epi)
        for g in (6, 7):
            slab = load_slab("w_in0", 0, 16, g * 512, 512)

            def epi(j, pb, g=g):
                P.act(og[:, j, (g - 6) * 512:(g - 5) * 512], pb[:, 0:512], AF.Sigmoid)
            tok_major(slab, 0, epi)
        for g in (8, 9, 10, 11):
            slab = load_slab("w_in0", 0, 16, g * 512, 512)
            for hl in range(2):
                fc = (g - 8) * 4 + hl * 2
                p1 = feat_major(slab, hl * 2, 0 + hl * 2)
                p2 = feat_major(slab, hl * 2 + 1, 1 + hl * 2)
                ta = rta
                tb_ = rtb
                P.tt('dve', ta, p1[:, 0:512], cs_t[:, 0, :], ALU.mult)
                P.tt('dve', tb_, p2[:, 0:512], cs_t[:, 1, :], ALU.mult)
                P.tt('dve', qkT_r[:, fc, :], ta, tb_, ALU.subtract)
                P.tt('dve', ta, p1[:, 0:512], cs_t[:, 1, :], ALU.mult)
                P.tt('dve', tb_, p2[:, 0:512], cs_t[:, 0, :], ALU.mult)
                P.tt('dve', qkT_r[:, fc + 1, :], ta, tb_, ALU.add)
        for g in (12, 13):
            slab = load_slab("w_in0", 0, 16, g * 512, 512)

            def epi(j, pb, g=g):
                for hl in range(2):
                    h = (g - 12) * 2 + hl
                    P.ts('dve', rv[:, j, h, :], pb[:, hl * 256:(hl + 1) * 256],
                         retc[:, h:h + 1], None, op0=ALU.mult)
            tok_major(slab, 0, epi)
        for g in (14, 15):
            slab = load_slab("w_in0", 0, 16, g * 512, 512)

            def epi(j, pb, g=g):
                P.act(gs[:, j, (g - 14) * 512:(g - 13) * 512], pb[:, 0:512], AF.Silu)
            tok_major(slab, 0, epi)
        P.dma('sp', 'gml_b', gml_b, vf["g_ml0"].partition_broadcast(128))
        P.dma('sp', 'gret_b', gret_b, vf["g_ret0"].partition_broadcast(128))
        for j in range(4):
            P.tt('dve', og[:, j, :], og[:, j, :], gml_b, ALU.mult)
            P.tt('dve', gs[:, j, :], gs[:, j, :], gret_b, ALU.mult)

        sm = small
        for j in range(4):
            jc = slice(j * 128, (j + 1) * 128)
            yt = ytok[j % 2]
            def hinfo(hh):
                is_ml = hh < 4
                h = hh % 4
                qk = qkT_ml if is_ml else qkT_r
                vv = vml[:, j, h, 0:257] if is_ml else rv[:, j, h, :]
                nv = 257 if is_ml else 256
                return is_ml, h, hh % 2, qk, vv, nv, (Cst if is_ml else Rst), (Cb if is_ml else Rb)

            def stageX(hh):
                is_ml, h, par, qk, vv, nv, Sf, Sb = hinfo(hh)
                pk = ps16(6 + par)
                for dc in range(2):
                    P.tr(pk[:, dc * 128:(dc + 1) * 128], qk[:, 8 + 2 * h + dc, jc], ident)
                P.copy('act', ktm[par], pk[:, 0:256])
                pst = psb[par]
                for dc in range(2):
                    P.mm(pst[:, 0:128], qk[:, 8 + 2 * h + dc, jc], qk[:, 2 * h + dc, jc],
                         start=(dc == 0), stop=(dc == 1))
                P.tt('dve', PT[par], pst[:, 0:128], maskle, ALU.mult)
                egs = geg[:, j * 4 + h:j * 4 + h + 1] if is_ml else RET_DECAY[h]
                P.act(Sf[:, h, :, 0:nv], Sf[:, h, :, 0:nv], AF.Copy, scale=egs)

            def stageY(hh):
                is_ml, h, par, qk, vv, nv, Sf, Sb = hinfo(hh)
                egs = geg[:, j * 4 + h:j * 4 + h + 1] if is_ml else RET_DECAY[h]
                po = psb[2 + par]
                P.mm(po[:, 0:nv], PT[par], vv, start=True, stop=False)
                for dc in range(2):
                    P.mm(po[:, 0:nv], qk[:, 2 * h + dc, jc], Sb[:, h, dc, 0:nv],
                         start=False, stop=(dc == 1))
                for dc in range(2):
                    pd = psb[4 + dc]
                    P.mm(pd[:, 0:nv], ktm[par][:, dc * 128:(dc + 1) * 128], vv, start=True, stop=True)
                    P.stt('dve', Sf[:, h, dc, 0:nv], pd[:, 0:nv], egs, Sf[:, h, dc, 0:nv],
                          ALU.mult, ALU.add)
                    P.copy('act', Sb[:, h, dc, 0:nv], Sf[:, h, dc, 0:nv])
                b0 = hh * 8
                if is_ml:
                    osc = gosc[:, j * 4 + h:j * 4 + h + 1]
                    P.act(sm[:, b0:b0 + 1], po[:, 256:257], AF.Abs, scale=osc)
                    P.ts('dve', sm[:, b0:b0 + 1], sm[:, b0:b0 + 1], 1.0, None, op0=ALU.max)
                    P.recip(sm[:, b0:b0 + 1], sm[:, b0:b0 + 1])
                    P.tt('dve', sm[:, b0 + 1:b0 + 2], sm[:, b0:b0 + 1], osc, ALU.mult)
                    P.memset('dve', sm[:, b0 + 2:b0 + 3], 0.0)
                    P.act(ctile[par], po[:, 0:256], AF.Square, scale=sm[:, b0 + 1:b0 + 2],
                          accum_out=sm[:, b0 + 2:b0 + 3])
                    P.act(sm[:, b0 + 3:b0 + 4], sm[:, b0 + 2:b0 + 3], AF.Sqrt, scale=1.0 / 256.0,
                          bias=epsb[:, 0:1])
                    P.recip(sm[:, b0 + 3:b0 + 4], sm[:, b0 + 3:b0 + 4])
                    P.tt('dve', sm[:, b0 + 4:b0 + 5], sm[:, b0 + 3:b0 + 4], sm[:, b0 + 1:b0 + 2],
                         ALU.mult)
                    P.stt('dve', yt[:, h * 256:(h + 1) * 256], po[:, 0:256], sm[:, b0 + 4:b0 + 5],
                          og[:, j, h * 256:(h + 1) * 256], ALU.mult, ALU.mult)
                else:
                    osc = retc[:, 4 + h:5 + h]
                    P.memset('dve', sm[:, b0:b0 + 2], 0.0)
                    P.act(ctile[par], po[:, 0:256], AF.Copy, accum_out=sm[:, b0:b0 + 1])
                    P.ts('dve', sm[:, b0 + 2:b0 + 3], sm[:, b0:b0 + 1], -1.0 / 256.0, None,
                         op0=ALU.mult)
                    P.act(ctile[par], po[:, 0:256], AF.Square, bias=sm[:, b0 + 2:b0 + 3],
                          accum_out=sm[:, b0 + 1:b0 + 2])
                    P.tt('dve', sm[:, b0 + 3:b0 + 4], sm[:, b0 + 1:b0 + 2], osc, ALU.mult)
                    P.tt('dve', sm[:, b0 + 3:b0 + 4], sm[:, b0 + 3:b0 + 4], osc, ALU.mult)
                    P.act(sm[:, b0 + 3:b0 + 4], sm[:, b0 + 3:b0 + 4], AF.Sqrt, scale=1.0 / 256.0,
                          bias=epsb[:, 0:1])
                    P.recip(sm[:, b0 + 3:b0 + 4], sm[:, b0 + 3:b0 + 4])
                    P.tt('dve', sm[:, b0 + 4:b0 + 5], sm[:, b0 + 3:b0 + 4], osc, ALU.mult)
                    P.tt('dve', sm[:, b0 + 5:b0 + 6], sm[:, b0 + 4:b0 + 5], sm[:, b0 + 2:b0 + 3],
                         ALU.mult)
                    P.act(ctile[par], po[:, 0:256], AF.Identity, scale=sm[:, b0 + 4:b0 + 5],
                          bias=sm[:, b0 + 5:b0 + 6])
                    P.tt('dve', yt[:, GW + h * 256:GW + (h + 1) * 256], ctile[par],
                         gs[:, j, h * 256:(h + 1) * 256], ALU.mult)

            stageX(0)
            for hh in range(8):
                if hh + 1 < 8:
                    stageX(hh + 1)
                stageY(hh)
            for cg in range(4):
                pv = ps16(7)
                for ci in range(4):
                    c = cg * 4 + ci
                    P.tr(pv[:, ci * 128:(ci + 1) * 128], yt[:, c * 128:(c + 1) * 128], ident)
                P.copy('act', actT[:, cg * 4:(cg + 1) * 4, jc],
                       pv[:, 0:512].rearrange("p (a b) -> p a b", a=4))
        if stop_after == 'ymix':
            for kc in range(4):
                P.copy('dve', xres[kc].rearrange("p (a b) -> p a b", a=4), actT[:, kc * 4:(kc + 1) * 4, :])
                P.dma('sp', 'yst%d' % kc, y[t0 + kc * 128:t0 + (kc + 1) * 128, :], xres[kc])
            continue
        slab_pool[0] = POOL4
        out_proj("w_out0")
        if stop_after == 'x1':
            for j in range(4):
                P.dma('sp', 'yst%d' % j, y[t0 + j * 128:t0 + (j + 1) * 128, :], xres[j])
            continue
        rms_to_T(1)
        ffn(0)
        slab_pool[0] = wbuf
        for j in range(4):
            P.dma('sp', 'x2st%d' % j, x2[t0 + j * 128:t0 + (j + 1) * 128, :], xres[j],
                  wk=["x2#%d" % tb])

    if stop_after in ('x1', 'ymix'):
        st = P.finalize(final_lanes=['yst%d' % j for j in range(4)])
        return nc, st, sbuf_used
    if stop_after == 'l0':
        for tb in range(NB):
            for j in range(4):
                t0 = tb * 512 + j * 128
                P.dma('sp', 'xres%d' % j, xres[j], x2[t0:t0 + 128, :], rk=["x2#%d" % tb])
                P.dma('sp', 'yst%d' % j, y[t0:t0 + 128, :], xres[j])
        st = P.finalize(final_lanes=['yst%d' % j for j in range(4)])
        return nc, st, sbuf_used

    slab_pool[0] = POOL4
    stg = view_at(big0, [128, 4, 512], BF16)
    stv = view_at(big0 + 4 * K, [128, 4, 512], BF16)
    for tb in range(NB):
        t0 = tb * 512
        for j in range(4):
            P.dma('sp', 'xres%d' % j, xres[j], x2[t0 + j * 128:t0 + (j + 1) * 128, :],
                  rk=["x2#%d" % tb])
        rms_to_T(2)
        for g in range(8):
            slab = load_slab("w_qkv1", 0, 16, g * 512, 512)
            for fcl in range(4):
                pb = feat_major(slab, fcl, fcl)
                if g < 4:
                    P.act(stg[:, fcl, :], pb[:, 0:512], AF.Copy, scale=128.0 ** -0.5)
                else:
                    P.copy('dve', stg[:, fcl, :], pb[:, 0:512])
            dst = (qT1 if g < 4 else kT1)[(g % 4) * 4:(g % 4) * 4 + 4, :, t0:t0 + 512]
            P.dma('sp', 'stg', dst.rearrange("h p t -> p h t"), stg,
                  wk=[("qT1#%d" if g < 4 else "kT1#%d") % tb])
        for g in range(8, 12):
            slab = load_slab("w_qkv1", 0, 16, g * 512, 512)

            def epi(j, pb, g=g):
                P.copy('act' if j % 2 else 'dve', stv[:, j, :], pb[:, 0:512])
            tok_major(slab, 0, epi)
            c0 = (g - 8) * 512
            P.dma('sp', 'stv', v1[t0:t0 + 512, c0:c0 + 512].rearrange("(j p) c -> p j c", p=128),
                  stv, wk=["v1#%d" % tb])

    amask = view_at(big0, [128, 4, 512], BF16)
    kTh = [view_at(big0 + 4 * K + i * 24 * K, [128, S], BF16) for i in range(2)]
    qTh = [view_at(big0 + 4 * K + i * 24 * K + 8 * K, [128, S], BF16) for i in range(2)]
    vh = [view_at(big0 + 4 * K + i * 24 * K + 16 * K, [128, NKB, 128], BF16) for i in range(2)]
    assert 4 * K + 48 * K <= BIGSZ and S * 2 <= 8 * K
    def wtiles(sidx):
        f = wbuf[sidx].bitcast(F32).rearrange("p a b -> p (a b)")
        b = wbuf[sidx].rearrange("p a b -> p (a b)")
        return dict(e=[f[:, 0:512], f[:, 512:1024]], ecs=[f[:, 1024:1536], f[:, 1536:2048]],
                    sp=[b[:, 4096:4608], b[:, 4608:5120]], a=[b[:, 5120:5632], b[:, 5632:6144]],
                    o=[b[:, 6144:6656], b[:, 6656:7168]])
    WT = [wtiles(0), wtiles(1)]
    P.dma('pool', 'c_amask', amask, amask_d.rearrange("p (a b) -> p a b", a=4))
    allq = ["qT1#%d" % tb for tb in range(NB)]
    allk = ["kT1#%d" % tb for tb in range(NB)]
    allv = ["v1#%d" % tb for tb in range(NB)]
    ocnt = [0, 0]
    for hpair in range(8):
        for sidx in range(2):
            h = hpair * 2 + sidx
            P.dma('sp', 'kTh%d' % sidx, kTh[sidx], kT1[h], rk=allk)
            P.dma('sp', 'qTh%d' % sidx, qTh[sidx], qT1[h], rk=allq)
            P.dma('sp', 'vh%d' % sidx, vh[sidx],
                  v1[:, h * 128:(h + 1) * 128].rearrange("(b p) e -> p b e", p=128), rk=allv)
        for G in range(NB):
            qs = slice(G * 512, (G + 1) * 512)
            kbs = list(range(4 * G + 3, -1, -1))

            L = len(kbs)

            def zmm(sidx, n):
                pz = psb[2 * sidx + n % 2]
                kb = kbs[n]
                P.mm(pz[:, 0:512], kTh[sidx][:, kb * 128:(kb + 1) * 128], qTh[sidx][:, qs],
                     start=True, stop=True)

            def expln(sidx, n):
                w = WT[sidx]
                kb = kbs[n]
                pz = psb[2 * sidx + n % 2]
                e = w['e'][n % 2]
                sp_ = w['sp'][n % 2]
                P.act(e, pz[:, 0:512], AF.Exp)
                P.act(sp_, e, AF.Ln, bias=oneb[:, 0:1])
                if kb >= 4 * G:
                    i = kb - 4 * G
                    P.tt('pool', sp_, sp_, amask[:, i, :], ALU.mult)
                    P.tt('pool', e, e, amask[:, i, :], ALU.mult)

            for n in range(-2, L + 1):
                for sidx in range(2):
                    if 0 <= n + 2 < L:
                        zmm(sidx, n + 2)
                for sidx in range(2):
                    if 0 <= n + 1 < L:
                        expln(sidx, n + 1)
                for sidx in range(2):
                    if 0 <= n - 1 < L:
                        w = WT[sidx]
                        m = n - 1
                        P.mm(psb[6 + sidx][:, 0:512], vh[sidx][:, kbs[m], :], w['a'][m % 2],
                             start=(m == 0), stop=(m == L - 1))
                for sidx in range(2):
                    if 0 <= n < L:
                        w = WT[sidx]
                        P.act(w['ecs'][n % 2], psb[4 + sidx][:, 0:512], AF.Exp, scale=-1.0)
                for sidx in range(2):
                    w = WT[sidx]
                    if 0 <= n < L - 1:
                        P.mm(psb[4 + sidx][:, 0:512], lrest, w['sp'][n % 2], start=False, stop=False)
                    if 0 <= n + 1 < L:
                        P.mm(psb[4 + sidx][:, 0:512], linc, w['sp'][(n + 1) % 2],
                             start=(n + 1 == 0), stop=False)
                for sidx in range(2):
                    if 0 <= n < L:
                        w = WT[sidx]
                        P.tt('dve', w['a'][n % 2], w['e'][n % 2], w['ecs'][n % 2], ALU.mult)
            for sidx in range(2):
                h = hpair * 2 + sidx
                ot = WT[sidx]['o'][ocnt[sidx] % 2]
                ocnt[sidx] += 1
                P.copy('dve', ot, psb[6 + sidx][:, 0:512])
                P.dma('sp', 'ost%d' % sidx, oT1[h, :, qs], ot, wk=["oT1#%d" % G])

    fnb = view_at(cst_off, [128, D], F32)
    P.dma('sp', 'fnb', fnb, vf["final_norm"].partition_broadcast(128))
    for tb in range(NB):
        t0 = tb * 512
        for j in range(4):
            P.dma('sp', 'xres%d' % j, xres[j], x2[t0 + j * 128:t0 + (j + 1) * 128, :],
                  rk=["x2#%d" % tb])
        def load_actT(tbn):
            P.dma('sp', 'actT', actT, oT1[:, :, tbn * 512:tbn * 512 + 512].rearrange("h p t -> p h t"),
                  rk=["oT1#%d" % tbn])
        if tb == 0:
            load_actT(0)
        out_proj("w_out1")
        rms_to_T(3)
        ffn(1, mid_hook=(lambda tbn=tb + 1: load_actT(tbn)) if tb + 1 < NB else None)
        P.memset('dve', ss4, 0.0)
        for j in range(4):
            P.act(xs[j % 2], xres[j], AF.Square, accum_out=ss4[:, j:j + 1])
        P.act(rstd4, ss4, AF.Sqrt, scale=1.0 / D, bias=epsb[:, 0:1])
        P.recip(rstd4, rstd4)
        for j in range(4):
            P.stt('dve', xres[j], xres[j], rstd4[:, j:j + 1], fnb,
                  ALU.mult, ALU.mult)
            P.dma('sp', 'yst%d' % j, y[t0 + j * 128:t0 + (j + 1) * 128, :], xres[j])
    st = P.finalize(final_lanes=['yst%d' % j for j in range(4)])
    return nc, st, sbuf_used


_CACHE = {}


def kernel(**inputs):
    x = np.asarray(inputs["x"], dtype=np.float32)
    B, S, _ = x.shape
    if S not in _CACHE:
        _CACHE[S] = (build(S)[0], host_consts(S))
    nc, consts = _CACHE[S]
    base = {}
    for n, _, _ in WSPEC:
        base[n] = np.ascontiguousarray(np.asarray(inputs[n], dtype=np.float32))
    for n, _ in VSPEC:
        base[n] = np.ascontiguousarray(np.asarray(inputs[n], dtype=np.float32))
    base["w_conv0"] = np.ascontiguousarray(np.asarray(inputs["w_conv0"], dtype=np.float32))
    base.update(consts)
    in_maps = []
    for b in range(B):
        m = dict(base)
        m["x"] = np.ascontiguousarray(x[b])
        in_maps.append(m)
    res = run_bass_kernel_spmd(nc, in_maps, core_ids=list(range(B)))
    return np.stack([np.asarray(r["y"], dtype=np.float32) for r in res.results], axis=0)
```

```python
import numpy as np
import concourse.bass as bass
import concourse.mybir as mybir
from concourse.bass_utils import run_bass_kernel_spmd

F32 = mybir.dt.float32
BF16 = mybir.dt.bfloat16
AF = mybir.ActivationFunctionType
ALU = mybir.AluOpType

D = 2048
GW = 1024
FF = 5632
NIN = 8200
EPS = 1e-6
EPOCH = 16000
COMPUTE = ('pe', 'act', 'dve', 'pool')
PG = 1024


def _keys(a):
    if isinstance(a, str):
        return (a,)
    if isinstance(a, tuple):
        return a
    sp = a.space.name
    name = a.tensor.name
    if sp == 'DRAM':
        return (name,)
    ap = a.ap
    stride = ap[0][0]
    off = a.offset % stride if stride > 0 else a.offset
    ext = 0
    for st, cnt in ap[1:]:
        ext += (cnt - 1) * abs(st)
    sz = mybir.dt.size(a.dtype)
    lo = off * sz
    hi = (off + ext) * sz + sz - 1
    pg = 2048 if sp == 'PSUM' else PG
    return tuple((name, p) for p in range(lo // pg, hi // pg + 1))


class Instr:
    __slots__ = ('idx', 'eng', 'fn', 'deps', 'lane', 'lane_idx', 'needs_inc', 'seq',
                 'waits', 'is_dma', 'vc')


class Prog:
    def __init__(self, nc):
        self.nc = nc
        self.instrs = []
        self.last_w = {}
        self.readers = {}
        self.lane_last = {}
        self.lane_cnt = {}
        self.lane_sem = {}

    def add(self, eng, fn, reads, writes, lane=None):
        ins = Instr()
        ins.idx = len(self.instrs)
        ins.eng = eng
        ins.fn = fn
        ins.lane = lane
        ins.is_dma = lane is not None
        ins.needs_inc = False
        ins.seq = None
        ins.waits = None
        ins.vc = None
        rk = []
        for r in reads:
            rk.extend(_keys(r))
        wk = []
        for w in writes:
            wk.extend(_keys(w))
        rk = list(dict.fromkeys(rk))
        wk = list(dict.fromkeys(wk))
        deps = set()
        for k in rk:
            if k in self.last_w:
                deps.add(self.last_w[k])
        for k in wk:
            if k in self.last_w:
                deps.add(self.last_w[k])
            for r in self.readers.get(k, ()):
                deps.add(r)
        if lane is not None:
            if lane in self.lane_last:
                deps.add(self.lane_last[lane])
            self.lane_last[lane] = ins.idx
            self.lane_cnt[lane] = self.lane_cnt.get(lane, 0) + 1
            ins.lane_idx = self.lane_cnt[lane]
        else:
            ins.lane_idx = None
        deps.discard(ins.idx)
        if eng == 'pe' and not ins.is_dma:
            deps = {d for d in deps
                    if not (self.instrs[d].eng == 'pe' and not self.instrs[d].is_dma)}
        ins.deps = deps
        for k in wk:
            self.last_w[k] = ins.idx
            self.readers[k] = []
        wks = set(wk)
        for k in rk:
            if k in wks:
                continue
            lst = self.readers.setdefault(k, [])
            if not ins.is_dma:
                lst[:] = [r for r in lst
                          if self.instrs[r].is_dma or self.instrs[r].eng != eng]
            lst.append(ins.idx)
        self.instrs.append(ins)
        return ins

    def dma(self, q, lane, out, in_, rk=None, wk=None, **kw):
        r = rk if rk is not None else [in_]
        w = wk if wk is not None else [out]
        return self.add(q, lambda e: e.dma_start(out=out, in_=in_, **kw), r, w, lane=lane)

    def mm(self, out, lhsT, rhs, start=True, stop=True):
        return self.add('pe', lambda e: e.matmul(out, lhsT, rhs, start=start, stop=stop),
                        [lhsT, rhs], [out])

    def tr(self, out, in_, ident):
        return self.add('pe', lambda e: e.transpose(out, in_, ident), [in_, ident], [out])

    def act(self, out, in_, func, bias=None, scale=None, accum_out=None):
        kw = {}
        r = [in_]
        w = [out]
        if bias is not None:
            kw['bias'] = bias
            if not isinstance(bias, (int, float)):
                r.append(bias)
        if scale is not None:
            kw['scale'] = scale
            if not isinstance(scale, (int, float)):
                r.append(scale)
        if accum_out is not None:
            kw['accum_out'] = accum_out
            w.append(accum_out)
        return self.add('act', lambda e: e.activation(out=out, in_=in_, func=func, **kw), r, w)

    def tt(self, eng, out, in0, in1, op):
        return self.add(eng, lambda e: e.tensor_tensor(out=out, in0=in0, in1=in1, op=op),
                        [in0, in1], [out])

    def ts(self, eng, out, in0, s1, s2=None, op0=ALU.mult, op1=None):
        r = [in0]
        for s in (s1, s2):
            if s is not None and not isinstance(s, (int, float)):
                r.append(s)
        kw = {}
        if op1 is not None:
            kw['op1'] = op1
        return self.add(eng, lambda e: e.tensor_scalar(out=out, in0=in0, scalar1=s1, scalar2=s2,
                                                       op0=op0, **kw), r, [out])

    def stt(self, eng, out, in0, scalar, in1, op0, op1):
        r = [in0, in1]
        if not isinstance(scalar, (int, float)):
            r.append(scalar)
        return self.add(eng, lambda e: e.scalar_tensor_tensor(
            out=out, in0=in0, scalar=scalar, in1=in1, op0=op0, op1=op1), r, [out])

    def copy(self, eng, out, in_):
        if eng == 'act':
            return self.add(eng, lambda e: e.copy(out=out, in_=in_), [in_], [out])
        return self.add(eng, lambda e: e.tensor_copy(out=out, in_=in_), [in_], [out])

    def memset(self, eng, out, val):
        return self.add(eng, lambda e: e.memset(out, val), [], [out])

    def recip(self, out, in_):
        return self.add('dve', lambda e: e.reciprocal(out=out, in_=in_), [in_], [out])

    def finalize(self, final_lanes=()):
        nc = self.nc
        instrs = self.instrs
        for ins in instrs:
            for d in ins.deps:
                di = instrs[d]
                if not di.is_dma:
                    di.needs_inc = True
        cnt = {e: 0 for e in COMPUTE}
        for ins in instrs:
            if not ins.is_dma and ins.needs_inc:
                ins.seq = cnt[ins.eng]
                cnt[ins.eng] += 1
        sems = {}
        for e in COMPUTE:
            nep = max(1, (cnt[e] + EPOCH - 1) // EPOCH)
            for k in range(nep):
                sems[(e, k)] = nc.alloc_semaphore("s_%s_%d" % (e, k))
        for ln in self.lane_cnt:
            self.lane_sem[ln] = nc.alloc_semaphore("l_%s" % ln)

        def src_of(di):
            if di.is_dma:
                return ('L', di.lane), di.lane_idx * 16
            return (di.eng, di.seq // EPOCH), di.seq % EPOCH + 1

        known = {e: {} for e in ('pe', 'act', 'dve', 'pool', 'sp')}
        nwaits = 0
        for ins in instrs:
            kn = known[ins.eng]
            waits = {}
            for d in ins.deps:
                s, v = src_of(instrs[d])
                if kn.get(s, 0) >= v:
                    continue
                if waits.get(s, 0) < v:
                    waits[s] = v
            for d in ins.deps:
                di = instrs[d]
                s, v = src_of(di)
                if s in waits and di.vc is not None:
                    for s2, v2 in di.vc.items():
                        if kn.get(s2, 0) < v2:
                            kn[s2] = v2
            for s, v in waits.items():
                if kn.get(s, 0) < v:
                    kn[s] = v
            ins.waits = waits
            nwaits += len(waits)
            if ins.is_dma:
                vc = dict(kn)
                vc[('L', ins.lane)] = ins.lane_idx * 16
                ins.vc = vc
            elif ins.needs_inc:
                s, v = src_of(ins)
                vc = dict(kn)
                vc[s] = v
                ins.vc = vc
        self.stats = dict(n=len(instrs), nwaits=nwaits, incs=dict(cnt),
                          nsem=len(sems) + len(self.lane_sem))
        per_eng = {e: [] for e in ('pe', 'act', 'dve', 'pool', 'sp')}
        for ins in instrs:
            per_eng[ins.eng].append(ins)

        def sem_of(s):
            if s[0] == 'L':
                return self.lane_sem[s[1]]
            return sems[s]

        def emit(eng_obj, lst, tail):
            for ins in lst:
                for s, v in ins.waits.items():
                    eng_obj.wait_ge(sem_of(s), v)
                bi = ins.fn(eng_obj)
                if ins.is_dma:
                    bi.then_inc(self.lane_sem[ins.lane], 16)
                elif ins.needs_inc:
                    bi.then_inc(sems[(ins.eng, ins.seq // EPOCH)], 1)
            for ln in tail:
                eng_obj.wait_ge(self.lane_sem[ln], self.lane_cnt[ln] * 16)

        with nc.Block() as block:
            @block.tensor
            def _(e):
                emit(e, per_eng['pe'], ())

            @block.scalar
            def _(e):
                emit(e, per_eng['act'], ())

            @block.vector
            def _(e):
                emit(e, per_eng['dve'], ())

            @block.gpsimd
            def _(e):
                emit(e, per_eng['pool'], ())

            @block.sync
            def _(e):
                emit(e, per_eng['sp'], final_lanes)
        return self.stats


def host_consts(S):
    c = np.zeros((128, 648), np.float32)
    i = np.arange(128)
    c[:, 0:128] = np.eye(128)
    c[:, 128:256] = (i[:, None] <= i[None, :])
    c[:, 256:384] = 1.0
    c[:, 384:512] = (i[:, None] >= i[None, :])
    c[:, 512:640] = (i[:, None] < i[None, :])
    lg = np.log(1.0 - 2.0 ** (-5.0 - 2.0 * np.arange(4, dtype=np.float64)))
    pos = np.arange(128, dtype=np.float64)
    c[:, 640:644] = np.exp(-(pos[:, None] + 1.0) * lg[None, :]) * (256.0 ** -0.5)
    c[:, 644:648] = np.exp((pos[:, None] + 1.0) * lg[None, :])
    am = np.zeros((128, 4, 512), np.float32)
    t = np.arange(512)
    for k in range(4):
        am[:, k, :] = ((128 * k + i)[:, None] < t[None, :])
    inv = (10000.0 ** (-np.arange(0, 256, 2, dtype=np.float32) / np.float32(256))).astype(np.float32)
    ang = inv[:, None] * np.arange(S, dtype=np.float32)[None, :]
    return dict(cst=c, amask=am.reshape(128, 2048),
                cosT=np.cos(ang).astype(np.float32), sinT=np.sin(ang).astype(np.float32))


RET_DECAY = [float(np.exp(128.0 * np.log(1.0 - 2.0 ** (-5.0 - 2.0 * h)))) for h in range(4)]

WSPEC = [("w_in0", D, NIN), ("w_out0", D, D), ("w_gu0", D, 2 * FF), ("w_down0", FF, D),
         ("w_qkv1", D, 3 * D), ("w_out1", D, D), ("w_gu1", D, 2 * FF), ("w_down1", FF, D)]
VSPEC = [("norm_mix0", D), ("b_gates0", 8), ("g_ml0", GW), ("g_ret0", GW), ("norm_ffn0", D),
         ("norm_mix1", D), ("norm_ffn1", D), ("final_norm", D)]


def build(S, stop_after=None):
    NB = S // 512
    NKB = S // 128
    nc = bass.Bass("TRN2", target_bir_lowering=False)
    P = Prog(nc)
    dt = {}

    def din(name, shape, d=F32):
        dt[name] = nc.dram_tensor(name, list(shape), d, kind="ExternalInput").ap()
        return dt[name]

    x = din("x", [S, D])
    wf = {}
    for n, k, m in WSPEC:
        wf[n] = din(n, [k, m])
    vf = {}
    for n, m in VSPEC:
        vf[n] = din(n, [m])
    wconv = din("w_conv0", [4, D])
    cst = din("cst", [128, 648])
    amask_d = din("amask", [128, 2048])
    cosT = din("cosT", [128, S])
    sinT = din("sinT", [128, S])
    y = nc.dram_tensor("y", [S, D], F32, kind="ExternalOutput").ap()
    def slab_specs(n):
        if n in ("w_in0",):
            return [(0, 16, g * 512, 512) for g in range(16)]
        if n in ("w_out0", "w_out1"):
            return [(0, 16, g * 512, 512) for g in range(4)]
        if n in ("w_qkv1",):
            return [(0, 16, g * 512, 512) for g in range(12)]
        if n in ("w_gu0", "w_gu1"):
            r = []
            for sgi in range(11):
                r.append((0, 16, sgi * 512, 512))
                r.append((0, 16, FF + sgi * 512, 512))
            return r
        r = []
        for g in range(4):
            for (kc0, nk) in ((0, 16), (16, 16), (32, 12)):
                r.append((kc0, nk, g * 512, 512))
        return r
    SL = {n: slab_specs(n) for n, _, _ in WSPEC}
    SLI = {n: {sp_: i for i, sp_ in enumerate(SL[n])} for n in SL}
    wbs = {n: nc.dram_tensor(n + "_s", [len(SL[n]), 128, 8192], BF16, kind="Internal").ap()
           for n, _, _ in WSPEC}
    wgb = nc.dram_tensor("wgate_s", [128, 128], BF16, kind="Internal").ap()
    x2 = nc.dram_tensor("x2s", [S, D], F32, kind="Internal").ap()
    qT1 = nc.dram_tensor("qT1", [16, 128, S], BF16, kind="Internal").ap()
    kT1 = nc.dram_tensor("kT1", [16, 128, S], BF16, kind="Internal").ap()
    v1 = nc.dram_tensor("v1", [S, D], BF16, kind="Internal").ap()
    oT1 = nc.dram_tensor("oT1", [16, 128, S], BF16, kind="Internal").ap()

    AR_BYTES = 206 * 1024
    ar = nc.alloc_sbuf_tensor("arena", [128, AR_BYTES // 2], BF16)
    cur = [0]

    def view_at(off, shape, d):
        n = 1
        for s_ in shape[1:]:
            n *= s_
        sz = 4 if d == F32 else 2
        a = ar[:, off // 2: off // 2 + n * sz // 2]
        if d == F32:
            a = a.bitcast(F32)
        if len(shape) == 3:
            a = a.rearrange("p (a b) -> p a b", a=shape[1])
        elif len(shape) == 4:
            a = a.rearrange("p (a b c) -> p a b c", a=shape[1], b=shape[2])
        elif len(shape) == 5:
            a = a.rearrange("p (a b c e) -> p a b c e", a=shape[1], b=shape[2], c=shape[3])
        return a

    def alloc(shape, d):
        n = 1
        for s_ in shape[1:]:
            n *= s_
        nb = n * (4 if d == F32 else 2)
        off = cur[0]
        cur[0] = (off + nb + 63) // 64 * 64
        assert cur[0] <= AR_BYTES, ("SBUF overflow", cur[0])
        return view_at(off, shape, d)

    K = 1024
    ident = alloc([128, 128], BF16)
    linc = alloc([128, 128], BF16)
    lrest = alloc([128, 128], BF16)
    maskle = alloc([128, 128], F32)
    onesf = alloc([128, 128], F32)
    retc = alloc([128, 8], F32)
    gains = alloc([128, 5, 16], F32)
    wcv = alloc([128, 4, 16], F32)
    bgt = alloc([128, 8], F32)
    epsb = alloc([128, 1], F32)
    oneb = alloc([128, 1], F32)
    small = alloc([128, 64], F32)
    gts = alloc([128, 4, 8], F32)
    gsp = alloc([128, 16], F32)
    gcs = alloc([128, 16], F32)
    gws = alloc([128, 16], F32)
    gosc = alloc([128, 16], F32)
    geg = alloc([128, 16], F32)
    ss4 = alloc([128, 4], F32)
    rstd4 = alloc([128, 4], F32)
    convc = alloc([128, 16, 3], F32)
    cst_off = cur[0]
    Cst = alloc([128, 4, 2, 257], F32)
    Cb = alloc([128, 4, 2, 257], BF16)
    Rst = alloc([128, 4, 2, 256], F32)
    Rb = alloc([128, 4, 2, 256], BF16)
    ktm = [alloc([128, 256], BF16) for _ in range(2)]
    PT = [alloc([128, 128], BF16) for _ in range(2)]
    ctile = [alloc([128, 256], F32) for _ in range(2)]
    ytok1 = alloc([128, D], BF16)
    ytok = [ytok1, ytok1]
    xs = [alloc([128, D], BF16), ytok1]
    NWB = 2
    wbuf = [alloc([128, 16, 512], BF16) for _ in range(NWB)]
    wgate = alloc([128, 16, 8], BF16)
    actT = alloc([128, 16, 512], BF16)
    xres = [alloc([128, D], F32) for _ in range(4)]
    big0 = cur[0]
    BIGSZ = 80 * K
    cur[0] += BIGSZ
    assert cur[0] <= AR_BYTES, ("SBUF overflow", cur[0])
    qkT_ml = view_at(big0, [128, 16, 512], BF16)
    qkT_r = view_at(big0 + 16 * K, [128, 16, 512], BF16)
    vml = view_at(big0 + 32 * K, [128, 4, 4, 258], BF16)
    rv = view_at(big0 + 41 * K, [128, 4, 4, 256], BF16)
    og = view_at(big0 + 49 * K, [128, 4, GW], BF16)
    gs = view_at(big0 + 57 * K, [128, 4, GW], BF16)
    tmpb = big0 + 65 * K
    ubufs = [view_at(tmpb, [128, 516], F32), view_at(tmpb + 8448, [128, 516], F32)]
    caccs = [view_at(tmpb + 2112, [128, 512], F32), view_at(tmpb + 10560, [128, 512], F32)]
    cs_t = view_at(tmpb + 4352, [128, 2, 512], F32)
    rta = view_at(tmpb + 8448, [128, 512], F32)
    rtb = view_at(tmpb + 10560, [128, 512], F32)
    gml_b = view_at(tmpb, [128, GW], F32)
    gret_b = view_at(tmpb + 4 * K, [128, GW], F32)
    hT = view_at(big0, [128, 44, 512], BF16)
    sg = [view_at(big0 + 44 * K + i * 2 * K, [128, 512], F32) for i in range(2)]
    sbuf_used = cur[0]

    psb = [nc.alloc_psum_tensor("psb%d" % i, [128, 512], F32) for i in range(8)]

    def ps16(i):
        return psb[i][:].bitcast(BF16)

    P.dma('pool', 'c_ident', ident, cst[:, 0:128])
    P.dma('pool', 'c_linc', linc, cst[:, 384:512])
    P.dma('pool', 'c_lrest', lrest, cst[:, 512:640])
    P.dma('sp', 'c_maskle', maskle, cst[:, 128:256])
    P.dma('sp', 'c_ones', onesf, cst[:, 256:384])
    P.dma('sp', 'c_retc', retc, cst[:, 640:648])
    gl = [("norm_mix0", 0), ("norm_ffn0", 1), ("norm_mix1", 2), ("norm_ffn1", 3)]
    for n, gi in gl:
        P.dma('sp', 'c_gain%d' % gi, gains[:, gi, :], vf[n].rearrange("(c p) -> p c", p=128),
              allow_slow_non_contiguous=True)
    for k_ in range(4):
        P.dma('sp', 'c_wcv', wcv[:, k_, :], wconv[k_, :].rearrange("(c p) -> p c", p=128),
              allow_slow_non_contiguous=True)
    P.dma('sp', 'c_bgt', bgt, vf["b_gates0"].partition_broadcast(128))
    P.memset('pool', epsb, EPS)
    P.memset('pool', oneb, 1.0)
    P.memset('pool', convc, 0.0)
    P.memset('pool', Cst, 0.0)
    P.memset('pool', Cb, 0.0)
    P.memset('pool', Rst, 0.0)
    P.memset('pool', Rb, 0.0)

    cast_i = [0]

    def cast_weights(names):
        for n in names:
            for si, (kc0, nk, c0, ncol) in enumerate(SL[n]):
                dst = wbs[n][si][:, 0:nk * ncol].rearrange("p (c n) -> p c n", c=nk)
                src = wf[n][kc0 * 128:(kc0 + nk) * 128, c0:c0 + ncol].rearrange("(c p) n -> p c n", p=128)
                P.dma('pool', 'cast%d' % (cast_i[0] % 16), dst, src, rk=[], wk=["%s#%d" % (n, si)])
                cast_i[0] += 1

    P.dma('pool', 'cast15', wgb.rearrange("p (c n) -> p c n", c=16),
          wf["w_in0"][:, 8192:8200].rearrange("(c p) n -> p c n", p=128), rk=[], wk=["wgate#"],
          allow_slow_non_contiguous=True)
    cast_weights(["w_in0", "w_out0", "w_gu0", "w_down0", "w_qkv1", "w_out1", "w_gu1", "w_down1"])

    slab_i = [0]

    xbuf = [view_at(big0 + 48 * K, [128, 16, 512], BF16), view_at(big0 + 64 * K, [128, 16, 512], BF16)]
    slab_pool = [wbuf]

    def load_slab(n, kc0, nk, c0, ncol):
        pool_ = slab_pool[0]
        i = slab_i[0] % len(pool_)
        slab_i[0] += 1
        buf = pool_[i]
        si = SLI[n][(kc0, nk, c0, ncol)]
        src = wbs[n][si][:, 0:nk * ncol].rearrange("p (c n) -> p c n", c=nk)
        P.dma('sp', 'wbuf%d' % i, buf[:, 0:nk, 0:ncol], src, rk=["%s#%d" % (n, si)])
        return buf

    tcnt = [0]

    def rms_to_T(gi):
        P.memset('dve', ss4, 0.0)
        for j in range(4):
            P.act(xs[j % 2], xres[j], AF.Square, accum_out=ss4[:, j:j + 1])
        P.act(rstd4, ss4, AF.Sqrt, scale=1.0 / D, bias=epsb[:, 0:1])
        P.recip(rstd4, rstd4)
        for j in range(4):
            xj = xs[j % 2]
            P.act(xj, xres[j], AF.Copy, scale=rstd4[:, j:j + 1])
            for cg in range(4):
                bank = 6 + (tcnt[0] % 2)
                tcnt[0] += 1
                pv = ps16(bank)
                for ci in range(4):
                    c = cg * 4 + ci
                    P.tr(pv[:, ci * 128:(ci + 1) * 128], xj[:, c * 128:(c + 1) * 128], ident)
                gb = gains[:, gi, cg * 4:(cg + 1) * 4].unsqueeze(2).to_broadcast([128, 4, 128])
                P.tt('dve', actT[:, cg * 4:(cg + 1) * 4, j * 128:(j + 1) * 128],
                     pv[:, 0:512].rearrange("p (a b) -> p a b", a=4), gb, ALU.mult)

    def tok_major(slab, bank0, epi):
        for j in range(4):
            pb = psb[bank0 + j]
            for kc in range(16):
                P.mm(pb[:, 0:512], actT[:, kc, j * 128:(j + 1) * 128], slab[:, kc, :],
                     start=(kc == 0), stop=(kc == 15))
            epi(j, pb)

    def feat_major(slab, fcl, bank):
        pb = psb[bank]
        for kc in range(16):
            P.mm(pb[:, 0:512], slab[:, kc, fcl * 128:(fcl + 1) * 128], actT[:, kc, :],
                 start=(kc == 0), stop=(kc == 15))
        return pb

    POOL4 = [wbuf[0], wbuf[1], xbuf[0], xbuf[1]]

    def ffn(layer, mid_hook=None):
        ffn_body(layer, mid_hook)

    def ffn_body(layer, mid_hook):
        gu = "w_gu%d" % layer
        dn = "w_down%d" % layer
        bi = 0
        for sgi in range(11):
            gslab = load_slab(gu, 0, 16, sgi * 512, 512)
            uslab = load_slab(gu, 0, 16, FF + sgi * 512, 512)
            for hcl in range(4):
                hc = sgi * 4 + hcl
                pa = feat_major(gslab, hcl, (bi % 2) * 2)
                pu = feat_major(uslab, hcl, (bi % 2) * 2 + 1)
                s_ = sg[bi % 2]
                bi += 1
                P.act(s_, pa[:, 0:512], AF.Silu)
                P.tt('dve', hT[:, hc, :], pu[:, 0:512], s_, ALU.mult)
        if mid_hook is not None:
            mid_hook()
        for g in range(4):
            for si, (kc0, nk) in enumerate(((0, 16), (16, 16), (32, 12))):
                slab = load_slab(dn, kc0, nk, g * 512, 512)
                for j in range(4):
                    pb = psb[4 + j]
                    for kc in range(nk):
                        P.mm(pb[:, 0:512], hT[:, kc0 + kc, j * 128:(j + 1) * 128], slab[:, kc, :],
                             start=(si == 0 and kc == 0), stop=(si == 2 and kc == nk - 1))
            for j in range(4):
                P.tt('dve', xres[j][:, g * 512:(g + 1) * 512], psb[4 + j][:, 0:512],
                     xres[j][:, g * 512:(g + 1) * 512], ALU.add)

    def out_proj(wname):
        for g in range(4):
            slab = load_slab(wname, 0, 16, g * 512, 512)

            def epi(j, pb, g=g):
                P.tt('dve', xres[j][:, g * 512:(g + 1) * 512], pb[:, 0:512],
                     xres[j][:, g * 512:(g + 1) * 512], ALU.add)
            tok_major(slab, 0, epi)

    for tb in range(NB):
        t0 = tb * 512
        for j in range(4):
            P.dma('sp', 'xres%d' % j, xres[j], x[t0 + j * 128:t0 + (j + 1) * 128, :])
        P.dma('sp', 'cs_t', cs_t[:, 0, :], cosT[:, t0:t0 + 512])
        P.dma('sp', 'cs_t', cs_t[:, 1, :], sinT[:, t0:t0 + 512])
        rms_to_T(0)
        P.dma('sp', 'wgate', wgate, wgb.rearrange("p (c n) -> p c n", c=16), rk=["wgate#"])
        pg = psb[4]
        for j in range(4):
            for kc in range(16):
                P.mm(pg[:, j * 8:(j + 1) * 8], actT[:, kc, j * 128:(j + 1) * 128], wgate[:, kc, :],
                     start=(kc == 0), stop=(kc == 15))
        for j in range(4):
            P.tt('dve', gts[:, j, :], pg[:, j * 8:(j + 1) * 8], bgt, ALU.add)
        gsp3 = gsp.rearrange("p (j h) -> p j h", j=4)
        P.act(gsp3, gts[:, :, 4:8], AF.Exp, scale=-1.0)
        P.act(gsp, gsp, AF.Ln, bias=oneb[:, 0:1])
        pcs = psb[5]
        P.mm(pcs[:, 0:16], maskle, gsp, start=True, stop=True)
        P.mm(pcs[:, 16:32], onesf, gsp, start=True, stop=True)
        P.copy('dve', gcs, pcs[:, 0:16])
        P.tt('dve', gws.rearrange("p (j h) -> p j h", j=4), gts[:, :, 0:4],
             gcs.rearrange("p (j h) -> p j h", j=4), ALU.add)
        P.act(gws, gws, AF.Exp)
        P.act(gosc, gcs, AF.Exp, scale=-1.0)
        P.ts('dve', gosc, gosc, 1.0 / 16.0, None, op0=ALU.mult)
        P.act(geg, pcs[:, 16:32], AF.Exp, scale=-1.0)

        if stop_after == 'gates':
            pr = [gts.rearrange("p a b -> p (a b)"), gsp, gcs, gws, gosc, geg, ss4, rstd4]
            for i_, a_ in enumerate(pr):
                w_ = a_.shape[1]
                P.copy('dve', xres[3][:, 0:w_], a_)
                P.dma('sp', 'yst3', y[i_ * 128:(i_ + 1) * 128, 0:w_], xres[3][:, 0:w_])
            P.copy('dve', xres[2][:, 0:512], actT[:, 0, :])
            P.dma('sp', 'yst2', y[7 * 128:8 * 128, 512:1024], xres[2][:, 0:512])
            st = P.finalize(final_lanes=['yst3', 'yst2'])
            return nc, st, sbuf_used
        for g in range(4):
            slab = load_slab("w_in0", 0, 16, g * 512, 512)
            for fcl in range(4):
                fc = g * 4 + fcl
                pb = feat_major(slab, fcl, fc % 4)
                ubuf = ubufs[fc % 2]
                cacc = caccs[fc % 2]
                P.copy('act', ubuf[:, 3:515], pb[:, 0:512])
                P.act(cacc, pb[:, 0:512], AF.Copy, scale=wcv[:, 3, fc:fc + 1])
                P.copy('act', ubuf[:, 0:3], convc[:, fc, :])
                for k in (2, 1, 0):
                    P.stt('dve', cacc, ubuf[:, k:k + 512], wcv[:, k, fc:fc + 1], cacc,
                          ALU.mult, ALU.add)
                P.copy('act', convc[:, fc, :], ubuf[:, 512:515])
                P.act(qkT_ml[:, fc, :], cacc, AF.Silu)
        for g in (4, 5):
            slab = load_slab("w_in0", 0, 16, g * 512, 512)

            def epi(j, pb, g=g):
                for hl in range(2):
                    h = (g - 4) * 2 + hl
                    P.ts('dve', vml[:, j, h, 0:256], pb[:, hl * 256:(hl + 1) * 256],
                         gws[:, j * 4 + h:j * 4 + h + 1], None, op0=ALU.mult)
                    P.copy('dve', vml[:, j, h, 256:257], gws[:, j * 4 + h:j * 4 + h + 1])
            tok_major(slab, 0, epi)
        for g in (6, 7):
            slab = load_slab("w_in0", 0, 16, g * 512, 512)

            def epi(j, pb, g=g):
                P.act(og[:, j, (g - 6) * 512:(g - 5) * 512], pb[:, 0:512], AF.Sigmoid)
            tok_major(slab, 0, epi)
        for g in (8, 9, 10, 11):
            slab = load_slab("w_in0", 0, 16, g * 512, 512)
            for hl in range(2):
                fc = (g - 8) * 4 + hl * 2
                p1 = feat_major(slab, hl * 2, 0 + hl * 2)
                p2 = feat_major(slab, hl * 2 + 1, 1 + hl * 2)
                ta = rta
                tb_ = rtb
                P.tt('dve', ta, p1[:, 0:512], cs_t[:, 0, :], ALU.mult)
                P.tt('dve', tb_, p2[:, 0:512], cs_t[:, 1, :], ALU.mult)
                P.tt('dve', qkT_r[:, fc, :], ta, tb_, ALU.subtract)
                P.tt('dve', ta, p1[:, 0:512], cs_t[:, 1, :], ALU.mult)
                P.tt('dve', tb_, p2[:, 0:512], cs_t[:, 0, :], ALU.mult)
                P.tt('dve', qkT_r[:, fc + 1, :], ta, tb_, ALU.add)
        for g in (12, 13):
            slab = load_slab("w_in0", 0, 16, g * 512, 512)

            def epi(j, pb, g=g):
                for hl in range(2):
                    h = (g - 12) * 2 + hl
                    P.ts('dve', rv[:, j, h, :], pb[:, hl * 256:(hl + 1) * 256],
                         retc[:, h:h + 1], None, op0=ALU.mult)
            tok_major(slab, 0, epi)
        for g in (14, 15):
            slab = load_slab("w_in0", 0, 16, g * 512, 512)

            def epi(j, pb, g=g):
                P.act(gs[:, j, (g - 14) * 512:(g - 13) * 512], pb[:, 0:512], AF.Silu)
            tok_major(slab, 0, epi)
        P.dma('sp', 'gml_b', gml_b, vf["g_ml0"].partition_broadcast(128))
        P.dma('sp', 'gret_b', gret_b, vf["g_ret0"].partition_broadcast(128))
        for j in range(4):
            P.tt('dve', og[:, j, :], og[:, j, :], gml_b, ALU.mult)
            P.tt('dve', gs[:, j, :], gs[:, j, :], gret_b, ALU.mult)

        sm = small
        for j in range(4):
            jc = slice(j * 128, (j + 1) * 128)
            yt = ytok[j % 2]
            def hinfo(hh):
                is_ml = hh < 4
                h = hh % 4
                qk = qkT_ml if is_ml else qkT_r
                vv = vml[:, j, h, 0:257] if is_ml else rv[:, j, h, :]
                nv = 257 if is_ml else 256
                return is_ml, h, hh % 2, qk, vv, nv, (Cst if is_ml else Rst), (Cb if is_ml else Rb)

            def stageX(hh):
                is_ml, h, par, qk, vv, nv, Sf, Sb = hinfo(hh)
                pk = ps16(6 + par)
                for dc in range(2):
                    P.tr(pk[:, dc * 128:(dc + 1) * 128], qk[:, 8 + 2 * h + dc, jc], ident)
                P.copy('act', ktm[par], pk[:, 0:256])
                pst = psb[par]
                for dc in range(2):
                    P.mm(pst[:, 0:128], qk[:, 8 + 2 * h + dc, jc], qk[:, 2 * h + dc, jc],
                         start=(dc == 0), stop=(dc == 1))
                P.tt('dve', PT[par], pst[:, 0:128], maskle, ALU.mult)
                egs = geg[:, j * 4 + h:j * 4 + h + 1] if is_ml else RET_DECAY[h]
                P.act(Sf[:, h, :, 0:nv], Sf[:, h, :, 0:nv], AF.Copy, scale=egs)

            def stageY(hh):
                is_ml, h, par, qk, vv, nv, Sf, Sb = hinfo(hh)
                egs = geg[:, j * 4 + h:j * 4 + h + 1] if is_ml else RET_DECAY[h]
                po = psb[2 + par]
                P.mm(po[:, 0:nv], PT[par], vv, start=True, stop=False)
                for dc in range(2):
                    P.mm(po[:, 0:nv], qk[:, 2 * h + dc, jc], Sb[:, h, dc, 0:nv],
                         start=False, stop=(dc == 1))
                for dc in range(2):
                    pd = psb[4 + dc]
                    P.mm(pd[:, 0:nv], ktm[par][:, dc * 128:(dc + 1) * 128], vv, start=True, stop=True)
                    P.stt('dve', Sf[:, h, dc, 0:nv], pd[:, 0:nv], egs, Sf[:, h, dc, 0:nv],
                          ALU.mult, ALU.add)
                    P.copy('act', Sb[:, h, dc, 0:nv], Sf[:, h, dc, 0:nv])
                b0 = hh * 8
                if is_ml:
                    osc = gosc[:, j * 4 + h:j * 4 + h + 1]
                    P.act(sm[:, b0:b0 + 1], po[:, 256:257], AF.Abs, scale=osc)
                    P.ts('dve', sm[:, b0:b0 + 1], sm[:, b0:b0 + 1], 1.0, None, op0=ALU.max)
                    P.recip(sm[:, b0:b0 + 1], sm[:, b0:b0 + 1])
                    P.tt('dve', sm[:, b0 + 1:b0 + 2], sm[:, b0:b0 + 1], osc, ALU.mult)
                    P.memset('dve', sm[:, b0 + 2:b0 + 3], 0.0)
                    P.act(ctile[par], po[:, 0:256], AF.Square, scale=sm[:, b0 + 1:b0 + 2],
                          accum_out=sm[:, b0 + 2:b0 + 3])
                    P.act(sm[:, b0 + 3:b0 + 4], sm[:, b0 + 2:b0 + 3], AF.Sqrt, scale=1.0 / 256.0,
                          bias=epsb[:, 0:1])
                    P.recip(sm[:, b0 + 3:b0 + 4], sm[:, b0 + 3:b0 + 4])
                    P.tt('dve', sm[:, b0 + 4:b0 + 5], sm[:, b0 + 3:b0 + 4], sm[:, b0 + 1:b0 + 2],
                         ALU.mult)
                    P.stt('dve', yt[:, h * 256:(h + 1) * 256], po[:, 0:256], sm[:, b0 + 4:b0 + 5],
                          og[:, j, h * 256:(h + 1) * 256], ALU.mult, ALU.mult)
                else:
                    osc = retc[:, 4 + h:5 + h]
                    P.memset('dve', sm[:, b0:b0 + 2], 0.0)
                    P.act(ctile[par], po[:, 0:256], AF.Copy, accum_out=sm[:, b0:b0 + 1])
                    P.ts('dve', sm[:, b0 + 2:b0 + 3], sm[:, b0:b0 + 1], -1.0 / 256.0, None,
                         op0=ALU.mult)
                    P.act(ctile[par], po[:, 0:256], AF.Square, bias=sm[:, b0 + 2:b0 + 3],
                          accum_out=sm[:, b0 + 1:b0 + 2])
                    P.tt('dve', sm[:, b0 + 3:b0 + 4], sm[:, b0 + 1:b0 + 2], osc, ALU.mult)
                    P.tt('dve', sm[:, b0 + 3:b0 + 4], sm[:, b0 + 3:b0 + 4], osc, ALU.mult)
                    P.act(sm[:, b0 + 3:b0 + 4], sm[:, b0 + 3:b0 + 4], AF.Sqrt, scale=1.0 / 256.0,
                          bias=epsb[:, 0:1])
                    P.recip(sm[:, b0 + 3:b0 + 4], sm[:, b0 + 3:b0 + 4])
                    P.tt('dve', sm[:, b0 + 4:b0 + 5], sm[:, b0 + 3:b0 + 4], osc, ALU.mult)
                    P.tt('dve', sm[:, b0 + 5:b0 + 6], sm[:, b0 + 4:b0 + 5], sm[:, b0 + 2:b0 + 3],
                         ALU.mult)
                    P.act(ctile[par], po[:, 0:256], AF.Identity, scale=sm[:, b0 + 4:b0 + 5],
                          bias=sm[:, b0 + 5:b0 + 6])
                    P.tt('dve', yt[:, GW + h * 256:GW + (h + 1) * 256], ctile[par],
                         gs[:, j, h * 256:(h + 1) * 256], ALU.mult)

            stageX(0)
            for hh in range(8):
                if hh + 1 < 8:
                    stageX(hh + 1)
                stageY(hh)
            for cg in range(4):
                pv = ps16(7)
                for ci in range(4):
                    c = cg * 4 + ci
                    P.tr(pv[:, ci * 128:(ci + 1) * 128], yt[:, c * 128:(c + 1) * 128], ident)
                P.copy('act', actT[:, cg * 4:(cg + 1) * 4, jc],
                       pv[:, 0:512].rearrange("p (a b) -> p a b", a=4))
        if stop_after == 'ymix':
            for kc in range(4):
                P.copy('dve', xres[kc].rearrange("p (a b) -> p a b", a=4), actT[:, kc * 4:(kc + 1) * 4, :])
                P.dma('sp', 'yst%d' % kc, y[t0 + kc * 128:t0 + (kc + 1) * 128, :], xres[kc])
            continue
        slab_pool[0] = POOL4
        out_proj("w_out0")
        if stop_after == 'x1':
            for j in range(4):
                P.dma('sp', 'yst%d' % j, y[t0 + j * 128:t0 + (j + 1) * 128, :], xres[j])
            continue
        rms_to_T(1)
        ffn(0)
        slab_pool[0] = wbuf
        for j in range(4):
            P.dma('pool', 'x2st%d' % j, x2[t0 + j * 128:t0 + (j + 1) * 128, :], xres[j],
                  wk=["x2#%d" % tb])

    if stop_after in ('x1', 'ymix'):
        st = P.finalize(final_lanes=['yst%d' % j for j in range(4)])
        return nc, st, sbuf_used
    if stop_after == 'l0':
        for tb in range(NB):
            for j in range(4):
                t0 = tb * 512 + j * 128
                P.dma('sp', 'xres%d' % j, xres[j], x2[t0:t0 + 128, :], rk=["x2#%d" % tb])
                P.dma('sp', 'yst%d' % j, y[t0:t0 + 128, :], xres[j])
        st = P.finalize(final_lanes=['yst%d' % j for j in range(4)])
        return nc, st, sbuf_used

    slab_pool[0] = POOL4
    stg = view_at(big0, [128, 4, 512], BF16)
    stv = view_at(big0 + 4 * K, [128, 4, 512], BF16)
    for tb in range(NB):
        t0 = tb * 512
        for j in range(4):
            P.dma('sp', 'xres%d' % j, xres[j], x2[t0 + j * 128:t0 + (j + 1) * 128, :],
                  rk=["x2#%d" % tb])
        rms_to_T(2)
        for g in range(8):
            slab = load_slab("w_qkv1", 0, 16, g * 512, 512)
            for fcl in range(4):
                pb = feat_major(slab, fcl, fcl)
                if g < 4:
                    P.act(stg[:, fcl, :], pb[:, 0:512], AF.Copy, scale=128.0 ** -0.5)
                else:
                    P.copy('dve', stg[:, fcl, :], pb[:, 0:512])
            dst = (qT1 if g < 4 else kT1)[(g % 4) * 4:(g % 4) * 4 + 4, :, t0:t0 + 512]
            P.dma('pool', 'stg', dst.rearrange("h p t -> p h t"), stg,
                  wk=[("qT1#%d" if g < 4 else "kT1#%d") % tb])
        for g in range(8, 12):
            slab = load_slab("w_qkv1", 0, 16, g * 512, 512)

            def epi(j, pb, g=g):
                P.copy('act' if j % 2 else 'dve', stv[:, j, :], pb[:, 0:512])
            tok_major(slab, 0, epi)
            c0 = (g - 8) * 512
            P.dma('pool', 'stv', v1[t0:t0 + 512, c0:c0 + 512].rearrange("(j p) c -> p j c", p=128),
                  stv, wk=["v1#%d" % tb])

    amask = view_at(big0, [128, 4, 512], BF16)
    kTh = [view_at(big0 + 4 * K + i * 24 * K, [128, S], BF16) for i in range(2)]
    qTh = [view_at(big0 + 4 * K + i * 24 * K + 8 * K, [128, S], BF16) for i in range(2)]
    vh = [view_at(big0 + 4 * K + i * 24 * K + 16 * K, [128, NKB, 128], BF16) for i in range(2)]
    assert 4 * K + 48 * K <= BIGSZ and S * 2 <= 8 * K
    def wtiles(sidx):
        f = wbuf[sidx].bitcast(F32).rearrange("p a b -> p (a b)")
        b = wbuf[sidx].rearrange("p a b -> p (a b)")
        return dict(e=[f[:, 0:512], f[:, 512:1024]], ecs=[f[:, 1024:1536], f[:, 1536:2048]],
                    sp=[b[:, 4096:4608], b[:, 4608:5120]], a=[b[:, 5120:5632], b[:, 5632:6144]],
                    o=[b[:, 6144:6656], b[:, 6656:7168]])
    WT = [wtiles(0), wtiles(1)]
    P.dma('pool', 'c_amask', amask, amask_d.rearrange("p (a b) -> p a b", a=4))
    allq = ["qT1#%d" % tb for tb in range(NB)]
    allk = ["kT1#%d" % tb for tb in range(NB)]
    allv = ["v1#%d" % tb for tb in range(NB)]
    ocnt = [0, 0]
    for hpair in range(8):
        for sidx in range(2):
            h = hpair * 2 + sidx
            P.dma('sp', 'kTh%d' % sidx, kTh[sidx], kT1[h], rk=allk)
            P.dma('sp', 'qTh%d' % sidx, qTh[sidx], qT1[h], rk=allq)
            P.dma('sp', 'vh%d' % sidx, vh[sidx],
                  v1[:, h * 128:(h + 1) * 128].rearrange("(b p) e -> p b e", p=128), rk=allv)
        items = []
        for G in range(NB):
            kbs = list(range(4 * G + 3, -1, -1))
            for n_, kb in enumerate(kbs):
                items.append((G, kb, n_ == 0, n_ == len(kbs) - 1))
        NI = len(items)

        def zmm(sidx, i):
            G, kb, first, last = items[i]
            pz = psb[2 * sidx + i % 2]
            P.mm(pz[:, 0:512], kTh[sidx][:, kb * 128:(kb + 1) * 128],
                 qTh[sidx][:, G * 512:(G + 1) * 512], start=True, stop=True)

        def expln(sidx, i):
            G, kb, first, last = items[i]
            w = WT[sidx]
            pz = psb[2 * sidx + i % 2]
            e = w['e'][i % 2]
            sp_ = w['sp'][i % 2]
            P.act(e, pz[:, 0:512], AF.Exp)
            P.act(sp_, e, AF.Ln, bias=oneb[:, 0:1])
            if kb >= 4 * G:
                k_ = kb - 4 * G
                P.tt('pool', sp_, sp_, amask[:, k_, :], ALU.mult)
                P.tt('pool', e, e, amask[:, k_, :], ALU.mult)

        for i in range(-2, NI + 1):
            for sidx in range(2):
                if 0 <= i + 2 < NI:
                    zmm(sidx, i + 2)
            for sidx in range(2):
                if 0 <= i + 1 < NI:
                    expln(sidx, i + 1)
            for sidx in range(2):
                if 0 <= i - 1 < NI:
                    m = i - 1
                    G, kb, first, last = items[m]
                    w = WT[sidx]
                    P.mm(psb[6 + sidx][:, 0:512], vh[sidx][:, kb, :], w['a'][m % 2],
                         start=first, stop=last)
                    if last:
                        h = hpair * 2 + sidx
                        ot = w['o'][ocnt[sidx] % 2]
                        ocnt[sidx] += 1
                        P.copy('dve', ot, psb[6 + sidx][:, 0:512])
                        P.dma('sp', 'ost%d' % sidx, oT1[h, :, G * 512:(G + 1) * 512], ot,
                              wk=["oT1#%d" % G])
            for sidx in range(2):
                if 0 <= i < NI:
                    w = WT[sidx]
                    P.act(w['ecs'][i % 2], psb[4 + sidx][:, 0:512], AF.Exp, scale=-1.0)
            for sidx in range(2):
                w = WT[sidx]
                if 0 <= i < NI and not items[i][3]:
                    P.mm(psb[4 + sidx][:, 0:512], lrest, w['sp'][i % 2], start=False, stop=False)
                if 0 <= i + 1 < NI:
                    P.mm(psb[4 + sidx][:, 0:512], linc, w['sp'][(i + 1) % 2],
                         start=items[i + 1][2], stop=False)
            for sidx in range(2):
                if 0 <= i < NI:
                    w = WT[sidx]
                    P.tt('dve', w['a'][i % 2], w['e'][i % 2], w['ecs'][i % 2], ALU.mult)

    fnb = view_at(cst_off, [128, D], F32)
    P.dma('sp', 'fnb', fnb, vf["final_norm"].partition_broadcast(128))
    for tb in range(NB):
        t0 = tb * 512
        for j in range(4):
            P.dma('sp', 'xres%d' % j, xres[j], x2[t0 + j * 128:t0 + (j + 1) * 128, :],
                  rk=["x2#%d" % tb])
        def load_actT(tbn):
            P.dma('sp', 'actT', actT, oT1[:, :, tbn * 512:tbn * 512 + 512].rearrange("h p t -> p h t"),
                  rk=["oT1#%d" % tbn])
        if tb == 0:
            load_actT(0)
        out_proj("w_out1")
        rms_to_T(3)
        ffn(1, mid_hook=(lambda tbn=tb + 1: load_actT(tbn)) if tb + 1 < NB else None)
        P.memset('dve', ss4, 0.0)
        for j in range(4):
            P.act(xs[j % 2], xres[j], AF.Square, accum_out=ss4[:, j:j + 1])
        P.act(rstd4, ss4, AF.Sqrt, scale=1.0 / D, bias=epsb[:, 0:1])
        P.recip(rstd4, rstd4)
        for j in range(4):
            P.stt('dve', xres[j], xres[j], rstd4[:, j:j + 1], fnb,
                  ALU.mult, ALU.mult)
            P.dma('pool', 'yst%d' % j, y[t0 + j * 128:t0 + (j + 1) * 128, :], xres[j])
    st = P.finalize(final_lanes=['yst%d' % j for j in range(4)])
    return nc, st, sbuf_used


_CACHE = {}


def kernel(**inputs):
    x = np.asarray(inputs["x"], dtype=np.float32)
    B, S, _ = x.shape
    if S not in _CACHE:
        _CACHE[S] = (build(S)[0], host_consts(S))
    nc, consts = _CACHE[S]
    base = {}
    for n, _, _ in WSPEC:
        base[n] = np.ascontiguousarray(np.asarray(inputs[n], dtype=np.float32))
    for n, _ in VSPEC:
        base[n] = np.ascontiguousarray(np.asarray(inputs[n], dtype=np.float32))
    base["w_conv0"] = np.ascontiguousarray(np.asarray(inputs["w_conv0"], dtype=np.float32))
    base.update(consts)
    in_maps = []
    for b in range(B):
        m = dict(base)
        m["x"] = np.ascontiguousarray(x[b])
        in_maps.append(m)
    res = run_bass_kernel_spmd(nc, in_maps, core_ids=list(range(B)))
    return np.stack([np.asarray(r["y"], dtype=np.float32) for r in res.results], axis=0)
```

```python
import numpy as np
import concourse.bass as bass
import concourse.mybir as mybir
from concourse.bass_utils import run_bass_kernel_spmd

F32 = mybir.dt.float32
BF16 = mybir.dt.bfloat16
AF = mybir.ActivationFunctionType
ALU = mybir.AluOpType

D = 2048
GW = 1024
FF = 5632
NIN = 8200
EPS = 1e-6
EPOCH = 16000
COMPUTE = ('pe', 'act', 'dve', 'pool')
PG = 1024


def _keys(a):
    if isinstance(a, str):
        return (a,)
    if isinstance(a, tuple):
        return a
    sp = a.space.name
    name = a.tensor.name
    if sp == 'DRAM':
        return (name,)
    ap = a.ap
    stride = ap[0][0]
    off = a.offset % stride if stride > 0 else a.offset
    ext = 0
    for st, cnt in ap[1:]:
        ext += (cnt - 1) * abs(st)
    sz = mybir.dt.size(a.dtype)
    lo = off * sz
    hi = (off + ext) * sz + sz - 1
    pg = 2048 if sp == 'PSUM' else PG
    return tuple((name, p) for p in range(lo // pg, hi // pg + 1))


class Instr:
    __slots__ = ('idx', 'eng', 'fn', 'deps', 'lane', 'lane_idx', 'needs_inc', 'seq',
                 'waits', 'is_dma', 'vc')


class Prog:
    def __init__(self, nc):
        self.nc = nc
        self.instrs = []
        self.last_w = {}
        self.readers = {}
        self.lane_last = {}
        self.lane_cnt = {}
        self.lane_sem = {}

    def add(self, eng, fn, reads, writes, lane=None):
        ins = Instr()
        ins.idx = len(self.instrs)
        ins.eng = eng
        ins.fn = fn
        ins.lane = lane
        ins.is_dma = lane is not None
        ins.needs_inc = False
        ins.seq = None
        ins.waits = None
        ins.vc = None
        rk = []
        for r in reads:
            rk.extend(_keys(r))
        wk = []
        for w in writes:
            wk.extend(_keys(w))
        rk = list(dict.fromkeys(rk))
        wk = list(dict.fromkeys(wk))
        deps = set()
        for k in rk:
            if k in self.last_w:
                deps.add(self.last_w[k])
        for k in wk:
            if k in self.last_w:
                deps.add(self.last_w[k])
            for r in self.readers.get(k, ()):
                deps.add(r)
        if lane is not None:
            if lane in self.lane_last:
                deps.add(self.lane_last[lane])
            self.lane_last[lane] = ins.idx
            self.lane_cnt[lane] = self.lane_cnt.get(lane, 0) + 1
            ins.lane_idx = self.lane_cnt[lane]
        else:
            ins.lane_idx = None
        deps.discard(ins.idx)
        if eng == 'pe' and not ins.is_dma:
            deps = {d for d in deps
                    if not (self.instrs[d].eng == 'pe' and not self.instrs[d].is_dma)}
        ins.deps = deps
        for k in wk:
            self.last_w[k] = ins.idx
            self.readers[k] = []
        wks = set(wk)
        for k in rk:
            if k in wks:
                continue
            lst = self.readers.setdefault(k, [])
            if not ins.is_dma:
                lst[:] = [r for r in lst
                          if self.instrs[r].is_dma or self.instrs[r].eng != eng]
            lst.append(ins.idx)
        self.instrs.append(ins)
        return ins

    def dma(self, q, lane, out, in_, rk=None, wk=None, **kw):
        r = rk if rk is not None else [in_]
        w = wk if wk is not None else [out]
        return self.add(q, lambda e: e.dma_start(out=out, in_=in_, **kw), r, w, lane=lane)

    def mm(self, out, lhsT, rhs, start=True, stop=True):
        return self.add('pe', lambda e: e.matmul(out, lhsT, rhs, start=start, stop=stop),
                        [lhsT, rhs], [out])

    def tr(self, out, in_, ident):
        return self.add('pe', lambda e: e.transpose(out, in_, ident), [in_, ident], [out])

    def act(self, out, in_, func, bias=None, scale=None, accum_out=None):
        kw = {}
        r = [in_]
        w = [out]
        if bias is not None:
            kw['bias'] = bias
            if not isinstance(bias, (int, float)):
                r.append(bias)
        if scale is not None:
            kw['scale'] = scale
            if not isinstance(scale, (int, float)):
                r.append(scale)
        if accum_out is not None:
            kw['accum_out'] = accum_out
            w.append(accum_out)
        return self.add('act', lambda e: e.activation(out=out, in_=in_, func=func, **kw), r, w)

    def tt(self, eng, out, in0, in1, op):
        return self.add(eng, lambda e: e.tensor_tensor(out=out, in0=in0, in1=in1, op=op),
                        [in0, in1], [out])

    def ts(self, eng, out, in0, s1, s2=None, op0=ALU.mult, op1=None):
        r = [in0]
        for s in (s1, s2):
            if s is not None and not isinstance(s, (int, float)):
                r.append(s)
        kw = {}
        if op1 is not None:
            kw['op1'] = op1
        return self.add(eng, lambda e: e.tensor_scalar(out=out, in0=in0, scalar1=s1, scalar2=s2,
                                                       op0=op0, **kw), r, [out])

    def stt(self, eng, out, in0, scalar, in1, op0, op1):
        r = [in0, in1]
        if not isinstance(scalar, (int, float)):
            r.append(scalar)
        return self.add(eng, lambda e: e.scalar_tensor_tensor(
            out=out, in0=in0, scalar=scalar, in1=in1, op0=op0, op1=op1), r, [out])

    def copy(self, eng, out, in_):
        if eng == 'act':
            return self.add(eng, lambda e: e.copy(out=out, in_=in_), [in_], [out])
        return self.add(eng, lambda e: e.tensor_copy(out=out, in_=in_), [in_], [out])

    def memset(self, eng, out, val):
        return self.add(eng, lambda e: e.memset(out, val), [], [out])

    def recip(self, out, in_):
        return self.add('dve', lambda e: e.reciprocal(out=out, in_=in_), [in_], [out])

    def finalize(self, final_lanes=()):
        nc = self.nc
        instrs = self.instrs
        for ins in instrs:
            for d in ins.deps:
                di = instrs[d]
                if not di.is_dma:
                    di.needs_inc = True
        cnt = {e: 0 for e in COMPUTE}
        for ins in instrs:
            if not ins.is_dma and ins.needs_inc:
                ins.seq = cnt[ins.eng]
                cnt[ins.eng] += 1
        sems = {}
        for e in COMPUTE:
            nep = max(1, (cnt[e] + EPOCH - 1) // EPOCH)
            for k in range(nep):
                sems[(e, k)] = nc.alloc_semaphore("s_%s_%d" % (e, k))
        for ln in self.lane_cnt:
            self.lane_sem[ln] = nc.alloc_semaphore("l_%s" % ln)

        def src_of(di):
            if di.is_dma:
                return ('L', di.lane), di.lane_idx * 16
            return (di.eng, di.seq // EPOCH), di.seq % EPOCH + 1

        known = {e: {} for e in ('pe', 'act', 'dve', 'pool', 'sp')}
        nwaits = 0
        for ins in instrs:
            kn = known[ins.eng]
            waits = {}
            for d in ins.deps:
                s, v = src_of(instrs[d])
                if kn.get(s, 0) >= v:
                    continue
                if waits.get(s, 0) < v:
                    waits[s] = v
            for d in ins.deps:
                di = instrs[d]
                s, v = src_of(di)
                if s in waits and di.vc is not None:
                    for s2, v2 in di.vc.items():
                        if kn.get(s2, 0) < v2:
                            kn[s2] = v2
            for s, v in waits.items():
                if kn.get(s, 0) < v:
                    kn[s] = v
            ins.waits = waits
            nwaits += len(waits)
            if ins.is_dma:
                vc = dict(kn)
                vc[('L', ins.lane)] = ins.lane_idx * 16
                ins.vc = vc
            elif ins.needs_inc:
                s, v = src_of(ins)
                vc = dict(kn)
                vc[s] = v
                ins.vc = vc
        self.stats = dict(n=len(instrs), nwaits=nwaits, incs=dict(cnt),
                          nsem=len(sems) + len(self.lane_sem))
        per_eng = {e: [] for e in ('pe', 'act', 'dve', 'pool', 'sp')}
        for ins in instrs:
            per_eng[ins.eng].append(ins)

        def sem_of(s):
            if s[0] == 'L':
                return self.lane_sem[s[1]]
            return sems[s]

        def emit(eng_obj, lst, tail):
            for ins in lst:
                for s, v in ins.waits.items():
                    eng_obj.wait_ge(sem_of(s), v)
                bi = ins.fn(eng_obj)
                if ins.is_dma:
                    bi.then_inc(self.lane_sem[ins.lane], 16)
                elif ins.needs_inc:
                    bi.then_inc(sems[(ins.eng, ins.seq // EPOCH)], 1)
            for ln in tail:
                eng_obj.wait_ge(self.lane_sem[ln], self.lane_cnt[ln] * 16)

        with nc.Block() as block:
            @block.tensor
            def _(e):
                emit(e, per_eng['pe'], ())

            @block.scalar
            def _(e):
                emit(e, per_eng['act'], ())

            @block.vector
            def _(e):
                emit(e, per_eng['dve'], ())

            @block.gpsimd
            def _(e):
                emit(e, per_eng['pool'], ())

            @block.sync
            def _(e):
                emit(e, per_eng['sp'], final_lanes)
        return self.stats


def host_consts(S):
    c = np.zeros((128, 648), np.float32)
    i = np.arange(128)
    c[:, 0:128] = np.eye(128)
    c[:, 128:256] = (i[:, None] <= i[None, :])
    c[:, 256:384] = 1.0
    c[:, 384:512] = (i[:, None] >= i[None, :])
    c[:, 512:640] = (i[:, None] < i[None, :])
    lg = np.log(1.0 - 2.0 ** (-5.0 - 2.0 * np.arange(4, dtype=np.float64)))
    pos = np.arange(128, dtype=np.float64)
    c[:, 640:644] = np.exp(-(pos[:, None] + 1.0) * lg[None, :]) * (256.0 ** -0.5)
    c[:, 644:648] = np.exp((pos[:, None] + 1.0) * lg[None, :])
    am = np.zeros((128, 4, 512), np.float32)
    t = np.arange(512)
    for k in range(4):
        am[:, k, :] = ((128 * k + i)[:, None] < t[None, :])
    inv = (10000.0 ** (-np.arange(0, 256, 2, dtype=np.float32) / np.float32(256))).astype(np.float32)
    ang = inv[:, None] * np.arange(S, dtype=np.float32)[None, :]
    return dict(cst=c, amask=am.reshape(128, 2048),
                cosT=np.cos(ang).astype(np.float32), sinT=np.sin(ang).astype(np.float32))


RET_DECAY = [float(np.exp(128.0 * np.log(1.0 - 2.0 ** (-5.0 - 2.0 * h)))) for h in range(4)]

WSPEC = [("w_in0", D, NIN), ("w_out0", D, D), ("w_gu0", D, 2 * FF), ("w_down0", FF, D),
         ("w_qkv1", D, 3 * D), ("w_out1", D, D), ("w_gu1", D, 2 * FF), ("w_down1", FF, D)]
VSPEC = [("norm_mix0", D), ("b_gates0", 8), ("g_ml0", GW), ("g_ret0", GW), ("norm_ffn0", D),
         ("norm_mix1", D), ("norm_ffn1", D), ("final_norm", D)]


def build(S, stop_after=None):
    NB = S // 512
    NKB = S // 128
    nc = bass.Bass("TRN2", target_bir_lowering=False)
    P = Prog(nc)
    dt = {}

    def din(name, shape, d=F32):
        dt[name] = nc.dram_tensor(name, list(shape), d, kind="ExternalInput").ap()
        return dt[name]

    x = din("x", [S, D])
    wf = {}
    for n, k, m in WSPEC:
        wf[n] = din(n, [k, m])
    vf = {}
    for n, m in VSPEC:
        vf[n] = din(n, [m])
    wconv = din("w_conv0", [4, D])
    cst = din("cst", [128, 648])
    amask_d = din("amask", [128, 2048])
    cosT = din("cosT", [128, S])
    sinT = din("sinT", [128, S])
    y = nc.dram_tensor("y", [S, D], F32, kind="ExternalOutput").ap()
    def slab_specs(n):
        if n in ("w_in0",):
            return [(0, 16, g * 512, 512) for g in range(16)]
        if n in ("w_out0", "w_out1"):
            return [(0, 16, g * 512, 512) for g in range(4)]
        if n in ("w_qkv1",):
            return [(0, 16, g * 512, 512) for g in range(12)]
        if n in ("w_gu0", "w_gu1"):
            r = []
            for sgi in range(11):
                r.append((0, 16, sgi * 512, 512))
                r.append((0, 16, FF + sgi * 512, 512))
            return r
        r = []
        for g in range(4):
            for (kc0, nk) in ((0, 16), (16, 16), (32, 12)):
                r.append((kc0, nk, g * 512, 512))
        return r
    SL = {n: slab_specs(n) for n, _, _ in WSPEC}
    SLI = {n: {sp_: i for i, sp_ in enumerate(SL[n])} for n in SL}
    wbs = {n: nc.dram_tensor(n + "_s", [len(SL[n]), 128, 8192], BF16, kind="Internal").ap()
           for n, _, _ in WSPEC}
    wgb = nc.dram_tensor("wgate_s", [128, 128], BF16, kind="Internal").ap()
    x2 = nc.dram_tensor("x2s", [S, D], F32, kind="Internal").ap()
    qT1 = nc.dram_tensor("qT1", [16, 128, S], BF16, kind="Internal").ap()
    kT1 = nc.dram_tensor("kT1", [16, 128, S], BF16, kind="Internal").ap()
    v1 = nc.dram_tensor("v1", [S, D], BF16, kind="Internal").ap()
    oT1 = nc.dram_tensor("oT1", [16, 128, S], BF16, kind="Internal").ap()

    AR_BYTES = 206 * 1024
    ar = nc.alloc_sbuf_tensor("arena", [128, AR_BYTES // 2], BF16)
    cur = [0]

    def view_at(off, shape, d):
        n = 1
        for s_ in shape[1:]:
            n *= s_
        sz = 4 if d == F32 else 2
        a = ar[:, off // 2: off // 2 + n * sz // 2]
        if d == F32:
            a = a.bitcast(F32)
        if len(shape) == 3:
            a = a.rearrange("p (a b) -> p a b", a=shape[1])
        elif len(shape) == 4:
            a = a.rearrange("p (a b c) -> p a b c", a=shape[1], b=shape[2])
        elif len(shape) == 5:
            a = a.rearrange("p (a b c e) -> p a b c e", a=shape[1], b=shape[2], c=shape[3])
        return a

    def alloc(shape, d):
        n = 1
        for s_ in shape[1:]:
            n *= s_
        nb = n * (4 if d == F32 else 2)
        off = cur[0]
        cur[0] = (off + nb + 63) // 64 * 64
        assert cur[0] <= AR_BYTES, ("SBUF overflow", cur[0])
        return view_at(off, shape, d)

    K = 1024
    ident = alloc([128, 128], BF16)
    linc = alloc([128, 128], BF16)
    lrest = alloc([128, 128], BF16)
    maskle = alloc([128, 128], F32)
    onesf = alloc([128, 128], F32)
    retc = alloc([128, 8], F32)
    gains = alloc([128, 5, 16], F32)
    wcv = alloc([128, 4, 16], F32)
    bgt = alloc([128, 8], F32)
    epsb = alloc([128, 1], F32)
    oneb = alloc([128, 1], F32)
    small = alloc([128, 64], F32)
    gts = alloc([128, 4, 8], F32)
    gsp = alloc([128, 16], F32)
    gcs = alloc([128, 16], F32)
    gws = alloc([128, 16], F32)
    gosc = alloc([128, 16], F32)
    geg = alloc([128, 16], F32)
    ss4 = alloc([128, 4], F32)
    rstd4 = alloc([128, 4], F32)
    convc = alloc([128, 16, 3], F32)
    cst_off = cur[0]
    Cst = alloc([128, 4, 2, 257], F32)
    Cb = alloc([128, 4, 2, 257], BF16)
    Rst = alloc([128, 4, 2, 256], F32)
    Rb = alloc([128, 4, 2, 256], BF16)
    ktm = [alloc([128, 256], BF16) for _ in range(2)]
    PT = [alloc([128, 128], BF16) for _ in range(2)]
    ctile = [alloc([128, 256], F32) for _ in range(2)]
    ytok1 = alloc([128, D], BF16)
    ytok = [ytok1, ytok1]
    xs = [alloc([128, D], BF16), ytok1]
    NWB = 2
    wbuf = [alloc([128, 16, 512], BF16) for _ in range(NWB)]
    wgate = alloc([128, 16, 8], BF16)
    actT = alloc([128, 16, 512], BF16)
    xres = [alloc([128, D], F32) for _ in range(4)]
    big0 = cur[0]
    BIGSZ = 80 * K
    cur[0] += BIGSZ
    assert cur[0] <= AR_BYTES, ("SBUF overflow", cur[0])
    qkT_ml = view_at(big0, [128, 16, 512], BF16)
    qkT_r = view_at(big0 + 16 * K, [128, 16, 512], BF16)
    vml = view_at(big0 + 32 * K, [128, 4, 4, 258], BF16)
    rv = view_at(big0 + 41 * K, [128, 4, 4, 256], BF16)
    og = view_at(big0 + 49 * K, [128, 4, GW], BF16)
    gs = view_at(big0 + 57 * K, [128, 4, GW], BF16)
    tmpb = big0 + 65 * K
    ubufs = [view_at(tmpb, [128, 516], F32), view_at(tmpb + 8448, [128, 516], F32)]
    caccs = [view_at(tmpb + 2112, [128, 512], F32), view_at(tmpb + 10560, [128, 512], F32)]
    cs_t = view_at(tmpb + 4352, [128, 2, 512], F32)
    rta = view_at(tmpb + 8448, [128, 512], F32)
    rtb = view_at(tmpb + 10560, [128, 512], F32)
    gml_b = view_at(tmpb, [128, GW], F32)
    gret_b = view_at(tmpb + 4 * K, [128, GW], F32)
    hT = view_at(big0, [128, 44, 512], BF16)
    sg = [view_at(big0 + 44 * K + i * 2 * K, [128, 512], F32) for i in range(2)]
    sbuf_used = cur[0]

    psb = [nc.alloc_psum_tensor("psb%d" % i, [128, 512], F32) for i in range(8)]

    def ps16(i):
        return psb[i][:].bitcast(BF16)

    P.dma('pool', 'c_ident', ident, cst[:, 0:128])
    P.dma('pool', 'c_linc', linc, cst[:, 384:512])
    P.dma('pool', 'c_lrest', lrest, cst[:, 512:640])
    P.dma('sp', 'c_maskle', maskle, cst[:, 128:256])
    P.dma('sp', 'c_ones', onesf, cst[:, 256:384])
    P.dma('sp', 'c_retc', retc, cst[:, 640:648])
    gl = [("norm_mix0", 0), ("norm_ffn0", 1), ("norm_mix1", 2), ("norm_ffn1", 3)]
    for n, gi in gl:
        P.dma('sp', 'c_gain%d' % gi, gains[:, gi, :], vf[n].rearrange("(c p) -> p c", p=128),
              allow_slow_non_contiguous=True)
    for k_ in range(4):
        P.dma('sp', 'c_wcv', wcv[:, k_, :], wconv[k_, :].rearrange("(c p) -> p c", p=128),
              allow_slow_non_contiguous=True)
    P.dma('sp', 'c_bgt', bgt, vf["b_gates0"].partition_broadcast(128))
    P.memset('pool', epsb, EPS)
    P.memset('pool', oneb, 1.0)
    P.memset('pool', convc, 0.0)
    P.memset('pool', Cst, 0.0)
    P.memset('pool', Cb, 0.0)
    P.memset('pool', Rst, 0.0)
    P.memset('pool', Rb, 0.0)

    cast_i = [0]

    def cast_weights(names):
        for n in names:
            for si, (kc0, nk, c0, ncol) in enumerate(SL[n]):
                dst = wbs[n][si][:, 0:nk * ncol].rearrange("p (c n) -> p c n", c=nk)
                src = wf[n][kc0 * 128:(kc0 + nk) * 128, c0:c0 + ncol].rearrange("(c p) n -> p c n", p=128)
                P.dma('pool', 'cast%d' % (cast_i[0] % 16), dst, src, rk=[], wk=["%s#%d" % (n, si)])
                cast_i[0] += 1

    P.dma('pool', 'cast15', wgb.rearrange("p (c n) -> p c n", c=16),
          wf["w_in0"][:, 8192:8200].rearrange("(c p) n -> p c n", p=128), rk=[], wk=["wgate#"],
          allow_slow_non_contiguous=True)
    cast_weights(["w_in0", "w_out0", "w_gu0", "w_down0", "w_qkv1", "w_out1", "w_gu1", "w_down1"])

    slab_i = [0]

    xbuf = [view_at(big0 + 48 * K, [128, 16, 512], BF16), view_at(big0 + 64 * K, [128, 16, 512], BF16)]
    slab_pool = [wbuf]

    def load_slab(n, kc0, nk, c0, ncol):
        pool_ = slab_pool[0]
        i = slab_i[0] % len(pool_)
        slab_i[0] += 1
        buf = pool_[i]
        si = SLI[n][(kc0, nk, c0, ncol)]
        src = wbs[n][si][:, 0:nk * ncol].rearrange("p (c n) -> p c n", c=nk)
        P.dma('sp', 'wbuf%d' % i, buf[:, 0:nk, 0:ncol], src, rk=["%s#%d" % (n, si)])
        return buf

    tcnt = [0]

    def rms_to_T(gi):
        P.memset('dve', ss4, 0.0)
        for j in range(4):
            P.act(xs[j % 2], xres[j], AF.Square, accum_out=ss4[:, j:j + 1])
        P.act(rstd4, ss4, AF.Sqrt, scale=1.0 / D, bias=epsb[:, 0:1])
        P.recip(rstd4, rstd4)
        for j in range(4):
            xj = xs[j % 2]
            P.act(xj, xres[j], AF.Copy, scale=rstd4[:, j:j + 1])
            for cg in range(4):
                bank = 6 + (tcnt[0] % 2)
                tcnt[0] += 1
                pv = ps16(bank)
                for ci in range(4):
                    c = cg * 4 + ci
                    P.tr(pv[:, ci * 128:(ci + 1) * 128], xj[:, c * 128:(c + 1) * 128], ident)
                gb = gains[:, gi, cg * 4:(cg + 1) * 4].unsqueeze(2).to_broadcast([128, 4, 128])
                P.tt('dve', actT[:, cg * 4:(cg + 1) * 4, j * 128:(j + 1) * 128],
                     pv[:, 0:512].rearrange("p (a b) -> p a b", a=4), gb, ALU.mult)

    def tok_major(slab, bank0, epi):
        for j in range(4):
            pb = psb[bank0 + j]
            for kc in range(16):
                P.mm(pb[:, 0:512], actT[:, kc, j * 128:(j + 1) * 128], slab[:, kc, :],
                     start=(kc == 0), stop=(kc == 15))
            epi(j, pb)

    def feat_major(slab, fcl, bank):
        pb = psb[bank]
        for kc in range(16):
            P.mm(pb[:, 0:512], slab[:, kc, fcl * 128:(fcl + 1) * 128], actT[:, kc, :],
                 start=(kc == 0), stop=(kc == 15))
        return pb

    POOL4 = [wbuf[0], wbuf[1], xbuf[0], xbuf[1]]

    def ffn(layer, mid_hook=None):
        ffn_body(layer, mid_hook)

    def ffn_body(layer, mid_hook):
        gu = "w_gu%d" % layer
        dn = "w_down%d" % layer
        bi = 0
        for sgi in range(11):
            gslab = load_slab(gu, 0, 16, sgi * 512, 512)
            uslab = load_slab(gu, 0, 16, FF + sgi * 512, 512)
            for hcl in range(4):
                hc = sgi * 4 + hcl
                pa = feat_major(gslab, hcl, (bi % 2) * 2)
                pu = feat_major(uslab, hcl, (bi % 2) * 2 + 1)
                s_ = sg[bi % 2]
                bi += 1
                P.act(s_, pa[:, 0:512], AF.Silu)
                P.tt('dve', hT[:, hc, :], pu[:, 0:512], s_, ALU.mult)
        if mid_hook is not None:
            mid_hook()
        for g in range(4):
            for si, (kc0, nk) in enumerate(((0, 16), (16, 16), (32, 12))):
                slab = load_slab(dn, kc0, nk, g * 512, 512)
                for j in range(4):
                    pb = psb[4 + j]
                    for kc in range(nk):
                        P.mm(pb[:, 0:512], hT[:, kc0 + kc, j * 128:(j + 1) * 128], slab[:, kc, :],
                             start=(si == 0 and kc == 0), stop=(si == 2 and kc == nk - 1))
            for j in range(4):
                P.tt('dve', xres[j][:, g * 512:(g + 1) * 512], psb[4 + j][:, 0:512],
                     xres[j][:, g * 512:(g + 1) * 512], ALU.add)

    def out_proj(wname):
        for g in range(4):
            slab = load_slab(wname, 0, 16, g * 512, 512)

            def epi(j, pb, g=g):
                P.tt('dve', xres[j][:, g * 512:(g + 1) * 512], pb[:, 0:512],
                     xres[j][:, g * 512:(g + 1) * 512], ALU.add)
            tok_major(slab, 0, epi)

    for tb in range(NB):
        t0 = tb * 512
        for j in range(4):
            P.dma('sp', 'xres%d' % j, xres[j], x[t0 + j * 128:t0 + (j + 1) * 128, :])
        P.dma('sp', 'cs_t', cs_t[:, 0, :], cosT[:, t0:t0 + 512])
        P.dma('sp', 'cs_t', cs_t[:, 1, :], sinT[:, t0:t0 + 512])
        rms_to_T(0)
        P.dma('sp', 'wgate', wgate, wgb.rearrange("p (c n) -> p c n", c=16), rk=["wgate#"])
        pg = psb[4]
        for j in range(4):
            for kc in range(16):
                P.mm(pg[:, j * 8:(j + 1) * 8], actT[:, kc, j * 128:(j + 1) * 128], wgate[:, kc, :],
                     start=(kc == 0), stop=(kc == 15))
        for j in range(4):
            P.tt('dve', gts[:, j, :], pg[:, j * 8:(j + 1) * 8], bgt, ALU.add)
        gsp3 = gsp.rearrange("p (j h) -> p j h", j=4)
        P.act(gsp3, gts[:, :, 4:8], AF.Exp, scale=-1.0)
        P.act(gsp, gsp, AF.Ln, bias=oneb[:, 0:1])
        pcs = psb[5]
        P.mm(pcs[:, 0:16], maskle, gsp, start=True, stop=True)
        P.mm(pcs[:, 16:32], onesf, gsp, start=True, stop=True)
        P.copy('dve', gcs, pcs[:, 0:16])
        P.tt('dve', gws.rearrange("p (j h) -> p j h", j=4), gts[:, :, 0:4],
             gcs.rearrange("p (j h) -> p j h", j=4), ALU.add)
        P.act(gws, gws, AF.Exp)
        P.act(gosc, gcs, AF.Exp, scale=-1.0)
        P.ts('dve', gosc, gosc, 1.0 / 16.0, None, op0=ALU.mult)
        P.act(geg, pcs[:, 16:32], AF.Exp, scale=-1.0)

        if stop_after == 'gates':
            pr = [gts.rearrange("p a b -> p (a b)"), gsp, gcs, gws, gosc, geg, ss4, rstd4]
            for i_, a_ in enumerate(pr):
                w_ = a_.shape[1]
                P.copy('dve', xres[3][:, 0:w_], a_)
                P.dma('sp', 'yst3', y[i_ * 128:(i_ + 1) * 128, 0:w_], xres[3][:, 0:w_])
            P.copy('dve', xres[2][:, 0:512], actT[:, 0, :])
            P.dma('sp', 'yst2', y[7 * 128:8 * 128, 512:1024], xres[2][:, 0:512])
            st = P.finalize(final_lanes=['yst3', 'yst2'])
            return nc, st, sbuf_used
        for g in range(4):
            slab = load_slab("w_in0", 0, 16, g * 512, 512)
            for fcl in range(4):
                fc = g * 4 + fcl
                pb = feat_major(slab, fcl, fc % 4)
                ubuf = ubufs[fc % 2]
                cacc = caccs[fc % 2]
                P.copy('act', ubuf[:, 3:515], pb[:, 0:512])
                P.act(cacc, pb[:, 0:512], AF.Copy, scale=wcv[:, 3, fc:fc + 1])
                P.copy('act', ubuf[:, 0:3], convc[:, fc, :])
                for k in (2, 1, 0):
                    P.stt('dve', cacc, ubuf[:, k:k + 512], wcv[:, k, fc:fc + 1], cacc,
                          ALU.mult, ALU.add)
                P.copy('act', convc[:, fc, :], ubuf[:, 512:515])
                P.act(qkT_ml[:, fc, :], cacc, AF.Silu)
        for g in (4, 5):
            slab = load_slab("w_in0", 0, 16, g * 512, 512)

            def epi(j, pb, g=g):
                for hl in range(2):
                    h = (g - 4) * 2 + hl
                    P.ts('dve', vml[:, j, h, 0:256], pb[:, hl * 256:(hl + 1) * 256],
                         gws[:, j * 4 + h:j * 4 + h + 1], None, op0=ALU.mult)
                    P.copy('dve', vml[:, j, h, 256:257], gws[:, j * 4 + h:j * 4 + h + 1])
            tok_major(slab, 0, epi)
        for g in (6, 7):
            slab = load_slab("w_in0", 0, 16, g * 512, 512)

            def epi(j, pb, g=g):
                P.act(og[:, j, (g - 6) * 512:(g - 5) * 512], pb[:, 0:512], AF.Sigmoid)
            tok_major(slab, 0, epi)
        for g in (8, 9, 10, 11):
            slab = load_slab("w_in0", 0, 16, g * 512, 512)
            for hl in range(2):
                fc = (g - 8) * 4 + hl * 2
                p1 = feat_major(slab, hl * 2, 0 + hl * 2)
                p2 = feat_major(slab, hl * 2 + 1, 1 + hl * 2)
                ta = rta
                tb_ = rtb
                P.tt('dve', ta, p1[:, 0:512], cs_t[:, 0, :], ALU.mult)
                P.tt('dve', tb_, p2[:, 0:512], cs_t[:, 1, :], ALU.mult)
                P.tt('dve', qkT_r[:, fc, :], ta, tb_, ALU.subtract)
                P.tt('dve', ta, p1[:, 0:512], cs_t[:, 1, :], ALU.mult)
                P.tt('dve', tb_, p2[:, 0:512], cs_t[:, 0, :], ALU.mult)
                P.tt('dve', qkT_r[:, fc + 1, :], ta, tb_, ALU.add)
        for g in (12, 13):
            slab = load_slab("w_in0", 0, 16, g * 512, 512)

            def epi(j, pb, g=g):
                for hl in range(2):
                    h = (g - 12) * 2 + hl
                    P.ts('dve', rv[:, j, h, :], pb[:, hl * 256:(hl + 1) * 256],
                         retc[:, h:h + 1], None, op0=ALU.mult)
            tok_major(slab, 0, epi)
        for g in (14, 15):
            slab = load_slab("w_in0", 0, 16, g * 512, 512)

            def epi(j, pb, g=g):
                P.act(gs[:, j, (g - 14) * 512:(g - 13) * 512], pb[:, 0:512], AF.Silu)
            tok_major(slab, 0, epi)
        P.dma('sp', 'gml_b', gml_b, vf["g_ml0"].partition_broadcast(128))
        P.dma('sp', 'gret_b', gret_b, vf["g_ret0"].partition_broadcast(128))
        for j in range(4):
            P.tt('dve', og[:, j, :], og[:, j, :], gml_b, ALU.mult)
            P.tt('dve', gs[:, j, :], gs[:, j, :], gret_b, ALU.mult)

        sm = small
        for j in range(4):
            jc = slice(j * 128, (j + 1) * 128)
            yt = ytok[j % 2]
            def hinfo(hh):
                is_ml = hh < 4
                h = hh % 4
                qk = qkT_ml if is_ml else qkT_r
                vv = vml[:, j, h, 0:257] if is_ml else rv[:, j, h, :]
                nv = 257 if is_ml else 256
                return is_ml, h, hh % 2, qk, vv, nv, (Cst if is_ml else Rst), (Cb if is_ml else Rb)

            def stageX(hh):
                is_ml, h, par, qk, vv, nv, Sf, Sb = hinfo(hh)
                pk = ps16(6 + par)
                for dc in range(2):
                    P.tr(pk[:, dc * 128:(dc + 1) * 128], qk[:, 8 + 2 * h + dc, jc], ident)
                P.copy('act', ktm[par], pk[:, 0:256])
                pst = psb[par]
                for dc in range(2):
                    P.mm(pst[:, 0:128], qk[:, 8 + 2 * h + dc, jc], qk[:, 2 * h + dc, jc],
                         start=(dc == 0), stop=(dc == 1))
                P.tt('dve', PT[par], pst[:, 0:128], maskle, ALU.mult)
                egs = geg[:, j * 4 + h:j * 4 + h + 1] if is_ml else RET_DECAY[h]
                P.act(Sf[:, h, :, 0:nv], Sf[:, h, :, 0:nv], AF.Copy, scale=egs)

            def stageY(hh):
                is_ml, h, par, qk, vv, nv, Sf, Sb = hinfo(hh)
                egs = geg[:, j * 4 + h:j * 4 + h + 1] if is_ml else RET_DECAY[h]
                po = psb[2 + par]
                P.mm(po[:, 0:nv], PT[par], vv, start=True, stop=False)
                for dc in range(2):
                    P.mm(po[:, 0:nv], qk[:, 2 * h + dc, jc], Sb[:, h, dc, 0:nv],
                         start=False, stop=(dc == 1))
                for dc in range(2):
                    pd = psb[4 + dc]
                    P.mm(pd[:, 0:nv], ktm[par][:, dc * 128:(dc + 1) * 128], vv, start=True, stop=True)
                    P.stt('dve', Sf[:, h, dc, 0:nv], pd[:, 0:nv], egs, Sf[:, h, dc, 0:nv],
                          ALU.mult, ALU.add)
                    P.copy('act', Sb[:, h, dc, 0:nv], Sf[:, h, dc, 0:nv])
                b0 = hh * 8
                steps = []
                if is_ml:
                    osc = gosc[:, j * 4 + h:j * 4 + h + 1]
                    steps.append(lambda: P.act(sm[:, b0:b0 + 1], po[:, 256:257], AF.Abs, scale=osc))
                    steps.append(lambda: P.ts('dve', sm[:, b0:b0 + 1], sm[:, b0:b0 + 1], 1.0, None, op0=ALU.max))
                    steps.append(lambda: P.recip(sm[:, b0:b0 + 1], sm[:, b0:b0 + 1]))
                    steps.append(lambda: P.tt('dve', sm[:, b0 + 1:b0 + 2], sm[:, b0:b0 + 1], osc, ALU.mult))
                    steps.append(lambda: P.memset('dve', sm[:, b0 + 2:b0 + 3], 0.0))
                    steps.append(lambda: P.act(ctile[par], po[:, 0:256], AF.Square, scale=sm[:, b0 + 1:b0 + 2],
                              accum_out=sm[:, b0 + 2:b0 + 3]))
                    steps.append(lambda: P.act(sm[:, b0 + 3:b0 + 4], sm[:, b0 + 2:b0 + 3], AF.Sqrt, scale=1.0 / 256.0,
                              bias=epsb[:, 0:1]))
                    steps.append(lambda: P.recip(sm[:, b0 + 3:b0 + 4], sm[:, b0 + 3:b0 + 4]))
                    steps.append(lambda: P.tt('dve', sm[:, b0 + 4:b0 + 5], sm[:, b0 + 3:b0 + 4], sm[:, b0 + 1:b0 + 2],
                             ALU.mult))
                    steps.append(lambda: P.stt('dve', yt[:, h * 256:(h + 1) * 256], po[:, 0:256], sm[:, b0 + 4:b0 + 5],
                              og[:, j, h * 256:(h + 1) * 256], ALU.mult, ALU.mult))
                else:
                    osc = retc[:, 4 + h:5 + h]
                    steps.append(lambda: P.memset('dve', sm[:, b0:b0 + 2], 0.0))
                    steps.append(lambda: P.act(ctile[par], po[:, 0:256], AF.Copy, accum_out=sm[:, b0:b0 + 1]))
                    steps.append(lambda: P.ts('dve', sm[:, b0 + 2:b0 + 3], sm[:, b0:b0 + 1], -1.0 / 256.0, None,
                             op0=ALU.mult))
                    steps.append(lambda: P.act(ctile[par], po[:, 0:256], AF.Square, bias=sm[:, b0 + 2:b0 + 3],
                              accum_out=sm[:, b0 + 1:b0 + 2]))
                    steps.append(lambda: P.tt('dve', sm[:, b0 + 3:b0 + 4], sm[:, b0 + 1:b0 + 2], osc, ALU.mult))
                    steps.append(lambda: P.tt('dve', sm[:, b0 + 3:b0 + 4], sm[:, b0 + 3:b0 + 4], osc, ALU.mult))
                    steps.append(lambda: P.act(sm[:, b0 + 3:b0 + 4], sm[:, b0 + 3:b0 + 4], AF.Sqrt, scale=1.0 / 256.0,
                              bias=epsb[:, 0:1]))
                    steps.append(lambda: P.recip(sm[:, b0 + 3:b0 + 4], sm[:, b0 + 3:b0 + 4]))
                    steps.append(lambda: P.tt('dve', sm[:, b0 + 4:b0 + 5], sm[:, b0 + 3:b0 + 4], osc, ALU.mult))
                    steps.append(lambda: P.tt('dve', sm[:, b0 + 5:b0 + 6], sm[:, b0 + 4:b0 + 5], sm[:, b0 + 2:b0 + 3],
                             ALU.mult))
                    steps.append(lambda: P.act(ctile[par], po[:, 0:256], AF.Identity, scale=sm[:, b0 + 4:b0 + 5],
                              bias=sm[:, b0 + 5:b0 + 6]))
                    steps.append(lambda: P.tt('dve', yt[:, GW + h * 256:GW + (h + 1) * 256], ctile[par],
                             gs[:, j, h * 256:(h + 1) * 256], ALU.mult))
                return steps

            stageX(0)
            pend = None
            for hh in range(8):
                if hh + 1 < 8:
                    stageX(hh + 1)
                st_ = stageY(hh)
                if pend is None:
                    pend = st_
                else:
                    for k_ in range(max(len(pend), len(st_))):
                        if k_ < len(pend):
                            pend[k_]()
                        if k_ < len(st_):
                            st_[k_]()
                    pend = None
            for cg in range(4):
                pv = ps16(7)
                for ci in range(4):
                    c = cg * 4 + ci
                    P.tr(pv[:, ci * 128:(ci + 1) * 128], yt[:, c * 128:(c + 1) * 128], ident)
                P.copy('act', actT[:, cg * 4:(cg + 1) * 4, jc],
                       pv[:, 0:512].rearrange("p (a b) -> p a b", a=4))
        if stop_after == 'ymix':
            for kc in range(4):
                P.copy('dve', xres[kc].rearrange("p (a b) -> p a b", a=4), actT[:, kc * 4:(kc + 1) * 4, :])
                P.dma('sp', 'yst%d' % kc, y[t0 + kc * 128:t0 + (kc + 1) * 128, :], xres[kc])
            continue
        slab_pool[0] = POOL4
        out_proj("w_out0")
        if stop_after == 'x1':
            for j in range(4):
                P.dma('sp', 'yst%d' % j, y[t0 + j * 128:t0 + (j + 1) * 128, :], xres[j])
            continue
        rms_to_T(1)
        ffn(0)
        slab_pool[0] = wbuf
        for j in range(4):
            P.dma('sp', 'x2st%d' % j, x2[t0 + j * 128:t0 + (j + 1) * 128, :], xres[j],
                  wk=["x2#%d" % tb])

    if stop_after in ('x1', 'ymix'):
        st = P.finalize(final_lanes=['yst%d' % j for j in range(4)])
        return nc, st, sbuf_used
    if stop_after == 'l0':
        for tb in range(NB):
            for j in range(4):
                t0 = tb * 512 + j * 128
                P.dma('sp', 'xres%d' % j, xres[j], x2[t0:t0 + 128, :], rk=["x2#%d" % tb])
                P.dma('sp', 'yst%d' % j, y[t0:t0 + 128, :], xres[j])
        st = P.finalize(final_lanes=['yst%d' % j for j in range(4)])
        return nc, st, sbuf_used

    slab_pool[0] = POOL4
    stg = view_at(big0, [128, 4, 512], BF16)
    stv = view_at(big0 + 4 * K, [128, 4, 512], BF16)
    for tb in range(NB):
        t0 = tb * 512
        for j in range(4):
            P.dma('sp', 'xres%d' % j, xres[j], x2[t0 + j * 128:t0 + (j + 1) * 128, :],
                  rk=["x2#%d" % tb])
        rms_to_T(2)
        for g in range(8):
            slab = load_slab("w_qkv1", 0, 16, g * 512, 512)
            for fcl in range(4):
                pb = feat_major(slab, fcl, fcl)
                if g < 4:
                    P.act(stg[:, fcl, :], pb[:, 0:512], AF.Copy, scale=128.0 ** -0.5)
                else:
                    P.copy('dve', stg[:, fcl, :], pb[:, 0:512])
            dst = (qT1 if g < 4 else kT1)[(g % 4) * 4:(g % 4) * 4 + 4, :, t0:t0 + 512]
            P.dma('pool', 'stg', dst.rearrange("h p t -> p h t"), stg,
                  wk=[("qT1#%d" if g < 4 else "kT1#%d") % tb])
        for g in range(8, 12):
            slab = load_slab("w_qkv1", 0, 16, g * 512, 512)

            def epi(j, pb, g=g):
                P.copy('act' if j % 2 else 'dve', stv[:, j, :], pb[:, 0:512])
            tok_major(slab, 0, epi)
            c0 = (g - 8) * 512
            P.dma('pool', 'stv', v1[t0:t0 + 512, c0:c0 + 512].rearrange("(j p) c -> p j c", p=128),
                  stv, wk=["v1#%d" % tb])

    amask = view_at(big0, [128, 4, 512], BF16)
    kTh = [view_at(big0 + 4 * K + i * 24 * K, [128, S], BF16) for i in range(2)]
    qTh = [view_at(big0 + 4 * K + i * 24 * K + 8 * K, [128, S], BF16) for i in range(2)]
    vh = [view_at(big0 + 4 * K + i * 24 * K + 16 * K, [128, NKB, 128], BF16) for i in range(2)]
    assert 4 * K + 48 * K <= BIGSZ and S * 2 <= 8 * K
    def wtiles(sidx):
        f = wbuf[sidx].bitcast(F32).rearrange("p a b -> p (a b)")
        b = wbuf[sidx].rearrange("p a b -> p (a b)")
        return dict(e=[f[:, 0:512], f[:, 512:1024]], ecs=[f[:, 1024:1536], f[:, 1536:2048]],
                    sp=[b[:, 4096:4608], b[:, 4608:5120]], a=[b[:, 5120:5632], b[:, 5632:6144]],
                    o=[b[:, 6144:6656], b[:, 6656:7168]])
    WT = [wtiles(0), wtiles(1)]
    P.dma('pool', 'c_amask', amask, amask_d.rearrange("p (a b) -> p a b", a=4))
    allq = ["qT1#%d" % tb for tb in range(NB)]
    allk = ["kT1#%d" % tb for tb in range(NB)]
    allv = ["v1#%d" % tb for tb in range(NB)]
    ocnt = [0, 0]
    for hpair in range(8):
        for sidx in range(2):
            h = hpair * 2 + sidx
            P.dma('sp', 'kTh%d' % sidx, kTh[sidx], kT1[h], rk=allk)
            P.dma('sp', 'qTh%d' % sidx, qTh[sidx], qT1[h], rk=allq)
            P.dma('sp', 'vh%d' % sidx, vh[sidx],
                  v1[:, h * 128:(h + 1) * 128].rearrange("(b p) e -> p b e", p=128), rk=allv)
        items = []
        for G in range(NB):
            kbs = list(range(4 * G + 3, -1, -1))
            for n_, kb in enumerate(kbs):
                items.append((G, kb, n_ == 0, n_ == len(kbs) - 1))
        NI = len(items)

        def zmm(sidx, i):
            G, kb, first, last = items[i]
            pz = psb[2 * sidx + i % 2]
            P.mm(pz[:, 0:512], kTh[sidx][:, kb * 128:(kb + 1) * 128],
                 qTh[sidx][:, G * 512:(G + 1) * 512], start=True, stop=True)

        def expln(sidx, i):
            G, kb, first, last = items[i]
            w = WT[sidx]
            pz = psb[2 * sidx + i % 2]
            e = w['e'][i % 2]
            P.act(e, pz[:, 0:512], AF.Exp)

        def lnmask(sidx, i):
            G, kb, first, last = items[i]
            w = WT[sidx]
            e = w['e'][i % 2]
            sp_ = w['sp'][i % 2]
            P.act(sp_, e, AF.Ln, bias=oneb[:, 0:1])
            if kb >= 4 * G:
                k_ = kb - 4 * G
                P.tt('pool', sp_, sp_, amask[:, k_, :], ALU.mult)
                P.tt('pool', e, e, amask[:, k_, :], ALU.mult)

        for i in range(-2, NI + 1):
            for sidx in range(2):
                if 0 <= i + 2 < NI:
                    zmm(sidx, i + 2)
            for sidx in range(2):
                if 0 <= i + 1 < NI:
                    expln(sidx, i + 1)
            for sidx in range(2):
                if 0 <= i + 1 < NI:
                    lnmask(sidx, i + 1)
            for sidx in range(2):
                if 0 <= i - 1 < NI:
                    m = i - 1
                    G, kb, first, last = items[m]
                    w = WT[sidx]
                    P.mm(psb[6 + sidx][:, 0:512], vh[sidx][:, kb, :], w['a'][m % 2],
                         start=first, stop=last)
                    if last:
                        h = hpair * 2 + sidx
                        ot = w['o'][ocnt[sidx] % 2]
                        ocnt[sidx] += 1
                        P.copy('dve', ot, psb[6 + sidx][:, 0:512])
                        P.dma('sp', 'ost%d' % sidx, oT1[h, :, G * 512:(G + 1) * 512], ot,
                              wk=["oT1#%d" % G])
            for sidx in range(2):
                if 0 <= i < NI:
                    w = WT[sidx]
                    P.act(w['ecs'][i % 2], psb[4 + sidx][:, 0:512], AF.Exp, scale=-1.0)
            for sidx in range(2):
                w = WT[sidx]
                if 0 <= i < NI and not items[i][3]:
                    P.mm(psb[4 + sidx][:, 0:512], lrest, w['sp'][i % 2], start=False, stop=False)
                if 0 <= i + 1 < NI:
                    P.mm(psb[4 + sidx][:, 0:512], linc, w['sp'][(i + 1) % 2],
                         start=items[i + 1][2], stop=False)
            for sidx in range(2):
                if 0 <= i < NI:
                    w = WT[sidx]
                    P.tt('dve', w['a'][i % 2], w['e'][i % 2], w['ecs'][i % 2], ALU.mult)

    fnb = view_at(cst_off, [128, D], F32)
    P.dma('sp', 'fnb', fnb, vf["final_norm"].partition_broadcast(128))
    for tb in range(NB):
        t0 = tb * 512
        for j in range(4):
            P.dma('sp', 'xres%d' % j, xres[j], x2[t0 + j * 128:t0 + (j + 1) * 128, :],
                  rk=["x2#%d" % tb])
        def load_actT(tbn):
            P.dma('sp', 'actT', actT, oT1[:, :, tbn * 512:tbn * 512 + 512].rearrange("h p t -> p h t"),
                  rk=["oT1#%d" % tbn])
        if tb == 0:
            load_actT(0)
        out_proj("w_out1")
        rms_to_T(3)
        ffn(1, mid_hook=(lambda tbn=tb + 1: load_actT(tbn)) if tb + 1 < NB else None)
        P.memset('dve', ss4, 0.0)
        for j in range(4):
            P.act(xs[j % 2], xres[j], AF.Square, accum_out=ss4[:, j:j + 1])
        P.act(rstd4, ss4, AF.Sqrt, scale=1.0 / D, bias=epsb[:, 0:1])
        P.recip(rstd4, rstd4)
        for j in range(4):
            P.stt('dve', xres[j], xres[j], rstd4[:, j:j + 1], fnb,
                  ALU.mult, ALU.mult)
            P.dma('pool', 'yst%d' % j, y[t0 + j * 128:t0 + (j + 1) * 128, :], xres[j])
    st = P.finalize(final_lanes=['yst%d' % j for j in range(4)])
    return nc, st, sbuf_used


_CACHE = {}


def kernel(**inputs):
    x = np.asarray(inputs["x"], dtype=np.float32)
    B, S, _ = x.shape
    if S not in _CACHE:
        _CACHE[S] = (build(S)[0], host_consts(S))
    nc, consts = _CACHE[S]
    base = {}
    for n, _, _ in WSPEC:
        base[n] = np.ascontiguousarray(np.asarray(inputs[n], dtype=np.float32))
    for n, _ in VSPEC:
        base[n] = np.ascontiguousarray(np.asarray(inputs[n], dtype=np.float32))
    base["w_conv0"] = np.ascontiguousarray(np.asarray(inputs["w_conv0"], dtype=np.float32))
    base.update(consts)
    in_maps = []
    for b in range(B):
        m = dict(base)
        m["x"] = np.ascontiguousarray(x[b])
        in_maps.append(m)
    res = run_bass_kernel_spmd(nc, in_maps, core_ids=list(range(B)))
    return np.stack([np.asarray(r["y"], dtype=np.float32) for r in res.results], axis=0)
```
